# Optimizing a Trainium2 kernel written in Bass

```python
import math
import jax
import jax.numpy as jnp
from jax import lax
import numpy as np

D_MODEL = 1024
BATCH = 8
SEQ = 2048
DEPTH = 4

GRID_W = 64
CTX_LEN = 256
N_MIXERS = 4
EPS = 1e-6

SSD_D_INNER = 2 * D_MODEL
SSD_HEADDIM = 64
SSD_HEADS = SSD_D_INNER // SSD_HEADDIM
SSD_GROUPS = 8
SSD_STATE = 128
SSD_CONV = 3
SSD_CHUNK = 128
SSD_BC = SSD_GROUPS * SSD_STATE
SSD_XBC = SSD_D_INNER + 2 * SSD_BC
SSD_PROJ = SSD_D_INNER + SSD_XBC + 2 * SSD_HEADS

GM_HIDDEN = 2 * D_MODEL
GM_GROUPS = 8
GM_CHUNK = 128

HY_ORDER = 2
HY_SHORT = 3
HY_EMB = 33
HY_BANDS = (HY_EMB - 1) // 2
HY_FILTER_HIDDEN = 64
HY_FAST_DECAY = 0.3
HY_SLOW_DECAY = 1.5
HY_TARGET = 1e-2

FN_GROUPS = 4
FN_GROUP_W = D_MODEL // FN_GROUPS

FFN_HIDDEN = -(-8 * D_MODEL // (3 * 256)) * 256

N_SSD_LAYERS = (DEPTH + N_MIXERS - 1) // N_MIXERS
N_GM_LAYERS = (DEPTH + N_MIXERS - 2) // N_MIXERS
N_HY_LAYERS = (DEPTH + N_MIXERS - 3) // N_MIXERS
N_FN_LAYERS = DEPTH // N_MIXERS

kernel_name = "hybrid_interleaved_ssd_gmlp_hyena_fnet_dit"


def rmsnorm(x, w):
    xf = x.astype(jnp.float32)
    y = xf * lax.rsqrt(jnp.mean(xf * xf, axis=-1, keepdims=True) + EPS)
    return (y * w).astype(x.dtype)


def layernorm(x, w, b):
    xf = x.astype(jnp.float32)
    mu = jnp.mean(xf, axis=-1, keepdims=True)
    var = jnp.mean(jnp.square(xf - mu), axis=-1, keepdims=True)
    return ((xf - mu) * lax.rsqrt(var + EPS) * w + b).astype(x.dtype)


def adaln(cond, w, b, n_chunks):
    cond = cond.reshape(-1, D_MODEL)
    mod = jax.nn.silu(cond) @ w[:, :n_chunks * D_MODEL] + b[:n_chunks * D_MODEL]
    return jnp.split(mod[:, None, :], n_chunks, axis=-1)


def swiglu(h, w1, w3, w2):
    return (jax.nn.silu(h @ w1) * (h @ w3)) @ w2


def dwconv_centred(x, w, b):
    pad = w.shape[0] // 2
    y = lax.conv_general_dilated(x, w[:, None, :], window_strides=(1,), padding=[(pad, pad)],
                                 dimension_numbers=("NWC", "WIO", "NWC"),
                                 feature_group_count=x.shape[-1])
    return y + b


def grid_sincos(rows):
    t = jnp.arange(rows * GRID_W)
    row, col = t // GRID_W, t % GRID_W
    quarter = D_MODEL // 4
    omega = 10000.0 ** (-jnp.arange(quarter, dtype=jnp.float32) / quarter)

    def enc(p):
        ang = p.astype(jnp.float32)[:, None] * omega
        return jnp.concatenate([jnp.sin(ang), jnp.cos(ang)], axis=-1)

    return jnp.concatenate([enc(row), enc(col)], axis=-1)


def segsum(a):
    t = a.shape[-1]
    cs = jnp.cumsum(a, axis=-1)
    diff = cs[..., :, None] - cs[..., None, :]
    return jnp.where(jnp.tril(jnp.ones((t, t), dtype=bool)), diff, -jnp.inf)


def ssd_scan(x, dt, a, bm, cm, h0, with_output):
    bsz, L, H, P = x.shape
    G, N = bm.shape[-2:]
    R = H // G
    nc = L // SSD_CHUNK
    xs = (x * dt[..., None]).reshape(bsz, nc, SSD_CHUNK, G, R, P)
    adt = (dt * a).reshape(bsz, nc, SSD_CHUNK, G, R).transpose(0, 3, 4, 1, 2)
    bc = bm.reshape(bsz, nc, SSD_CHUNK, G, N)
    cc = cm.reshape(bsz, nc, SSD_CHUNK, G, N)
    a_cs = jnp.cumsum(adt, axis=-1)
    decay_states = jnp.exp(a_cs[..., -1:] - a_cs)
    states = jnp.einsum("bclgn,bgrcl,bclgrp->bcgrpn", bc, decay_states, xs)
    states = jnp.concatenate([h0.reshape(bsz, 1, G, R, P, N), states], axis=1)
    a_chunk = jnp.pad(a_cs[..., -1], ((0, 0), (0, 0), (0, 0), (1, 0)))
    decay_chunk = jnp.exp(segsum(a_chunk))
    states = jnp.einsum("bgrzc,bcgrpn->bzgrpn", decay_chunk, states)
    final = states[:, -1].reshape(bsz, H, P, N)
    if not with_output:
        return None, final
    states = states[:, :-1]
    scores = jnp.einsum("bclgn,bcsgn->bgcls", cc, bc)[:, :, None] * jnp.exp(segsum(adt))
    y_diag = jnp.einsum("bgrcls,bcsgrp->bclgrp", scores, xs)
    y_off = jnp.einsum("bclgn,bcgrpn,bgrcl->bclgrp", cc, states, jnp.exp(a_cs))
    return (y_diag + y_off).reshape(bsz, L, H, P), final


def ssd_mixer(hn, hcn, w_in, conv_w, conv_b, dt_bias, a_log, d_skip, norm_w, w_out, ctx_out):
    f32 = jnp.float32
    a = -jnp.exp(a_log.astype(f32))
    flip = lambda t: jnp.flip(t, axis=1)

    def branch(h):
        bsz, L, _ = h.shape
        xbcdt = h @ w_in[:, SSD_D_INNER:]
        xbc = jax.nn.silu(dwconv_centred(xbcdt[..., :SSD_XBC], conv_w, conv_b)).astype(f32)
        xs = xbc[..., :SSD_D_INNER].reshape(bsz, L, SSD_HEADS, SSD_HEADDIM)
        bm = xbc[..., SSD_D_INNER:SSD_D_INNER + SSD_BC].reshape(bsz, L, SSD_GROUPS, SSD_STATE)
        cm = xbc[..., SSD_D_INNER + SSD_BC:].reshape(bsz, L, SSD_GROUPS, SSD_STATE)
        dt = jax.nn.softplus(xbcdt[..., SSD_XBC:].astype(f32).reshape(bsz, L, 2, SSD_HEADS)
                             + dt_bias.astype(f32))
        return xs, bm, cm, dt

    def bidir(h, h0_f, h0_b, with_output):
        xs, bm, cm, dt = branch(h)
        fwd = ssd_scan(xs, dt[:, :, 0], a[0], bm, cm, h0_f, with_output)
        bwd = ssd_scan(flip(xs), flip(dt[:, :, 1]), a[1], flip(bm), flip(cm), h0_b, with_output)
        return xs, fwd, bwd

    def gated_out(h, xs, y):
        bsz, L, _ = h.shape
        z = (h @ w_in[:, :SSD_D_INNER]).astype(f32)
        y = (y + d_skip.astype(f32)[:, None] * xs).reshape(bsz, L, SSD_D_INNER) * jax.nn.silu(z)
        yg = y.reshape(bsz, L, SSD_GROUPS, SSD_D_INNER // SSD_GROUPS)
        yg = yg * lax.rsqrt(jnp.mean(yg * yg, axis=-1, keepdims=True) + EPS)
        y = yg.reshape(bsz, L, SSD_D_INNER) * norm_w
        return y.astype(h.dtype) @ w_out

    h0 = jnp.zeros((hn.shape[0], SSD_HEADS, SSD_HEADDIM, SSD_STATE), f32)
    xs_c, (yf_c, sf_c), (yb_c, sb_c) = bidir(hcn, h0, h0, ctx_out)
    xs, (yf, _), (yb, _) = bidir(hn, sf_c, sb_c, True)
    lat_mix = gated_out(hn, xs, yf + flip(yb))
    ctx_mix = gated_out(hcn, xs_c, yf_c + flip(yb_c)) if ctx_out else None
    return lat_mix, ctx_mix


def chunk_mlp_mixer(h, w_in, ln_w, ln_b, w_s, b_s, w_out):
    bsz, L, _ = h.shape
    u, v = jnp.split(jax.nn.gelu(h @ w_in, approximate=False), 2, axis=-1)
    v = layernorm(v, ln_w, ln_b).reshape(bsz, L // GM_CHUNK, GM_CHUNK, GM_GROUPS, GM_HIDDEN // GM_GROUPS)
    s = jnp.einsum("gpq,bnqgd->bnpgd", w_s, v) + b_s.T[None, None, :, :, None]
    return (u * s.reshape(bsz, L, GM_HIDDEN)) @ w_out


def hyena_filters(L, f_w0, f_b0, f_w1, f_b1, f_w2, sin_freq):
    f32 = jnp.float32
    pos = jnp.arange(L, dtype=f32)[:, None]
    t = pos / max(L - 1, 1)
    freqs = jnp.linspace(1e-4, HY_BANDS - 1, HY_BANDS, dtype=f32)[None, :]
    ang = freqs * (2.0 * math.pi * pos / L)
    feats = jnp.concatenate([t, jnp.cos(ang), -jnp.sin(ang)], axis=-1)
    hid = jnp.sin(sin_freq[0] * (feats @ f_w0 + f_b0))
    hid = jnp.sin(sin_freq[1] * (hid @ f_w1 + f_b1))
    filt = (hid @ f_w2).astype(f32).reshape(L, 2, HY_ORDER, D_MODEL)
    max_decay = math.log(HY_TARGET) / HY_FAST_DECAY
    min_decay = math.log(HY_TARGET) / HY_SLOW_DECAY
    deltas = jnp.linspace(min_decay, max_decay, D_MODEL, dtype=f32)
    window = jnp.exp(-t * jnp.abs(deltas))
    filt = filt * window[:, None, None, :]
    filt = filt / jnp.sum(jnp.abs(filt), axis=0, keepdims=True)
    return filt.transpose(1, 2, 0, 3)


def fft_conv(u, h):
    L = u.shape[1]
    n = 2 * L
    y = jnp.fft.irfft(jnp.fft.rfft(u, n=n, axis=1) * jnp.fft.rfft(h, n=n, axis=0)[None], n=n, axis=1)
    return y[:, :L]


def hyena_mixer(h, w_in, conv_w, conv_b, f_w0, f_b0, f_w1, f_b1, f_w2, sin_freq, bias, w_out):
    f32 = jnp.float32
    L = h.shape[1]
    proj = dwconv_centred(h @ w_in, conv_w, conv_b)
    v, x1, x2 = jnp.split(proj, 3, axis=-1)
    filt = hyena_filters(L, f_w0, f_b0, f_w1, f_b1, f_w2, sin_freq)
    z = v.astype(f32)
    for o, gate in enumerate((x1, x2)):
        conv = fft_conv(z, filt[0, o]) + jnp.flip(fft_conv(jnp.flip(z, axis=1), filt[1, o]), axis=1)
        z = gate.astype(f32) * (conv + bias[o] * z)
    return z.astype(h.dtype) @ w_out


def fourier_mixer(h, w_out, b_out):
    bsz, L, _ = h.shape
    hg = h.astype(jnp.float32).reshape(bsz, L, FN_GROUPS, FN_GROUP_W)
    f = jnp.fft.fft2(hg, axes=(1, 3), norm="ortho").real
    return f.reshape(bsz, L, D_MODEL).astype(h.dtype) @ w_out + b_out


def setup_inputs(seed: int = 0) -> dict:
    key = jax.random.key(seed)
    ks = iter(jax.random.split(key, 48))
    f32 = jnp.float32
    D = D_MODEL

    def nrm(shape, scale):
        return jax.random.normal(next(ks), shape, f32) * scale

    def gain(shape):
        return 1.0 + nrm(shape, 0.02)

    dt0 = jnp.exp(jax.random.uniform(next(ks), (N_SSD_LAYERS, 2, SSD_HEADS), f32,
                                     math.log(1e-3), math.log(1e-1)))
    return {
        "x": nrm((BATCH, SEQ, D), 1.0),
        "c": nrm((BATCH, D), 1.0),
        "ctx": nrm((BATCH, CTX_LEN, D), 1.0),
        "c_ctx": nrm((D,), 1.0),
        "ada_w": nrm((DEPTH, D, 6 * D), D ** -0.5),
        "ada_b": nrm((DEPTH, 6 * D), 0.02),
        "norm_mix_w": gain((DEPTH, D)),
        "norm_ffn_w": gain((DEPTH, D)),
        "ffn_w1": nrm((DEPTH, D, FFN_HIDDEN), D ** -0.5),
        "ffn_w3": nrm((DEPTH, D, FFN_HIDDEN), D ** -0.5),
        "ffn_w2": nrm((DEPTH, FFN_HIDDEN, D), FFN_HIDDEN ** -0.5),
        "ssd_w_in": nrm((N_SSD_LAYERS, D, SSD_PROJ), D ** -0.5),
        "ssd_conv_w": nrm((N_SSD_LAYERS, SSD_CONV, SSD_XBC), SSD_CONV ** -0.5),
        "ssd_conv_b": nrm((N_SSD_LAYERS, SSD_XBC), 0.02),
        "ssd_dt_bias": dt0 + jnp.log(-jnp.expm1(-dt0)),
        "ssd_a_log": jnp.log(jax.random.uniform(next(ks), (N_SSD_LAYERS, 2, SSD_HEADS), f32, 1.0, 16.0)),
        "ssd_d": gain((N_SSD_LAYERS, SSD_HEADS)),
        "ssd_norm_w": gain((N_SSD_LAYERS, SSD_D_INNER)),
        "ssd_w_out": nrm((N_SSD_LAYERS, SSD_D_INNER, D), SSD_D_INNER ** -0.5),
        "gm_w_in": nrm((N_GM_LAYERS, D, 2 * GM_HIDDEN), D ** -0.5),
        "gm_ln_w": gain((N_GM_LAYERS, GM_HIDDEN)),
        "gm_ln_b": nrm((N_GM_LAYERS, GM_HIDDEN), 0.02),
        "gm_w_s": nrm((N_GM_LAYERS, GM_GROUPS, GM_CHUNK, GM_CHUNK), GM_CHUNK ** -0.5),
        "gm_b_s": gain((N_GM_LAYERS, GM_GROUPS, GM_CHUNK)),
        "gm_w_out": nrm((N_GM_LAYERS, GM_HIDDEN, D), GM_HIDDEN ** -0.5),
        "hy_w_in": nrm((N_HY_LAYERS, D, 3 * D), D ** -0.5),
        "hy_conv_w": nrm((N_HY_LAYERS, HY_SHORT, 3 * D), HY_SHORT ** -0.5),
        "hy_conv_b": nrm((N_HY_LAYERS, 3 * D), 0.02),
        "hy_f_w0": nrm((N_HY_LAYERS, HY_EMB, HY_FILTER_HIDDEN), HY_EMB ** -0.5),
        "hy_f_b0": nrm((N_HY_LAYERS, HY_FILTER_HIDDEN), 0.1),
        "hy_f_w1": nrm((N_HY_LAYERS, HY_FILTER_HIDDEN, HY_FILTER_HIDDEN), HY_FILTER_HIDDEN ** -0.5),
        "hy_f_b1": nrm((N_HY_LAYERS, HY_FILTER_HIDDEN), 0.1),
        "hy_f_w2": nrm((N_HY_LAYERS, HY_FILTER_HIDDEN, 2 * HY_ORDER * D), HY_FILTER_HIDDEN ** -0.5),
        "hy_sin_freq": gain((N_HY_LAYERS, 2, HY_FILTER_HIDDEN)),
        "hy_bias": nrm((N_HY_LAYERS, HY_ORDER, D), 0.5),
        "hy_w_out": nrm((N_HY_LAYERS, D, D), D ** -0.5),
        "fn_w_out": nrm((N_FN_LAYERS, D, D), D ** -0.5),
        "fn_b_out": nrm((N_FN_LAYERS, D), 0.02),
        "final_norm_w": gain((D,)),
    }


def reference(x, c, ctx, c_ctx, ada_w, ada_b, norm_mix_w, norm_ffn_w, ffn_w1, ffn_w3, ffn_w2,
              ssd_w_in, ssd_conv_w, ssd_conv_b, ssd_dt_bias, ssd_a_log, ssd_d, ssd_norm_w, ssd_w_out,
              gm_w_in, gm_ln_w, gm_ln_b, gm_w_s, gm_b_s, gm_w_out,
              hy_w_in, hy_conv_w, hy_conv_b, hy_f_w0, hy_f_b0, hy_f_w1, hy_f_b1, hy_f_w2, hy_sin_freq,
              hy_bias, hy_w_out, fn_w_out, fn_b_out, final_norm_w):
    seq = x.shape[1]
    rows = seq // GRID_W
    h = x + grid_sincos(rows).astype(x.dtype)[None]
    hc = ctx
    last_reader = ((DEPTH - 1) // N_MIXERS) * N_MIXERS

    def seq_mixer(kind, j, hs):
        if kind == 1:
            return chunk_mlp_mixer(hs, gm_w_in[j], gm_ln_w[j], gm_ln_b[j], gm_w_s[j], gm_b_s[j], gm_w_out[j])
        if kind == 2:
            return hyena_mixer(hs, hy_w_in[j], hy_conv_w[j], hy_conv_b[j], hy_f_w0[j], hy_f_b0[j],
                               hy_f_w1[j], hy_f_b1[j], hy_f_w2[j], hy_sin_freq[j], hy_bias[j], hy_w_out[j])
        return fourier_mixer(hs, fn_w_out[j], fn_b_out[j])

    for i in range(DEPTH):
        kind, j = i % N_MIXERS, i // N_MIXERS
        upd_ctx = i < last_reader
        sh1, sc1, g1, sh2, sc2, g2 = adaln(c, ada_w[i], ada_b[i], 6)
        hn = rmsnorm(h, norm_mix_w[i]) * (1 + sc1) + sh1
        hcn = None
        if kind == 0 or upd_ctx:
            cmod = adaln(c_ctx, ada_w[i], ada_b[i], 6 if upd_ctx else 2)
            hcn = rmsnorm(hc, norm_mix_w[i]) * (1 + cmod[1]) + cmod[0]
        if kind == 0:
            mix, mix_c = ssd_mixer(hn, hcn, ssd_w_in[j], ssd_conv_w[j], ssd_conv_b[j], ssd_dt_bias[j],
                                   ssd_a_log[j], ssd_d[j], ssd_norm_w[j], ssd_w_out[j], upd_ctx)
        else:
            mix = seq_mixer(kind, j, hn)
            mix_c = seq_mixer(kind, j, hcn) if upd_ctx else None
        h = h + g1 * mix
        h = h + g2 * swiglu(rmsnorm(h, norm_ffn_w[i]) * (1 + sc2) + sh2, ffn_w1[i], ffn_w3[i], ffn_w2[i])
        if upd_ctx:
            hc = hc + cmod[2] * mix_c
            hc = hc + cmod[5] * swiglu(rmsnorm(hc, norm_ffn_w[i]) * (1 + cmod[4]) + cmod[3],
                                       ffn_w1[i], ffn_w3[i], ffn_w2[i])
    return rmsnorm(h, final_norm_w)
```

```python
import contextlib
import math
import os
import numpy as np
import concourse.bass as bass
import concourse.mybir as mybir
from concourse.bass_utils import run_bass_kernel_spmd

F32 = mybir.dt.float32
BF16 = mybir.dt.bfloat16
AF = mybir.ActivationFunctionType
ALU = mybir.AluOpType
AX = mybir.AxisListType

D = 1024
L = 2048
LC = 256
FF = 2816
NF = 22
EPS = 1e-6
NCORES = 8

ENGS = ["sync", "scalar", "tensor", "vector", "gpsimd"]
DMA_K = 8
SAME_ENGINE_SYNC = True


class Res:
    __slots__ = ("last_w", "readers", "dma_readers")

    def __init__(self):
        self.last_w = None
        self.readers = {}
        self.dma_readers = []


class Buf:
    def __init__(self, name, t):
        self.name = name
        self.t = t
        self.res = {}

    def __getitem__(self, key):
        return (self, key)


class PsBuf(Buf):
    def __getitem__(self, key):
        return (self, None)


def RK(buf, keys):
    return [(buf, k) for k in keys]


class Prog:
    def __init__(self, nc):
        self.nc = nc
        self.stream = {e: [] for e in ENGS}
        self.ops = []
        self.ccount = {e: 0 for e in ENGS}
        self.dcount = {e: 0 for e in ENGS}
        self.seen = {e: {} for e in ENGS}
        self.pending_noinc = {e: [] for e in ENGS}
        self.semkeys = set()

    def _collect(self, reads, writes):
        deps = set()
        self._war = set()
        for (buf, key) in reads:
            keys = list(buf.res.keys()) if key is None else [key, None]
            for k in keys:
                r = buf.res.get(k)
                if r is not None and r.last_w is not None:
                    deps.add(r.last_w)
        for (buf, key) in writes:
            keys = list(buf.res.keys()) if key is None else [key, None]
            for k in keys:
                r = buf.res.get(k)
                if r is None:
                    continue
                if r.last_w is not None:
                    deps.add(r.last_w)
                self._war.update(r.readers.values())
                self._war.update(r.dma_readers)
        self._war -= deps
        return deps | self._war

    def _mark(self, opid, eng, is_dma, reads, writes):
        for (buf, key) in reads:
            r = buf.res.get(key)
            if r is None:
                r = buf.res[key] = Res()
            if is_dma:
                r.dma_readers.append(opid)
            else:
                r.readers[eng] = opid
        for (buf, key) in writes:
            if key is None:
                buf.res = {}
            r = buf.res.get(key)
            if r is None:
                r = buf.res[key] = Res()
            r.last_w = opid
            r.readers = {}
            r.dma_readers = []

    def _emit_waits(self, eng, deps):
        for d in sorted(deps):
            deng, kind, semkey, val = self.ops[d]
            if kind == "c" and deng == eng:
                if eng == "tensor" or not SAME_ENGINE_SYNC or d in self._war:
                    continue
            if val is None:
                raise RuntimeError("dependency on op without completion signal")
            if self.seen[eng].get(semkey, 0) >= val:
                continue
            self.seen[eng][semkey] = val
            self.stream[eng].append(("wait", semkey, val))

    def op(self, eng, fn, reads=(), writes=(), inc=True):
        deps = self._collect(reads, writes)
        self._emit_waits(eng, deps)
        opid = len(self.ops)
        if inc:
            self.ccount[eng] += 1
            semkey = "c_" + eng
            self.semkeys.add(semkey)
            val = self.ccount[eng]
            self.ops.append((eng, "c", semkey, val))
            for pid in self.pending_noinc[eng]:
                self.ops[pid] = (eng, "c", semkey, val)
            self.pending_noinc[eng] = []
            self.stream[eng].append(("op", fn, semkey, 1))
        else:
            self.ops.append((eng, "c", "c_" + eng, None))
            self.pending_noinc[eng].append(opid)
            self.stream[eng].append(("op", fn, None, 0))
        self._mark(opid, eng, False, reads, writes)
        return opid

    def dma(self, eng, out, in_, reads=(), writes=(), **kw):
        deps = self._collect(reads, writes)
        i = self.dcount[eng]
        self.dcount[eng] += 1
        slot = i % DMA_K
        semkey = "d_%s_%d" % (eng, slot)
        self.semkeys.add(semkey)
        prev_val = 16 * (i // DMA_K)
        val = prev_val + 16
        self._emit_waits(eng, deps)
        if prev_val > 0 and self.seen[eng].get(semkey, 0) < prev_val:
            self.seen[eng][semkey] = prev_val
            self.stream[eng].append(("wait", semkey, prev_val))
        opid = len(self.ops)
        self.ops.append((eng, "d", semkey, val))
        fn = lambda e, out=out, in_=in_, kw=kw: e.dma_start(out=out, in_=in_, **kw)
        self.stream[eng].append(("op", fn, semkey, 16))
        self._mark(opid, eng, True, reads, writes)
        return opid

    def barrier(self, engines=None):
        for e in ENGS:
            if self.pending_noinc[e]:
                raise RuntimeError("barrier with trailing no-inc ops on " + e)
        for e in (engines or ENGS):
            for e2 in ENGS:
                n = self.dcount[e2]
                for slot in range(min(n, DMA_K)):
                    cnt = (n - 1 - slot) // DMA_K + 1
                    semkey = "d_%s_%d" % (e2, slot)
                    if self.seen[e].get(semkey, 0) < 16 * cnt:
                        self.seen[e][semkey] = 16 * cnt
                        self.stream[e].append(("wait", semkey, 16 * cnt))
                if self.ccount[e2]:
                    semkey = "c_" + e2
                    if self.seen[e].get(semkey, 0) < self.ccount[e2]:
                        self.seen[e][semkey] = self.ccount[e2]
                        self.stream[e].append(("wait", semkey, self.ccount[e2]))

    def finish(self, final_eng="sync"):
        self.barrier([final_eng])

    def run_block(self):
        nc = self.nc
        with contextlib.ExitStack() as st:
            sems = {}
            for k in sorted(self.semkeys):
                sems[k] = st.enter_context(nc.semaphore(k))
            block = st.enter_context(nc.Block())

            def mk(engname):
                def body(e):
                    for act in self.stream[engname]:
                        if act[0] == "wait":
                            e.wait_ge(sems[act[1]], act[2])
                        else:
                            ins = act[1](e)
                            if act[2] is not None:
                                ins.then_inc(sems[act[2]], act[3])
                return body

            for engname in ENGS:
                if self.stream[engname]:
                    getattr(block, engname)(mk(engname))


class Arena:
    def __init__(self, name, ap, nwords):
        self.name = name
        self.ap = ap
        self.nwords = nwords
        self.off = 0
        self.n = 0

    def reset(self):
        self.off = 0

    def alloc(self, name, free_shape, dt):
        free_shape = tuple(free_shape)
        nel = int(np.prod(free_shape))
        words = nel if dt == F32 else (nel + 1) // 2
        words = (words + 7) // 8 * 8
        if self.off + words > self.nwords:
            raise RuntimeError("arena %s overflow: %s needs %d words at %d / %d" % (self.name, name, words, self.off, self.nwords))
        v = self.ap[:, self.off:self.off + words]
        if dt != F32:
            v = v.bitcast(dt)
        v = v[:, 0:nel]
        if len(free_shape) == 2:
            v = v.rearrange("p (a b) -> p a b", a=free_shape[0])
        elif len(free_shape) == 3:
            v = v.rearrange("p (a b c) -> p a b c", a=free_shape[0], b=free_shape[1])
        elif len(free_shape) == 4:
            v = v.rearrange("p (a b c d) -> p a b c d", a=free_shape[0], b=free_shape[1], c=free_shape[2])
        self.off += words
        self.n += 1
        return Buf("%s.%s.%d" % (self.name, name, self.n), v)


ARENA_WORDS = 26432


class KB:
    def __init__(self, phases, add_grid=True, final_norm=True):
        self.phases = phases
        self.add_grid = add_grid
        self.final_norm = final_norm
        self.nc = bass.Bass("TRN2", target_bir_lowering=False)
        self.P = Prog(self.nc)
        self.din = {}
        self.st = contextlib.ExitStack()
        self.layers = sorted(set(i for (_, i) in phases))

    def inp(self, name, shape):
        if name not in self.din:
            self.din[name] = self.nc.dram_tensor(name, list(shape), F32, kind="ExternalInput").ap()
        return self.din[name]

    def scratch(self, name, shape, dt):
        return Buf(name, self.nc.dram_tensor(name, list(shape), dt, kind="Internal").ap())

    def sb(self, name, shape, dt):
        return Buf(name, self.st.enter_context(self.nc.sbuf_tensor("s_" + name, list(shape), dt)))

    def mm(self, out, lhsT, rhs, start, stop, reads, writes, inc=True):
        self.P.op("tensor", lambda e: e.matmul(out, lhsT=lhsT, rhs=rhs, start=start, stop=stop),
                  reads=reads, writes=writes, inc=inc)

    def act(self, out, in_, func, reads, writes, bias=None, scale=None):
        kw = {}
        if bias is not None:
            kw["bias"] = bias
        if scale is not None:
            kw["scale"] = scale
        self.P.op("scalar", lambda e: e.activation(out=out, in_=in_, func=func, **kw), reads=reads, writes=writes)

    def tt(self, eng, out, in0, in1, op, reads, writes):
        self.P.op(eng, lambda e: e.tensor_tensor(out=out, in0=in0, in1=in1, op=op), reads=reads, writes=writes)

    def ts(self, eng, out, in0, s1, s2, op0, op1, reads, writes):
        self.P.op(eng, lambda e: e.tensor_scalar(out=out, in0=in0, scalar1=s1, scalar2=s2, op0=op0, op1=op1),
                  reads=reads, writes=writes)

    def stt(self, eng, out, in0, scalar, in1, op0, op1, reads, writes):
        self.P.op(eng, lambda e: e.scalar_tensor_tensor(out=out, in0=in0, scalar=scalar, in1=in1, op0=op0, op1=op1),
                  reads=reads, writes=writes)

    def cp(self, eng, out, in_, reads, writes):
        if eng == "scalar":
            self.P.op(eng, lambda e: e.copy(out=out, in_=in_), reads=reads, writes=writes)
        else:
            self.P.op(eng, lambda e: e.tensor_copy(out=out, in_=in_), reads=reads, writes=writes)

    def new_epoch(self, use_hn=False):
        self.P.barrier()
        self.ar.reset()
        self.ar2.reset()

    def build(self):
        nc, P = self.nc, self.P
        with self.st:
            self.h = self.sb("h", [128, 8, L], F32)
            self.hn = self.sb("hn", [128, 8, L], BF16)
            arena_t = self.st.enter_context(nc.sbuf_tensor("arena", [128, ARENA_WORDS], F32))
            self.ar = Arena("ar", arena_t[:], ARENA_WORDS)
            self.ar2 = Arena("ar2", self.hn.t[:].rearrange("p a b -> p (a b)").bitcast(F32), 8 * L // 2)
            self.ident = self.sb("ident", [128, 128], BF16)
            self.ones_bf = self.sb("ones_bf", [128, 128], BF16)
            self.ones_f = self.sb("ones_f", [128, 128], F32)
            self.modb = self.sb("modb", [128, 4, 48, 2], F32)
            self.adab = self.sb("adab", [128, 4, 48], F32)
            self.nw = self.sb("nw", [128, 4, 2, 8], F32)
            self.fnw = self.sb("fnw", [128, 8], F32)
            self.lv = self.sb("lv", [128, 4, 4, 8], F32)
            self.ccs = self.sb("ccs", [128, 8, 2], F32)
            self.cs = self.sb("cs", [128, 8, 2], BF16)
            self.adw = [self.sb("adw%d" % n, [128, 8, 128], BF16) for n in range(2)]
            self.ps = [PsBuf("ps%d" % i, self.st.enter_context(nc.psum_tensor("ps%d" % i, [128, 512], F32))) for i in range(8)]

            P.op("gpsimd", lambda e: e.memset(self.ident.t[:], 1.0), writes=[self.ident[None]])
            P.op("gpsimd", lambda e: e.affine_select(out=self.ident.t[:], in_=self.ident.t[:], pattern=[[-1, 128]],
                                                     compare_op=ALU.is_equal, fill=0.0, base=0, channel_multiplier=1),
                 reads=[self.ident[None]], writes=[self.ident[None]])
            P.op("vector", lambda e: e.memset(self.ones_bf.t[:], 1.0), writes=[self.ones_bf[None]])
            P.op("vector", lambda e: e.memset(self.ones_f.t[:], 1.0), writes=[self.ones_f[None]])
            P.dma("sync", self.adab.t[:], self.inp("ada_bT", [128, 4, 48]), writes=[self.adab[None]])
            P.dma("sync", self.nw.t[:], self.inp("nw", [128, 4, 2, 8]), writes=[self.nw[None]])
            P.dma("sync", self.fnw.t[:], self.inp("fnw", [128, 8]), writes=[self.fnw[None]])
            P.dma("sync", self.ccs.t[:], self.inp("cc", [128, 8, 2]), writes=[self.ccs[None]])
            self.act(self.cs.t[:], self.ccs.t[:], AF.Silu, [self.ccs[None]], [self.cs[None]])

            xT = self.inp("xT", [D, L])
            for c in range(8):
                P.dma("sync", self.h.t[:, c, :], xT[c * 128:(c + 1) * 128, :], writes=RK(self.h, [(c, t) for t in range(4)]))
            if self.add_grid:
                gT = self.inp("gridT", [D, L])
                gb = [self.ar.alloc("grid%d" % n, (L,), F32) for n in range(2)]
                for c in range(8):
                    g = gb[c % 2]
                    hk = RK(self.h, [(c, t) for t in range(4)])
                    P.dma("sync", g.t[:], gT[c * 128:(c + 1) * 128, :], writes=[g[None]])
                    self.tt("vector", self.h.t[:, c, :], self.h.t[:, c, :], g.t[:], ALU.add, hk + [g[None]], hk)

            inter = set()
            for n, (kind, i) in enumerate(self.phases):
                if kind == "ffn" and any(i2 == i + 1 for (_, i2) in self.phases[n + 1:]):
                    inter.add(i + 1)
            for i in self.layers:
                if i not in inter:
                    self.new_epoch()
                    wide = [self.ar.alloc("adawide%d" % n, (8, 768), BF16) for n in range(2)]
                    for _ in self.adaln_gen(i, self.ps[0], wide, 6):
                        pass
            for (kind, i) in self.phases:
                self.new_epoch()
                if kind == "ffn":
                    self.ffn(i, self.adaln_gen(i + 1, self.ps[7]) if (i + 1) in inter else None)
                else:
                    getattr(self, ["mix_ssd", "mix_gmlp", "mix_hyena", "mix_fnet"][i % 4])(i)
            self.new_epoch()
            self.final()
            P.finish("sync")
            P.run_block()
        return self.nc

    def adaln_gen(self, i, psA, wbs=None, nper=1):
        P = self.P
        aw = self.inp("ada_w%d" % i, [D, 6 * D])
        wbs = wbs or self.adw
        for piece in range(48 // nper):
            wb = wbs[piece % 2]
            wcol = nper * 128
            P.dma("gpsimd", wb.t[:], aw[:, piece * wcol:(piece + 1) * wcol].rearrange("(k p) n -> p k n", p=128), writes=[wb[None]])
            for jn in range(nper):
                j = piece * nper + jn
                for k in range(8):
                    self.mm(psA.t[:, 2 * j:2 * j + 2], wb.t[:, k, jn * 128:(jn + 1) * 128], self.cs.t[:, k, :], k == 0, k == 7,
                            [wb[None], self.cs[None]], [psA[None]], inc=(k == 7))
                yield
        self.tt("vector", self.modb.t[:, i, :, :], psA.t[:, 0:96].rearrange("p (j c) -> p j c", c=2),
                self.adab.t[:, i, :].rearrange("p (j o) -> p j o", o=1).to_broadcast([128, 48, 2]), ALU.add,
                [psA[None], self.adab[None]], [self.modb[i]])
        for (slot, which, col, nwi) in ((0, 1, 0, 0), (1, 4, 0, 1), (2, 1, 1, 0)):
            self.stt("vector", self.lv.t[:, i, slot, :], self.modb.t[:, i, which * 8:(which + 1) * 8, col], 1.0,
                     self.nw.t[:, i, nwi, :], ALU.add, ALU.mult, [self.modb[i], self.nw[None]], [self.lv[(i, slot)]])
        yield

    def modv(self, i, which, c, col=0):
        return self.modb.t[:, i, which * 8 + c, col:col + 1]

    def norm_mod(self, src, ntok, Afn, Bfn, dst, extra_reads, ar):
        P = self.P
        TW = min(512, ntok)
        sqb = [ar.alloc("nsq%d" % n, (TW,), BF16) for n in range(2)]
        rsb = [ar.alloc("nrs%d" % n, (TW,), F32) for n in range(2)]
        tmb = [ar.alloc("ntm%d" % n, (TW,), F32) for n in range(2)]
        for t in range(ntok // TW):
            sl = slice(t * TW, (t + 1) * TW)
            pss = self.ps[6 + t % 2]
            rs = rsb[t % 2]
            for c in range(8):
                sq = sqb[c % 2]
                self.tt("gpsimd", sq.t[:], src.t[:, c, sl], src.t[:, c, sl], ALU.mult, [src[(c, t)]], [sq[None]])
                self.mm(pss.t[:, :TW], self.ones_bf.t[:], sq.t[:], c == 0, c == 7, [sq[None], self.ones_bf[None]], [pss[None]])
            self.act(rs.t[:], pss.t[:, :TW], AF.Sqrt, [pss[None]], [rs[None]], bias=EPS, scale=1.0 / D)
            P.op("vector", lambda e, rs=rs: e.reciprocal(out=rs.t[:], in_=rs.t[:]), reads=[rs[None]], writes=[rs[None]])
            for c in range(8):
                tm = tmb[c % 2]
                self.tt("vector", tm.t[:], src.t[:, c, sl], rs.t[:], ALU.mult, [src[(c, t)], rs[None]], [tm[None]])
                self.act(dst.t[:, c, sl], tm.t[:], AF.Identity, [tm[None]] + extra_reads, [dst[(c, t)]],
                         bias=Bfn(c), scale=Afn(c))

    def ffn(self, i, side=None):
        P, ar = self.P, self.ar

        def step():
            if side is not None:
                next(side, None)

        w13d = self.inp("w13_%d" % i, [NF, 128, 2 * 8 * 128])
        w2d = self.inp("w2_%d" % i, [8, 128, NF * 128])
        w13 = [ar.alloc("w13_%d" % n, (2, 8, 128), BF16) for n in range(3)]
        for j in range(3):
            P.dma("gpsimd", w13[j].t[:], w13d[j].rearrange("p (a k f) -> p a k f", a=2, k=8), writes=[w13[j][None]])
        self.norm_mod(self.h, L, lambda c: self.lv.t[:, i, 1, c:c + 1], lambda c: self.modv(i, 3, c), self.hn,
                      [self.lv[(i, 1)], self.modb[i]], ar)
        a = ar.alloc("a", (NF, 1024), BF16)
        w2b = [ar.alloc("w2_%d" % n, (NF, 128), BF16) for n in range(2)]
        stb = [ar.alloc("st%d" % n, (512,), BF16) for n in range(2)]
        n13 = 0
        for half in range(2):
            for j in range(NF):
                wb = w13[j % 3]
                if not (half == 0 and j < 3):
                    P.dma("gpsimd", wb.t[:], w13d[j].rearrange("p (a k f) -> p a k f", a=2, k=8), writes=[wb[None]])
                for tt_ in range(2):
                    T = half * 2 + tt_
                    sl = slice(T * 512, (T + 1) * 512)
                    p1 = self.ps[(n13 % 2) * 2]
                    p3 = self.ps[(n13 % 2) * 2 + 1]
                    stt_ = stb[n13 % 2]
                    n13 += 1
                    hr = RK(self.hn, [(k, T) for k in range(8)])
                    for k in range(8):
                        self.mm(p1.t[:], wb.t[:, 0, k, :], self.hn.t[:, k, sl], k == 0, k == 7, [wb[None], self.hn[(k, T)]], [p1[None]], inc=(k == 7))
                    for k in range(8):
                        self.mm(p3.t[:], wb.t[:, 1, k, :], self.hn.t[:, k, sl], k == 0, k == 7, [wb[None], self.hn[(k, T)]], [p3[None]], inc=(k == 7))
                    self.act(stt_.t[:], p1.t[:], AF.Silu, [p1[None]], [stt_[None]])
                    self.tt("vector", a.t[:, j, tt_ * 512:(tt_ + 1) * 512], stt_.t[:], p3.t[:], ALU.mult,
                            [stt_[None], p3[None]], [a[(j, tt_)]])
                step()
            for dc in range(8):
                w2 = w2b[dc % 2]
                P.dma("gpsimd", w2.t[:], w2d[dc].rearrange("p (j d) -> p j d", j=NF), writes=[w2[None]])
                for tt_ in range(2):
                    T = half * 2 + tt_
                    sl = slice(T * 512, (T + 1) * 512)
                    po = self.ps[4 + (dc * 2 + tt_) % 2]
                    for j in range(NF):
                        self.mm(po.t[:], w2.t[:, j, :], a.t[:, j, tt_ * 512:(tt_ + 1) * 512], j == 0, j == NF - 1,
                                [w2[None], a[(j, tt_)]], [po[None]], inc=(j == NF - 1))
                    self.stt("vector", self.h.t[:, dc, sl], po.t[:], self.modv(i, 5, dc), self.h.t[:, dc, sl], ALU.mult, ALU.add,
                             [po[None], self.modb[i], self.h[(dc, T)]], [self.h[(dc, T)]])
                step()
        if side is not None:
            for _ in side:
                pass

    def final(self):
        P, ar = self.P, self.ar
        yT = self.nc.dram_tensor("yT", [D, L], F32, kind="ExternalOutput").ap()
        ob = [ar.alloc("ob%d" % n, (512,), F32) for n in range(3)]
        if not self.final_norm:
            for c in range(8):
                P.dma("sync", yT[c * 128:(c + 1) * 128, :], self.h.t[:, c, :], reads=RK(self.h, [(c, t) for t in range(4)]))
            return
        sqb = [ar.alloc("fsq%d" % n, (512,), BF16) for n in range(2)]
        rsb = [ar.alloc("frs%d" % n, (512,), F32) for n in range(2)]
        n = 0
        for t in range(4):
            sl = slice(t * 512, (t + 1) * 512)
            pss = self.ps[6 + t % 2]
            rs = rsb[t % 2]
            for c in range(8):
                sq = sqb[c % 2]
                self.tt("gpsimd", sq.t[:], self.h.t[:, c, sl], self.h.t[:, c, sl], ALU.mult, [self.h[(c, t)]], [sq[None]])
                self.mm(pss.t[:], self.ones_bf.t[:], sq.t[:], c == 0, c == 7, [sq[None], self.ones_bf[None]], [pss[None]])
            self.act(rs.t[:], pss.t[:], AF.Sqrt, [pss[None]], [rs[None]], bias=EPS, scale=1.0 / D)
            P.op("vector", lambda e, rs=rs: e.reciprocal(out=rs.t[:], in_=rs.t[:]), reads=[rs[None]], writes=[rs[None]])
            for c in range(8):
                o = ob[n % 3]
                n += 1
                self.stt("vector", o.t[:], self.h.t[:, c, sl], self.fnw.t[:, c:c + 1], rs.t[:], ALU.mult, ALU.mult,
                         [self.h[(c, t)], self.fnw[None], rs[None]], [o[None]])
                P.dma("sync", yT[c * 128:(c + 1) * 128, sl], o.t[:], reads=[o[None]])

    def mix_fnet(self, i):
        P, ar, ar2 = self.P, self.ar, self.ar2
        j = i // 4
        self.norm_phase(i, 0, 0)
        cwd = self.inp("fn_cw", [128, 2 * 2 * 256])
        cld = self.inp("fn_cl", [4, 128, 16 * 2 * 512])
        wod = self.inp("fn_wo", [128, 8 * D])
        bod = self.inp("fn_bo", [128, 8])
        A = ar.alloc("fnA", (16, 2, D), BF16)
        cw = ar.alloc("fncw", (2, 512), BF16)
        wo = ar.alloc("fnwo", (8, D), BF16)
        bo = ar.alloc("fnbo", (8,), F32)
        bg = ar.alloc("fnbg", (8,), F32)
        fT = ar.alloc("fnfT", (8, 512), BF16)
        tmb = [ar.alloc("fntm%d" % n, (512,), F32) for n in range(2)]
        P.dma("gpsimd", cw.t[:], cwd.rearrange("p (k n) -> p k n", k=2), writes=[cw[None]])
        P.dma("gpsimd", wo.t[:], wod.rearrange("p (k n) -> p k n", k=8), writes=[wo[None]])
        P.dma("sync", bo.t[:], bod, writes=[bo[None]])
        self.tt("vector", bg.t[:], bo.t[:], self.modb.t[:, i, 16:24, 0], ALU.mult, [bo[None], self.modb[i]], [bg[None]])
        n = 0
        for tc in range(16):
            T = tc // 4
            for g in range(4):
                pa = self.ps[n % 2]
                n += 1
                for kk in range(2):
                    k = 2 * g + kk
                    self.mm(pa.t[:], self.hn.t[:, k, tc * 128:(tc + 1) * 128], cw.t[:, kk, :], kk == 0, kk == 1,
                            [self.hn[(k, T)], cw[None]], [pa[None]], inc=(kk == 1))
                self.cp("scalar", A.t[:, tc, :, g * 256:(g + 1) * 256], pa.t[:].rearrange("p (a b) -> p a b", a=2),
                        [pa[None]], [A[(tc, g)]])
        P.barrier()
        clb = ar2.alloc("fncl", (16, 2, 512), BF16)
        n = 0
        for T in range(4):
            sl = slice(T * 512, (T + 1) * 512)
            P.dma("gpsimd", clb.t[:], cld[T].rearrange("p (a b c) -> p a b c", a=16, b=2), writes=[clb[None]])
            for dc in range(8):
                pf = self.ps[2 + dc % 2]
                g = dc // 2
                for tc in range(16):
                    for cs_ in range(2):
                        self.mm(pf.t[:], A.t[:, tc, cs_, dc * 128:(dc + 1) * 128], clb.t[:, tc, cs_, :],
                                tc == 0 and cs_ == 0, tc == 15 and cs_ == 1, [A[(tc, g)], clb[None]], [pf[None]],
                                inc=(tc == 15 and cs_ == 1))
                self.cp("scalar", fT.t[:, dc, :], pf.t[:], [pf[None]], [fT[dc]])
            for dc in range(8):
                po = self.ps[4 + dc % 2]
                tm = tmb[dc % 2]
                for k in range(8):
                    self.mm(po.t[:], wo.t[:, k, dc * 128:(dc + 1) * 128], fT.t[:, k, :], k == 0, k == 7,
                            [wo[None], fT[k]], [po[None]], inc=(k == 7))
                self.act(tm.t[:], po.t[:], AF.Identity, [po[None], bg[None], self.modb[i]], [tm[None]],
                         bias=bg.t[:, dc:dc + 1], scale=self.modv(i, 2, dc))
                self.tt("vector", self.h.t[:, dc, sl], self.h.t[:, dc, sl], tm.t[:], ALU.add,
                        [self.h[(dc, T)], tm[None]], [self.h[(dc, T)]])

    def norm_phase(self, i, slot_A, which_B):
        self.norm_mod(self.h, L, lambda c: self.lv.t[:, i, slot_A, c:c + 1], lambda c: self.modv(i, which_B, c), self.hn,
                      [self.lv[(i, slot_A)], self.modb[i]], self.ar)
        self.P.barrier()
        self.ar.reset()

    def mix_gmlp(self, i):
        P, ar = self.P, self.ar
        self.norm_phase(i, 0, 0)
        wud = self.inp("gm_wu", [128, 8 * 2048]).rearrange("p (k n) -> p k n", k=8)
        wvd = self.inp("gm_wv", [128, 8 * 2048]).rearrange("p (k n) -> p k n", k=8)
        wod = self.inp("gm_wo", [128, 16 * D]).rearrange("p (k n) -> p k n", k=16)
        wsd = self.inp("gm_wsT", [128, 8 * 128])
        bsd = self.inp("gm_bs", [1, 8 * 128])
        lnd = self.inp("gm_ln", [128, 2 * 16])
        wv = ar.alloc("gwv", (8, 2048), BF16)
        v32 = ar.alloc("gv32", (2048,), F32)
        vln = ar.alloc("gvln", (2048,), BF16)
        wsT = ar.alloc("gwsT", (8, 128), BF16)
        extra = ar.alloc("gextra", (16, 128), F32)
        sT = ar.alloc("gsT", (16, 512), BF16)
        wub = [ar.alloc("gwu%d" % n, (8, 128), BF16) for n in range(3)]
        ugb = [ar.alloc("gug%d" % n, (512,), BF16) for n in range(2)]
        prod = ar.alloc("gprod", (16, 512), BF16)
        wob = [ar.alloc("gwo%d" % n, (16, 128), BF16) for n in range(2)]
        ln = ar.alloc("gln", (2, 16), F32)
        stats = ar.alloc("gstats", (4, 6), F32)
        mv = ar.alloc("gmv", (4,), F32)
        for q in range(4):
            P.dma("gpsimd", wv.t[:, :, q * 512:(q + 1) * 512], wvd[:, :, q * 512:(q + 1) * 512], writes=[wv[q]])
        P.dma("gpsimd", wsT.t[:], wsd.rearrange("p (g q) -> p g q", g=8), writes=[wsT[None]])
        P.dma("sync", ln.t[:], lnd.rearrange("p (a b) -> p a b", a=2), writes=[ln[None]])
        rs_bc = v32.t[:, 0:1024]
        bs_bc = v32.t[:, 1024:2048]
        P.dma("sync", bs_bc, bsd[0:1, :].to_broadcast([128, 1024]), writes=[v32[None]])
        for hh in range(2):
            pr = self.ps[hh]
            self.mm(pr.t[:], self.ones_bf.t[:], wsT.t[:, hh * 4:(hh + 1) * 4, :].rearrange("p a b -> p (a b)"), True, True,
                    [self.ones_bf[None], wsT[None]], [pr[None]])
            self.cp("vector", rs_bc[:, hh * 512:(hh + 1) * 512], pr.t[:], [pr[None]], [v32[None]])
        for dcx in range(16):
            g = dcx // 2
            self.stt("vector", extra.t[:, dcx, :], rs_bc[:, g * 128:(g + 1) * 128], ln.t[:, 1, dcx:dcx + 1],
                     bs_bc[:, g * 128:(g + 1) * 128], ALU.mult, ALU.add, [v32[None], ln[None]], [extra[dcx]])
        nwu = 0
        STOP = int(os.environ.get('GM_STOP', '99'))
        if STOP <= 1:
            return
        for T in range(4):
            sl = slice(T * 512, (T + 1) * 512)
            for tcc in range(4):
                tc = T * 4 + tcc
                tsl = slice(tc * 128, (tc + 1) * 128)
                for ct in range(4):
                    pv = self.ps[ct % 2]
                    for k in range(8):
                        self.mm(pv.t[:], self.hn.t[:, k, tsl], wv.t[:, k, ct * 512:(ct + 1) * 512], k == 0, k == 7,
                                [self.hn[(k, T)], wv[ct]], [pv[None]], inc=(k == 7))
                    self.act(v32.t[:, ct * 512:(ct + 1) * 512], pv.t[:], AF.Gelu, [pv[None]], [v32[ct]])
                    P.op("vector", lambda e, ct=ct: e.bn_stats(out=stats.t[:, ct, :], in_=v32.t[:, ct * 512:(ct + 1) * 512]),
                         reads=[v32[ct]], writes=[stats[ct]])
                P.op("vector", lambda e: e.bn_aggr(out=mv.t[:, 0:2], in_=stats.t[:].rearrange("p a b -> p (a b)")),
                     reads=[stats[None]], writes=[mv[None]])
                self.act(mv.t[:, 2:3], mv.t[:, 1:2], AF.Sqrt, [mv[None]], [mv[None]], bias=EPS, scale=1.0)
                P.op("vector", lambda e: e.reciprocal(out=mv.t[:, 3:4], in_=mv.t[:, 2:3]), reads=[mv[None]], writes=[mv[None]])
                self.ts("vector", vln.t[:], v32.t[:], mv.t[:, 0:1], mv.t[:, 3:4], ALU.subtract, ALU.mult,
                        [v32[None], mv[None]], [vln[None]])
                if STOP <= 2:
                    return
                for q4 in range(4):
                    pS = self.ps[2 + q4 % 2]
                    for dd in range(4):
                        dcx = q4 * 4 + dd
                        self.mm(pS.t[:, dd * 128:(dd + 1) * 128], vln.t[:, dcx * 128:(dcx + 1) * 128], wsT.t[:, dcx // 2, :], True, True,
                                [vln[None], wsT[None]], [pS[dd]])
                    for dd in range(4):
                        dcx = q4 * 4 + dd
                        self.stt("vector", sT.t[:, dcx, tcc * 128:(tcc + 1) * 128], pS.t[:, dd * 128:(dd + 1) * 128],
                                 ln.t[:, 0, dcx:dcx + 1], extra.t[:, dcx, :], ALU.mult, ALU.add,
                                 [pS[dd], ln[None], extra[dcx]], [sT[(dcx, tcc)]])
                if STOP == 25:
                    return
            if STOP <= 3:
                return
            for fc in range(16):
                wu = wub[nwu % 3]
                ug = ugb[nwu % 2]
                nwu += 1
                P.dma("gpsimd", wu.t[:], wud[:, :, fc * 128:(fc + 1) * 128], writes=[wu[None]])
                pu = self.ps[4 + fc % 2]
                for k in range(8):
                    self.mm(pu.t[:], wu.t[:, k, :], self.hn.t[:, k, sl], k == 0, k == 7, [wu[None], self.hn[(k, T)]], [pu[None]], inc=(k == 7))
                self.act(ug.t[:], pu.t[:], AF.Gelu, [pu[None]], [ug[None]])
                self.tt("vector", prod.t[:, fc, :], ug.t[:], sT.t[:, fc, :], ALU.mult,
                        [ug[None]] + RK(sT, [(fc, q) for q in range(4)]), [prod[fc]])
            if STOP <= 4:
                return
            for dc in range(8):
                wo = wob[dc % 2]
                P.dma("gpsimd", wo.t[:], wod[:, :, dc * 128:(dc + 1) * 128], writes=[wo[None]])
                po = self.ps[6 + dc % 2]
                for fc in range(16):
                    self.mm(po.t[:], wo.t[:, fc, :], prod.t[:, fc, :], fc == 0, fc == 15, [wo[None], prod[fc]], [po[None]], inc=(fc == 15))
                self.stt("vector", self.h.t[:, dc, sl], po.t[:], self.modv(i, 2, dc), self.h.t[:, dc, sl], ALU.mult, ALU.add,
                         [po[None], self.modb[i], self.h[(dc, T)]], [self.h[(dc, T)]])

    def sin_mlp(self, out, ps_in, b_ap, f_ap, tmps, reads, writes):
        P = self.P
        arg, m, ki = tmps
        I32 = mybir.dt.int32
        kiv = ki.t[:].bitcast(I32)
        self.ts("vector", arg.t[:], ps_in, b_ap, f_ap, ALU.add, ALU.mult, reads, [arg[None]])
        self.ts("vector", m.t[:], arg.t[:], 1.0 / (2 * math.pi), 64.0, ALU.mult, ALU.add, [arg[None]], [m[None]])
        self.cp("vector", kiv, m.t[:], [m[None]], [ki[None]])
        self.cp("vector", m.t[:], kiv, [ki[None]], [m[None]])
        self.ts("vector", m.t[:], m.t[:], -64.0, -2 * math.pi, ALU.add, ALU.mult, [m[None]], [m[None]])
        self.tt("vector", arg.t[:], m.t[:], arg.t[:], ALU.add, [m[None], arg[None]], [arg[None]])
        self.ts("vector", m.t[:], arg.t[:], math.pi, -2 * math.pi, ALU.is_gt, ALU.mult, [arg[None]], [m[None]])
        self.tt("vector", arg.t[:], m.t[:], arg.t[:], ALU.add, [m[None], arg[None]], [arg[None]])
        self.act(out, arg.t[:], AF.Sin, [arg[None]], writes)

    def dft_fwd(self, src, ftab, cs_, banks, extra_reads=()):
        for dh in range(2):
            pb = banks[dh]
            for tc in range(16):
                self.mm(pb.t[:], ftab.t[:, tc, cs_, :], src.t[:, tc, dh * 512:(dh + 1) * 512], tc == 0, tc == 15,
                        [ftab[None], src[tc]] + list(extra_reads), [pb[None]], inc=(tc == 15))

    def mix_hyena(self, i):
        P, ar, ar2 = self.P, self.ar, self.ar2
        j = i // 4
        featd = self.inp("hy_featsT", [33, L])
        w0d = self.inp("hy_fw0", [33, 64])
        w1d = self.inp("hy_fw1", [64, 64])
        w2d = self.inp("hy_fw2", [64, 4 * D])
        fbd = self.inp("hy_fb", [64, 4])
        wind = self.inp("hy_win", [L, D])
        fwdd = self.inp("hy_fwd", [16, 128, 16 * 2 * 128])
        invd = self.inp("hy_inv", [16, 128, 32 * 128])
        biasd = self.inp("hy_bias", [2, D])
        wid = self.inp("hy_wi", [24, 128, 8 * 128])
        cvd = self.inp("hy_cv", [128, 24 * 4])
        wod = self.inp("hy_wo", [128, 8 * D])
        Gd = self.scratch("hy_G", [2, 16, 128, 3 * D], F32)
        xd = [self.scratch("hy_x%d" % o, [L, D], BF16) for o in range(2)]

        hid2T = ar.alloc("hid2T", (L,), BF16)
        w2 = ar.alloc("hw2", (4 * D,), BF16)
        acc = ar.alloc("hacc", (4 * D,), F32)
        hp = ar.alloc("hhp", (16, D), BF16)
        hm = ar.alloc("hhm", (16, D), BF16)
        tmps = [ar.alloc("htmp%d" % n, (512,), F32) for n in range(4)]
        featsT = ar2.alloc("featsT", (L,), F32)
        hid1T = ar2.alloc("hid1T", (L,), F32)
        hid2f = ar2.alloc("hid2f", (L,), F32)
        w0 = ar2.alloc("hw0", (64,), F32)
        w1 = ar2.alloc("hw1", (64,), F32)
        fb = ar2.alloc("hfb", (4,), F32)
        st_ = [ar2.alloc("hst%d" % n, (512,), F32) for n in range(3)]
        st64 = [Buf("hst64_%d" % n, b.t[0:64, :]) for n, b in enumerate(st_)]
        P.dma("sync", featsT.t[0:33, :], featd, writes=[featsT[None]])
        P.dma("sync", w0.t[0:33, :], w0d, writes=[w0[None]])
        P.dma("sync", w1.t[0:64, :], w1d, writes=[w1[None]])
        P.dma("sync", fb.t[0:64, :], fbd, writes=[fb[None]])
        P.dma("gpsimd", w2.t[0:64, :], w2d, writes=[w2[None]])
        self.P.op("vector", lambda e: e.memset(acc.t[:], 0.0), writes=[acc[None]])
        for t in range(4):
            sl = slice(t * 512, (t + 1) * 512)
            pp = self.ps[t % 2]
            self.mm(pp.t[0:64, :], w0.t[0:33, :], featsT.t[0:33, sl], True, True, [w0[None], featsT[None]], [pp[None]])
            self.sin_mlp(hid1T.t[0:64, sl], pp.t[0:64, :], fb.t[0:64, 0:1], fb.t[0:64, 2:3],
                         st64,
                         [pp[None], fb[None]], [hid1T[t]])
        for t in range(4):
            sl = slice(t * 512, (t + 1) * 512)
            pp = self.ps[2 + t % 2]
            self.mm(pp.t[0:64, :], w1.t[0:64, :], hid1T.t[0:64, sl], True, True, [w1[None], hid1T[t]], [pp[None]])
            self.sin_mlp(hid2f.t[0:64, sl], pp.t[0:64, :], fb.t[0:64, 1:2], fb.t[0:64, 3:4],
                         st64,
                         [pp[None], fb[None]], [hid2f[t]])
            self.cp("vector", hid2T.t[0:64, sl], hid2f.t[0:64, sl], [hid2f[t]], [hid2T[t]])
        P.barrier()
        ar2.reset()
        winb = [ar2.alloc("hwin%d" % n, (D,), F32) for n in range(2)]

        def filt_tile(lc, ct, pb):
            self.mm(pb.t[:], hid2T.t[0:64, lc * 128:(lc + 1) * 128], w2.t[0:64, ct * 512:(ct + 1) * 512], True, True,
                    [hid2T[None], w2[None]], [pb[None]])

        for lc in range(16):
            wn = winb[lc % 2]
            P.dma("sync", wn.t[:], wind[lc * 128:(lc + 1) * 128, :], writes=[wn[None]])
            for ct in range(8):
                pb = self.ps[ct % 4]
                tm = tmps[ct % 2]
                dh = ct % 2
                filt_tile(lc, ct, pb)
                self.tt("vector", tm.t[:], pb.t[:], wn.t[:, dh * 512:(dh + 1) * 512], ALU.mult, [pb[None], wn[None]], [tm[None]])
                tm2 = tmps[2 + ct % 2]
                self.act(tm2.t[:], tm.t[:], AF.Abs, [tm[None]], [tm2[None]])
                self.tt("gpsimd" if ct % 4 < 3 else "vector", acc.t[:, ct * 512:(ct + 1) * 512], tm2.t[:], acc.t[:, ct * 512:(ct + 1) * 512], ALU.add,
                        [tm2[None], acc[ct]], [acc[ct]])
        for ct in range(8):
            pb = self.ps[4 + ct % 2]
            self.mm(pb.t[:], self.ones_f.t[:], acc.t[:, ct * 512:(ct + 1) * 512], True, True, [self.ones_f[None], acc[ct]], [pb[None]])
            P.op("vector", lambda e, ct=ct, pb=pb: e.reciprocal(out=acc.t[:, ct * 512:(ct + 1) * 512], in_=pb.t[:]),
                 reads=[pb[None]], writes=[acc[ct]])
        w2pm = ar2.alloc("hw2pm", (2, D), BF16)
        for o in range(2):
            for dh in range(2):
                c0 = (0 * 4 + o * 2 + dh) * 512
                c1 = (1 * 4 + o * 2 + dh) * 512
                ta, tb = tmps[0], tmps[1]
                self.tt("vector", ta.t[0:64, :], w2.t[0:64, c0:c0 + 512], acc.t[0:64, c0:c0 + 512], ALU.mult, [w2[None], acc[None]], [ta[None]])
                self.tt("vector", tb.t[0:64, :], w2.t[0:64, c1:c1 + 512], acc.t[0:64, c1:c1 + 512], ALU.mult, [w2[None], acc[None]], [tb[None]])
                self.tt("vector", w2pm.t[0:64, 0, dh * 512:(dh + 1) * 512], ta.t[0:64, :], tb.t[0:64, :], ALU.add, [ta[None], tb[None]], [w2pm[(0, dh)]])
                self.tt("vector", w2pm.t[0:64, 1, dh * 512:(dh + 1) * 512], ta.t[0:64, :], tb.t[0:64, :], ALU.subtract, [ta[None], tb[None]], [w2pm[(1, dh)]])
            nb = 0
            for lc in range(16):
                wn = winb[lc % 2]
                P.dma("sync", wn.t[:], wind[lc * 128:(lc + 1) * 128, :], writes=[wn[None]])
                for dh in range(2):
                    dsl = slice(dh * 512, (dh + 1) * 512)
                    for pm, dst in ((0, hp), (1, hm)):
                        pb = self.ps[nb % 4]
                        nb += 1
                        self.mm(pb.t[:], hid2T.t[0:64, lc * 128:(lc + 1) * 128], w2pm.t[0:64, pm, dsl], True, True,
                                [hid2T[None], w2pm[(pm, dh)]], [pb[None]])
                        self.tt("vector", dst.t[:, lc, dsl], pb.t[:], wn.t[:, dsl], ALU.mult, [pb[None], wn[None]], [dst[lc]])
            P.barrier()
            ar2.reset()
            ftb = [ar2.alloc("hft%d" % n, (16, 2, 128), BF16) for n in range(2)]
            stage = ar2.alloc("hstage", (3, D), F32)
            bias_bc = ar2.alloc("hbias", (D,), F32)
            P.dma("sync", bias_bc.t[:], biasd[o:o + 1, :].to_broadcast([128, D]), writes=[bias_bc[None]])
            for fc in range(16):
                ft = ftb[fc % 2]
                P.dma("gpsimd", ft.t[:], fwdd[fc].rearrange("p (a b c) -> p a b c", a=16, b=2), writes=[ft[None]])
                b0 = 4 * (fc % 2)
                self.dft_fwd(hp, ft, 0, [self.ps[b0], self.ps[b0 + 1]])
                self.dft_fwd(hm, ft, 1, [self.ps[b0 + 2], self.ps[b0 + 3]])
                if fc == 0:
                    self.dft_fwd(hp, ft, 1, [self.ps[4], self.ps[5]])
                for dh in range(2):
                    dsl = slice(dh * 512, (dh + 1) * 512)
                    self.tt("vector", stage.t[:, 0, dsl], self.ps[b0 + dh].t[:], bias_bc.t[:, dsl], ALU.add,
                            [self.ps[b0 + dh][None], bias_bc[None]], [stage[(0, dh)]])
                    self.cp("scalar", stage.t[:, 2, dsl], stage.t[:, 0, dsl], [stage[(0, dh)]], [stage[(2, dh)]])
                    self.cp("scalar", stage.t[:, 1, dsl], self.ps[b0 + 2 + dh].t[:], [self.ps[b0 + 2 + dh][None]], [stage[(1, dh)]])
                    if fc == 0:
                        P.op("vector", lambda e, dsl=dsl: e.memset(stage.t[0:1, 1, dsl], 0.0), reads=[], writes=[stage[(1, dh)]])
                        self.tt("vector", stage.t[0:1, 2, dsl], self.ps[4 + dh].t[0:1, :], bias_bc.t[0:1, dsl], ALU.add,
                                [self.ps[4 + dh][None], bias_bc[None]], [stage[(2, dh)]])
                P.dma("sync", Gd.t[o, fc], stage.t[:].rearrange("p a b -> p (a b)"), reads=[stage[None]], writes=[Gd[(o, fc)]])
            P.barrier()
            ar2.reset()
            winb = [ar2.alloc("hwin%d_%d" % (n, o), (D,), F32) for n in range(2)]
            w2pm = ar2.alloc("hw2pm_%d" % o, (2, D), BF16)

        P.barrier()
        ar.reset()
        ar2.reset()
        self.norm_phase(i, 0, 0)
        z_tok = ar.alloc("hz", (16, D), BF16)
        mark = ar.off
        wib = [ar.alloc("hwi%d" % n, (8, 128), BF16) for n in range(2)]
        raw = [ar.alloc("hraw%d" % n, (L + 2,), F32) for n in range(2)]
        cacc = [ar.alloc("hcacc%d" % n, (L,), F32) for n in range(2)]
        obf = [ar.alloc("hobf%d" % n, (L,), BF16) for n in range(2)]
        xst = [ar.alloc("hxst%d" % n, (16, 128), BF16) for n in range(2)]
        cv = ar.alloc("hcv", (24, 4), F32)
        P.dma("sync", cv.t[:], cvd.rearrange("p (a b) -> p a b", b=4), writes=[cv[None]])
        for n in range(2):
            P.op("vector", lambda e, n=n: e.memset(raw[n].t[:, 0:1], 0.0), writes=[raw[n]["pad"]])
            P.op("vector", lambda e, n=n: e.memset(raw[n].t[:, L + 1:L + 2], 0.0), writes=[raw[n]["pad"]])
        ntrc = {"n": 0}

        def hy_part_b(fcx, ob, ca):
            self.act(ob.t[:], ca.t[:], AF.Identity, [ca[None], cv[None]], [ob[None]], bias=cv.t[:, fcx, 3:4], scale=1.0)
            kind = fcx // 8
            fcol = fcx % 8
            xs_ = xst[fcx % 2]
            for q in range(4):
                pT = self.ps[2 + ntrc["n"] % 2]
                ntrc["n"] += 1
                pTv = pT.t[:].bitcast(BF16)[:, 0:512].rearrange("p (a b) -> p a b", a=4)
                for tt_ in range(4):
                    tc = q * 4 + tt_
                    P.op("tensor", lambda e, pTv=pTv, tt_=tt_, ob=ob, tc=tc: e.transpose(pTv[:, tt_, :], ob.t[:, tc * 128:(tc + 1) * 128], self.ident.t[:]),
                         reads=[ob[None], self.ident[None]], writes=[pT[None]])
                if kind == 0:
                    self.cp("vector", z_tok.t[:, q * 4:(q + 1) * 4, fcol * 128:(fcol + 1) * 128], pTv, [pT[None]],
                            RK(z_tok, [q * 4 + a for a in range(4)]))
                else:
                    self.cp("vector", xs_.t[:, q * 4:(q + 1) * 4, :], pTv, [pT[None]], [xs_[q]])
            if kind > 0:
                xdd = xd[kind - 1]
                P.dma("sync", xdd.t.rearrange("(tc p) f -> p tc f", p=128)[:, :, fcol * 128:(fcol + 1) * 128], xs_.t[:],
                      reads=[xs_[None]], writes=[xdd[fcol]])

        hy_def = [None]
        for fcx in range(24):
            wi = wib[fcx % 2]
            rw = raw[fcx % 2]
            ca = cacc[fcx % 2]
            ob = obf[fcx % 2]
            P.dma("gpsimd", wi.t[:], wid[fcx].rearrange("p (k f) -> p k f", k=8), writes=[wi[None]])
            for T in range(4):
                pp = self.ps[T % 2]
                for k in range(8):
                    self.mm(pp.t[:], wi.t[:, k, :], self.hn.t[:, k, T * 512:(T + 1) * 512], k == 0, k == 7,
                            [wi[None], self.hn[(k, T)]], [pp[None]], inc=(k == 7))
                self.cp("scalar", rw.t[:, 1 + T * 512:1 + (T + 1) * 512], pp.t[:], [pp[None]], [rw[T]])
            rall = [rw[T] for T in range(4)] + [rw["pad"]]
            self.ts("vector", ca.t[:], rw.t[:, 0:L], cv.t[:, fcx, 0:1], 0.0, ALU.mult, ALU.add, rall + [cv[None]], [ca[None]])
            self.stt("vector", ca.t[:], rw.t[:, 1:L + 1], cv.t[:, fcx, 1:2], ca.t[:], ALU.mult, ALU.add, rall + [cv[None], ca[None]], [ca[None]])
            self.stt("vector", ca.t[:], rw.t[:, 2:L + 2], cv.t[:, fcx, 2:3], ca.t[:], ALU.mult, ALU.add, rall + [cv[None], ca[None]], [ca[None]])
            if hy_def[0] is not None:
                hy_def[0]()
            hy_def[0] = (lambda fcx=fcx, ob=ob, ca=ca: hy_part_b(fcx, ob, ca))
        hy_def[0]()
        P.barrier()
        ar.off = mark
        RQ = ar.alloc("hRQ", (32, D), BF16)
        tqa = [ar.alloc("htq%d" % n, (512,), F32) for n in range(2)]
        for o in range(2):
            ar2.reset()
            ftb = [ar2.alloc("hcft%d_%d" % (n, o), (16, 2, 128), BF16) for n in range(2)]
            gtb = [ar2.alloc("hcg%d_%d" % (n, o), (3, D), BF16) for n in range(2)]
            tq = tqa + [ar2.alloc("htqb%d_%d" % (n, o), (512,), F32) for n in range(2)]
            for fc in range(16):
                ft = ftb[fc % 2]
                gt = gtb[fc % 2]
                P.dma("gpsimd", ft.t[:], fwdd[fc].rearrange("p (a b c) -> p a b c", a=16, b=2), writes=[ft[None]])
                P.dma("gpsimd", gt.t[:], Gd.t[o, fc].rearrange("p (a b) -> p a b", a=3), reads=[Gd[(o, fc)]], writes=[gt[None]])
                b0 = 4 * (fc % 2)
                self.dft_fwd(z_tok, ft, 0, [self.ps[b0], self.ps[b0 + 1]])
                self.dft_fwd(z_tok, ft, 1, [self.ps[b0 + 2], self.ps[b0 + 3]])
                for dh in range(2):
                    dsl = slice(dh * 512, (dh + 1) * 512)
                    pA, pB = self.ps[b0 + dh], self.ps[b0 + 2 + dh]
                    t1, t2, t3, t4 = tq
                    self.tt("vector", t1.t[:], pA.t[:], gt.t[:, 0, dsl], ALU.mult, [pA[None], gt[None]], [t1[None]])
                    self.tt("vector", t2.t[:], pB.t[:], gt.t[:, 1, dsl], ALU.mult, [pB[None], gt[None]], [t2[None]])
                    self.tt("vector", RQ.t[:, fc, dsl], t1.t[:], t2.t[:], ALU.subtract, [t1[None], t2[None]], [RQ[fc]])
                    self.tt("vector", t3.t[:], pA.t[:], gt.t[:, 1, dsl], ALU.mult, [pA[None], gt[None]], [t3[None]])
                    self.tt("vector", t4.t[:], pB.t[:], gt.t[:, 2, dsl], ALU.mult, [pB[None], gt[None]], [t4[None]])
                    self.tt("vector", RQ.t[:, 16 + fc, dsl], t3.t[:], t4.t[:], ALU.add, [t3[None], t4[None]], [RQ[16 + fc]])
            P.barrier()
            ar2.reset()
            ivb = [ar2.alloc("hiv%d_%d" % (n, o), (32, 128), BF16) for n in range(2)]
            xcb = [ar2.alloc("hxc%d_%d" % (n, o), (D,), BF16) for n in range(2)]
            for tc in range(16):
                iv = ivb[tc % 2]
                xc = xcb[tc % 2]
                P.dma("gpsimd", iv.t[:], invd[tc].rearrange("p (a b) -> p a b", a=32), writes=[iv[None]])
                P.dma("sync", xc.t[:], xd[o].t[tc * 128:(tc + 1) * 128, :], reads=[xd[o][None]], writes=[xc[None]])
                for dh in range(2):
                    dsl = slice(dh * 512, (dh + 1) * 512)
                    py = self.ps[4 + dh]
                    for fcx in range(32):
                        self.mm(py.t[:], iv.t[:, fcx, :], RQ.t[:, fcx, dsl], fcx == 0, fcx == 31, [iv[None], RQ[fcx]], [py[None]], inc=(fcx == 31))
                    self.tt("vector", z_tok.t[:, tc, dsl], py.t[:], xc.t[:, dsl], ALU.mult, [py[None], xc[None]], [z_tok[tc]])
            P.barrier()
        ar2.reset()
        wo = ar2.alloc("hwo", (8, D), BF16)
        z3T = ar2.alloc("hz3T", (8, 512), BF16)
        P.dma("gpsimd", wo.t[:], wod.rearrange("p (k n) -> p k n", k=8), writes=[wo[None]])
        ntr = 0
        for T in range(4):
            sl = slice(T * 512, (T + 1) * 512)
            for tcc in range(4):
                tc = T * 4 + tcc
                for q in range(2):
                    pT = self.ps[ntr % 2]
                    ntr += 1
                    pTv = pT.t[:].bitcast(BF16)[:, 0:512].rearrange("p (a b) -> p a b", a=4)
                    for dd in range(4):
                        dc = q * 4 + dd
                        P.op("tensor", lambda e, pTv=pTv, dd=dd, tc=tc, dc=dc: e.transpose(pTv[:, dd, :], z_tok.t[:, tc, dc * 128:(dc + 1) * 128], self.ident.t[:]),
                             reads=[z_tok[tc], self.ident[None]], writes=[pT[None]])
                    self.cp("vector", z3T.t[:, q * 4:(q + 1) * 4, tcc * 128:(tcc + 1) * 128], pTv, [pT[None]],
                            RK(z3T, [q * 4 + a for a in range(4)]))
            for dc in range(8):
                po = self.ps[2 + dc % 2]
                for k in range(8):
                    self.mm(po.t[:], wo.t[:, k, dc * 128:(dc + 1) * 128], z3T.t[:, k, :], k == 0, k == 7, [wo[None], z3T[k]], [po[None]], inc=(k == 7))
                self.stt("vector", self.h.t[:, dc, sl], po.t[:], self.modv(i, 2, dc), self.h.t[:, dc, sl], ALU.mult, ALU.add,
                         [po[None], self.modb[i], self.h[(dc, T)]], [self.h[(dc, T)]])

    def mix_ssd(self, i):
        P, ar, ar2 = self.P, self.ar, self.ar2
        j = i // 4
        wid = self.inp("ss_wi", [32, 128, 8 * 128])
        wdtd = self.inp("ss_wdt", [128, 8 * 64])
        cvd = self.inp("ss_cv", [128, 32 * 4])
        smalld = self.inp("ss_small", [1, 64 + 64 + 32])
        nwd = self.inp("ss_nw", [1, 2048])
        wod = self.inp("ss_wo", [128, 16 * D])
        ctxd = self.inp("ctxT", [D, LC])
        zd = self.scratch("ss_z", [L, 2048], BF16)
        xd = self.scratch("ss_x", [L, 2048], BF16)
        btokd = self.scratch("ss_btok", [L, 1024], BF16)
        BTd = self.scratch("ss_BT", [1024, L], BF16)
        CTd = self.scratch("ss_CT", [1024, L], BF16)
        xcd = self.scratch("ss_xc", [LC, 2048], BF16)
        bcd = self.scratch("ss_bc", [LC, 1024], BF16)
        Hbd = self.scratch("ss_Hb", [16, 128, 2048], BF16)
        ynTd = self.scratch("ss_ynT", [2048, L], BF16)

        self.norm_phase(i, 0, 0)
        dt_tok = ar.alloc("sdt", (18, 64), F32)
        small = ar.alloc("ssmall", (160,), F32)
        a_bc = ar.alloc("sabc", (64,), F32)
        mark0 = ar.off
        P.dma("sync", small.t[:], smalld[0:1, :].to_broadcast([128, 160]), writes=[small[None]])
        self.act(a_bc.t[:], small.t[:, 0:64], AF.Exp, [small[None]], [a_bc[None]])
        self.ts("vector", a_bc.t[:], a_bc.t[:], -1.0, 0.0, ALU.mult, ALU.add, [a_bc[None]], [a_bc[None]])
        dtb_bc = small.t[:, 64:128]
        dsk_bc = small.t[:, 128:160]
        hc = ar.alloc("shc", (8, LC), F32)
        hcn = ar.alloc("shcn", (8, LC), BF16)
        for c in range(8):
            P.dma("sync", hc.t[:, c, :], ctxd[c * 128:(c + 1) * 128, :], writes=[hc[(c, 0)]])
        self.norm_mod(hc, LC, lambda c: self.lv.t[:, i, 2, c:c + 1], lambda c: self.modv(i, 0, c, 1), hcn,
                      [self.lv[(i, 2)], self.modb[i]], ar)
        wib = [ar.alloc("swi%d" % n, (8, 128), BF16) for n in range(2)]
        raw = [ar.alloc("sraw%d" % n, (L + 2,), F32) for n in range(2)]
        cacc = [ar.alloc("scacc%d" % n, (L,), F32) for n in range(2)]
        obf = [ar.alloc("sobf%d" % n, (L,), BF16) for n in range(2)]
        xst = [ar.alloc("sxst%d" % n, (16, 128), BF16) for n in range(2)]
        cv = ar.alloc("scv", (32, 4), F32)
        wdt = ar.alloc("swdt", (8, 64), BF16)
        sp = [ar.alloc("ssp%d" % n, (64,), F32) for n in range(4)]
        P.dma("sync", cv.t[:], cvd.rearrange("p (a b) -> p a b", b=4), writes=[cv[None]])
        P.dma("gpsimd", wdt.t[:], wdtd.rearrange("p (k f) -> p k f", k=8), writes=[wdt[None]])
        for n in range(2):
            P.op("vector", lambda e, n=n: e.memset(raw[n].t[:, 0:1], 0.0), writes=[raw[n]["pad"]])
        cnt = {"n": 0, "tr": 0}

        def proj_chunk(fcx, src, srckeys, ntok, dst_tok, dst_fm):
            n = cnt["n"]
            cnt["n"] += 1
            wi, rw, ca, ob, xs_ = wib[n % 2], raw[n % 2], cacc[n % 2], obf[n % 2], xst[n % 2]
            TW = min(512, ntok)
            NT = ntok // TW
            P.dma("gpsimd", wi.t[:], wid[fcx - 16].rearrange("p (k f) -> p k f", k=8), writes=[wi[None]])
            is_z = fcx < 16
            for T in range(NT):
                pp = self.ps[T % 2]
                for k in range(8):
                    self.mm(pp.t[:, 0:TW], wi.t[:, k, :], src.t[:, k, T * TW:(T + 1) * TW], k == 0, k == 7,
                            [wi[None], srckeys(k, T)], [pp[None]], inc=(k == 7))
                if is_z:
                    self.act(ob.t[:, T * TW:(T + 1) * TW], pp.t[:, 0:TW], AF.Silu, [pp[None]], [ob[T]])
                else:
                    self.cp("scalar", rw.t[:, 1 + T * TW:1 + (T + 1) * TW], pp.t[:, 0:TW], [pp[None]], [rw[T]])
            if not is_z:
                cx = fcx - 16
                P.op("vector", lambda e, rw=rw: e.memset(rw.t[:, ntok + 1:ntok + 2], 0.0), writes=[rw["pad2"]])
                rall = [rw[T] for T in range(NT)] + [rw["pad"], rw["pad2"]]
                self.ts("vector", ca.t[:, 0:ntok], rw.t[:, 0:ntok], cv.t[:, cx, 0:1], 0.0, ALU.mult, ALU.add, rall + [cv[None]], [ca[None]])
                self.stt("vector", ca.t[:, 0:ntok], rw.t[:, 1:ntok + 1], cv.t[:, cx, 1:2], ca.t[:, 0:ntok], ALU.mult, ALU.add, rall + [cv[None], ca[None]], [ca[None]])
                self.stt("vector", ca.t[:, 0:ntok], rw.t[:, 2:ntok + 2], cv.t[:, cx, 2:3], ca.t[:, 0:ntok], ALU.mult, ALU.add, rall + [cv[None], ca[None]], [ca[None]])
            prev = cnt.get("deferred")
            cnt["deferred"] = None
            if prev is not None:
                prev()
            cnt["deferred"] = lambda: part_b(dst_tok, dst_fm, ntok, ob, xs_, ca, fcx - 16)

        def part_b(dst_tok, dst_fm, ntok, ob, xs_, ca, cx):
            self.act(ob.t[:, 0:ntok], ca.t[:, 0:ntok], AF.Silu, [ca[None], cv[None]], [ob[None]], bias=cv.t[:, cx, 3:4], scale=1.0)
            if dst_fm is not None:
                dbuf, row0 = dst_fm
                P.dma("sync", dbuf.t[row0:row0 + 128, 0:ntok], ob.t[:, 0:ntok], reads=[ob[None]], writes=[dbuf[row0]])
            if dst_tok is not None:
                dbuf, fcol = dst_tok
                ntc = ntok // 128
                for q in range((ntc + 3) // 4):
                    nq = min(4, ntc - q * 4)
                    pT = self.ps[2 + cnt["tr"] % 2]
                    cnt["tr"] += 1
                    pTv = pT.t[:].bitcast(BF16)[:, 0:512].rearrange("p (a b) -> p a b", a=4)
                    for tt_ in range(nq):
                        tc = q * 4 + tt_
                        P.op("tensor", lambda e, pTv=pTv, tt_=tt_, ob=ob, tc=tc: e.transpose(pTv[:, tt_, :], ob.t[:, tc * 128:(tc + 1) * 128], self.ident.t[:]),
                             reads=[ob[None], self.ident[None]], writes=[pT[None]])
                    self.cp("vector", xs_.t[:, q * 4:q * 4 + nq, :], pTv[:, 0:nq, :], [pT[None]], [xs_[q]])
                P.dma("sync", dbuf.t.rearrange("(tc p) f -> p tc f", p=128)[:, :, fcol * 128:(fcol + 1) * 128], xs_.t[:, 0:ntc, :],
                      reads=[xs_[None]], writes=[dbuf[fcol]])

        hk = lambda k, T: self.hn[(k, T)]
        ck = lambda k, T: hcn[(k, 0)]
        wzd = self.inp("ss_wz", [128, 8 * 2048]).rearrange("p (k n) -> p k n", k=8)
        wzb = [ar.alloc("swz%d" % n, (8, 512), BF16) for n in range(2)]
        zst = [ar.alloc("szst%d" % n, (512,), BF16) for n in range(3)]
        nz = 0
        for piece in range(4):
            wz = wzb[piece % 2]
            P.dma("gpsimd", wz.t[:], wzd[:, :, piece * 512:(piece + 1) * 512], writes=[wz[None]])
            for tc in range(16):
                pp = self.ps[4 + nz % 2]
                zs_ = zst[nz % 3]
                nz += 1
                for k in range(8):
                    self.mm(pp.t[:], self.hn.t[:, k, tc * 128:(tc + 1) * 128], wz.t[:, k, :], k == 0, k == 7,
                            [self.hn[(k, tc // 4)], wz[None]], [pp[None]], inc=(k == 7))
                self.act(zs_.t[:], pp.t[:], AF.Silu, [pp[None]], [zs_[None]])
                P.dma("sync", zd.t[tc * 128:(tc + 1) * 128, piece * 512:(piece + 1) * 512], zs_.t[:], reads=[zs_[None]], writes=[zd[(tc, piece)]])
        for fcx in range(16, 32):
            proj_chunk(fcx, self.hn, hk, L, (xd, fcx - 16), None)
            proj_chunk(fcx, hcn, ck, LC, (xcd, fcx - 16), None)
        for fcx in range(32, 40):
            proj_chunk(fcx, self.hn, hk, L, (btokd, fcx - 32), (BTd, (fcx - 32) * 128))
            proj_chunk(fcx, hcn, ck, LC, (bcd, fcx - 32), None)
        for fcx in range(40, 48):
            proj_chunk(fcx, self.hn, hk, L, None, (CTd, (fcx - 40) * 128))
        if cnt.get("deferred") is not None:
            cnt["deferred"]()
            cnt["deferred"] = None
        for c in range(18):
            pp = self.ps[4 + c % 2]
            for k in range(8):
                if c < 16:
                    lhs, rk = self.hn.t[:, k, c * 128:(c + 1) * 128], self.hn[(k, c // 4)]
                else:
                    lhs, rk = hcn.t[:, k, (c - 16) * 128:(c - 15) * 128], hcn[(k, 0)]
                self.mm(pp.t[:, 0:64], lhs, wdt.t[:, k, :], k == 0, k == 7, [rk, wdt[None]], [pp[None]], inc=(k == 7))
            x_, ax, e_, l_ = sp
            self.tt("vector", x_.t[:], pp.t[:, 0:64], dtb_bc, ALU.add, [pp[None], small[None]], [x_[None]])
            self.act(ax.t[:], x_.t[:], AF.Abs, [x_[None]], [ax[None]])
            self.act(e_.t[:], ax.t[:], AF.Exp, [ax[None]], [e_[None]], scale=-1.0)
            self.act(l_.t[:], e_.t[:], AF.Ln, [e_[None]], [l_[None]], bias=1.0, scale=1.0)
            self.ts("vector", ax.t[:], x_.t[:], 0.0, 0.0, ALU.max, ALU.add, [x_[None]], [ax[None]])
            self.tt("vector", dt_tok.t[:, c, :], ax.t[:], l_.t[:], ALU.add, [ax[None], l_[None]], [dt_tok[c]])
        P.barrier()
        ar.off = mark0
        ar2.reset()

        tri = [ar.alloc("stri%d" % n, (128,), F32) for n in range(2)]
        neg = [ar.alloc("sneg%d" % n, (128,), F32) for n in range(2)]
        for d_, (pat, cm) in enumerate((([[1, 128]], -1), ([[-1, 128]], 1))):
            P.op("gpsimd", lambda e, d_=d_: e.memset(tri[d_].t[:], 1.0), writes=[tri[d_][None]])
            P.op("gpsimd", lambda e, d_=d_, pat=pat, cm=cm: e.affine_select(out=tri[d_].t[:], in_=tri[d_].t[:], pattern=pat,
                                                                             compare_op=ALU.is_ge, fill=0.0, base=0, channel_multiplier=cm),
                 reads=[tri[d_][None]], writes=[tri[d_][None]])
        for d_, (pat, cm) in enumerate((([[-1, 128]], 1), ([[1, 128]], -1))):
            P.op("gpsimd", lambda e, d_=d_: e.memset(neg[d_].t[:], -30000.0), writes=[neg[d_][None]])
            P.op("gpsimd", lambda e, d_=d_, pat=pat, cm=cm: e.affine_select(out=neg[d_].t[:], in_=neg[d_].t[:], pattern=pat,
                                                                             compare_op=ALU.is_gt, fill=0.0, base=0, channel_multiplier=cm),
                 reads=[neg[d_][None]], writes=[neg[d_][None]])
        neg4 = [ar.alloc("sneg4_%d" % n, (4, 128), BF16) for n in range(2)]
        for d_ in range(2):
            self.cp("vector", neg4[d_].t[:], neg[d_].t[:].rearrange("p (o l) -> p o l", o=1).to_broadcast([128, 4, 128]),
                    [neg[d_][None]], [neg4[d_][None]])
        nadt = [ar.alloc("snadt%d" % n, (32,), F32) for n in range(2)]
        ahi = [ar.alloc("sahi%d" % n, (32,), BF16) for n in range(2)]
        alo = [ar.alloc("salo%d" % n, (32,), BF16) for n in range(2)]
        nahi = [ar.alloc("snahi%d" % n, (32,), BF16) for n in range(2)]
        nalo = [ar.alloc("snalo%d" % n, (32,), BF16) for n in range(2)]
        tri_bf = [ar.alloc("stribf%d" % n, (128,), BF16) for n in range(2)]
        for d_ in range(2):
            self.cp("vector", tri_bf[d_].t[:], tri[d_].t[:], [tri[d_][None]], [tri_bf[d_][None]])
        H = [ar.alloc("sH%d" % n, (2048,), F32) for n in range(2)]
        for d_ in range(2):
            P.op("gpsimd", lambda e, d_=d_: e.memset(H[d_].t[:], 0.0), writes=[H[d_][None]])
        xtb = [ar.alloc("sxt%d" % n, (2048,), BF16) for n in range(2)]
        btb = [ar.alloc("sbt%d" % n, (8, 128), BF16) for n in range(2)]
        dxs = [ar.alloc("sdxs%d" % n, (2048,), BF16) for n in range(2)]
        cst = [ar.alloc("scst%d" % n, (64,), F32) for n in range(2)]
        adt = [ar.alloc("sadt%d" % n, (32,), F32) for n in range(2)]
        sw = [ar.alloc("ssw%d" % n, (32,), F32) for n in range(4)]
        edec = [ar.alloc("sedec%d" % n, (32,), F32) for n in range(2)]
        nld = {"n": 0}

        def load_xb(xsrc, bsrc, r0):
            n = nld["n"]
            nld["n"] += 1
            xt, bt = xtb[n % 2], btb[n % 2]
            P.dma("sync", xt.t[:], xsrc.t[r0:r0 + 128, :], reads=[xsrc[None]], writes=[xt[None]])
            P.dma("sync", bt.t[:], bsrc.t[r0:r0 + 128, :].rearrange("p (g n) -> p g n", g=8), reads=[bsrc[None]], writes=[bt[None]])
            return xt, bt

        def cums(d_, c):
            dtc = dt_tok.t[:, c, d_ * 32:(d_ + 1) * 32]
            self.tt("vector", adt[d_].t[:], dtc, a_bc.t[:, d_ * 32:(d_ + 1) * 32], ALU.mult, [dt_tok[c], a_bc[None]], [adt[d_][None]])
            pc = self.ps[6]
            self.mm(pc.t[:, 0:32], tri[d_].t[:], adt[d_].t[:], True, True, [tri[d_][None], adt[d_][None]], [pc[None]])
            self.mm(pc.t[:, 32:64], self.ones_f.t[:], adt[d_].t[:], True, True, [self.ones_f[None], adt[d_][None]], [pc[None]])
            self.cp("scalar", cst[d_].t[:], pc.t[:, 0:64], [pc[None]], [cst[d_][None]])
            self.cp("vector", ahi[d_].t[:], adt[d_].t[:], [adt[d_][None]], [ahi[d_][None]])
            self.tt("vector", alo[d_].t[:], adt[d_].t[:], ahi[d_].t[:], ALU.subtract, [adt[d_][None], ahi[d_][None]], [alo[d_][None]])
            self.ts("gpsimd", nahi[d_].t[:], ahi[d_].t[:], -1.0, 0.0, ALU.mult, ALU.add, [ahi[d_][None]], [nahi[d_][None]])
            self.ts("gpsimd", nalo[d_].t[:], alo[d_].t[:], -1.0, 0.0, ALU.mult, ALU.add, [alo[d_][None]], [nalo[d_][None]])

        def state_step(d_, c, xt, bt):
            dtc = dt_tok.t[:, c, d_ * 32:(d_ + 1) * 32]
            dd, ee, ww = sw[d_ * 2], sw[d_ * 2 + 1], sw[d_ * 2]
            self.tt("vector", dd.t[:], cst[d_].t[:, 32:64], cst[d_].t[:, 0:32], ALU.subtract, [cst[d_][None]], [dd[None]])
            self.act(ee.t[:], dd.t[:], AF.Exp, [dd[None]], [ee[None]])
            self.tt("vector", ww.t[:], ee.t[:], dtc, ALU.mult, [ee[None], dt_tok[c]], [ww[None]])
            self.act(edec[d_].t[:], cst[d_].t[:, 32:64], AF.Exp, [cst[d_][None]], [edec[d_][None]])
            dx = dxs[d_]
            self.tt("vector", dx.t[:].rearrange("p (h q) -> p h q", h=32), xt.t[:].rearrange("p (h q) -> p h q", h=32),
                    ww.t[:].rearrange("p (h o) -> p h o", o=1).to_broadcast([128, 32, 64]), ALU.mult, [xt[None], ww[None]], [dx[None]])
            Hd = H[d_]
            self.tt("gpsimd", Hd.t[:].rearrange("p (h q) -> p h q", h=32), Hd.t[:].rearrange("p (h q) -> p h q", h=32),
                    edec[d_].t[:].rearrange("p (h o) -> p h o", o=1).to_broadcast([128, 32, 64]), ALU.mult, [Hd[None], edec[d_][None]], [Hd[None]])
            for gp in range(4):
                pS = self.ps[7]
                for gi in range(2):
                    g = gp * 2 + gi
                    self.mm(pS.t[:, gi * 256:(gi + 1) * 256], bt.t[:, g, :], dx.t[:, g * 256:(g + 1) * 256], True, True,
                            [bt[None], dx[None]], [pS[None]])
                self.tt("vector", Hd.t[:, gp * 512:(gp + 1) * 512], Hd.t[:, gp * 512:(gp + 1) * 512], pS.t[:], ALU.add,
                        [Hd[None], pS[None]], [Hd[None]])

        for c in (0, 1):
            xt, bt = load_xb(xcd, bcd, c * 128)
            cums(0, 16 + c)
            state_step(0, 16 + c, xt, bt)
        for c in (1, 0):
            xt, bt = load_xb(xcd, bcd, c * 128)
            cums(1, 16 + c)
            state_step(1, 16 + c, xt, bt)
        hbst = [ar.alloc("shbst%d" % n, (2048,), BF16) for n in range(1)]
        for c in range(15, -1, -1):
            hb_ = hbst[0]
            self.cp("scalar", hb_.t[:], H[1].t[:], [H[1][None]], [hb_[None]])
            P.dma("sync", Hbd.t[c], hb_.t[:], reads=[hb_[None]], writes=[Hbd[c]])
            if c > 0:
                xt, bt = load_xb(xd, btokd, c * 128)
                cums(1, c)
                state_step(1, c, xt, bt)

        nw_bc = ar.alloc("snw", (2048,), F32)
        P.dma("sync", nw_bc.t[:], nwd[0:1, :].to_broadcast([128, 2048]), writes=[nw_bc[None]])
        ztb = [ar.alloc("szt%d" % n, (2048,), BF16) for n in range(2)]
        BTb = [ar.alloc("sBT%d" % n, (8, 128), BF16) for n in range(2)]
        CTb = [ar.alloc("sCT%d" % n, (8, 128), BF16) for n in range(2)]
        Hbb = [ar.alloc("sHbb%d" % n, (2048,), BF16) for n in range(2)]
        Hfb = ar.alloc("sHfb", (2048,), BF16)
        xsdt = [ar.alloc("sxsdt%d" % n, (2048,), BF16) for n in range(2)]
        ea = ar.alloc("sea", (64,), F32)
        cscol = ar.alloc("scscol", (64,), F32)
        yacc = ar2.alloc("syacc", (2048,), F32)
        Eb = [ar2.alloc("sE%d" % n, (4, 128), BF16) for n in range(4)]
        xdsk = ar2.alloc("sxdsk", (2048,), BF16)
        Mb = [ar2.alloc("sM%d" % n, (4, 128), BF16) for n in range(4)]
        GTs = [ar.alloc("sGT%d" % n, (128,), BF16) for n in range(2)]
        t12 = [ar2.alloc("st12_%d" % n, (512,), F32) for n in range(2)]
        junk = ar.alloc("sjunk", (256,), BF16)
        ss = ar.alloc("sss", (16,), F32)
        ynb = ar2.alloc("synb", (2048,), BF16)
        ynT = [ar.alloc("synT%d" % n, (16, 128), BF16) for n in range(1)]
        nrt = {"n": 0, "m": 0, "g": 0, "t": 0}
        for c in range(16):
            r0 = c * 128
            xt, bt = load_xb(xd, btokd, r0)
            zt, BT, CT, Hbc = ztb[c % 2], BTb[c % 2], CTb[c % 2], Hbb[c % 2]
            P.dma("sync", zt.t[:], zd.t[r0:r0 + 128, :], reads=[zd[None]], writes=[zt[None]])
            P.dma("sync", BT.t[:], BTd.t.rearrange("(g n) t -> n g t", n=128)[:, :, r0:r0 + 128], reads=[BTd[None]], writes=[BT[None]])
            P.dma("sync", CT.t[:], CTd.t.rearrange("(g n) t -> n g t", n=128)[:, :, r0:r0 + 128], reads=[CTd[None]], writes=[CT[None]])
            P.dma("sync", Hbc.t[:], Hbd.t[c], reads=[Hbd[c]], writes=[Hbc[None]])
            for d_ in range(2):
                cums(d_, c)
                self.cp("vector", cscol.t[:, d_ * 32:(d_ + 1) * 32], cst[d_].t[:, 0:32], [cst[d_][None]], [cscol[d_]])
                dtc = dt_tok.t[:, c, d_ * 32:(d_ + 1) * 32]
                self.tt("vector" if d_ == 0 else "gpsimd", xsdt[d_].t[:].rearrange("p (h q) -> p h q", h=32), xt.t[:].rearrange("p (h q) -> p h q", h=32),
                        dtc.rearrange("p (h o) -> p h o", o=1).to_broadcast([128, 32, 64]), ALU.mult, [xt[None], dt_tok[c]], [xsdt[d_][None]])
            self.tt("vector", xdsk.t[:].rearrange("p (h q) -> p h q", h=32), xt.t[:].rearrange("p (h q) -> p h q", h=32),
                    dsk_bc.rearrange("p (h o) -> p h o", o=1).to_broadcast([128, 32, 64]), ALU.mult, [xt[None], small[None]], [xdsk[None]])
            self.act(ea.t[:], cscol.t[:], AF.Exp, [cscol[None]], [ea[None]])
            self.cp("scalar", Hfb.t[:], H[0].t[:], [H[0][None]], [Hfb[None]])
            pyd, pyf, pyb = self.ps[3], self.ps[4], self.ps[5]
            v3 = lambda ap: ap.rearrange("p (h q) -> p h q", h=8)
            bc8 = lambda ap: ap.rearrange("p (h o) -> p h o", o=1).to_broadcast([128, 8, 64])

            def emit_y(g, Mg):
                gp, gi = divmod(g, 2)
                if gi == 0:
                    self.mm(pyd.t[:], self.ident.t[:], xdsk.t[:, gp * 512:(gp + 1) * 512], True, False,
                            [self.ident[None], xdsk[None]], [pyd[None]])
                for hh in range(4):
                    hcol = slice((g * 4 + hh) * 64, (g * 4 + hh + 1) * 64)
                    ocol = slice((gi * 4 + hh) * 64, (gi * 4 + hh + 1) * 64)
                    self.mm(pyd.t[:, ocol], Mg[0].t[:, hh, :], xsdt[0].t[:, hcol], False, False,
                            [Mg[0][None], xsdt[0][None]], [pyd[None]])
                    self.mm(pyd.t[:, ocol], Mg[1].t[:, hh, :], xsdt[1].t[:, hcol], False, gi == 1 and hh == 3,
                            [Mg[1][None], xsdt[1][None]], [pyd[None]])
                self.mm(pyf.t[:, gi * 256:(gi + 1) * 256], CT.t[:, g, :], Hfb.t[:, g * 256:(g + 1) * 256], True, True,
                        [CT[None], Hfb[None]], [pyf[None]])
                self.mm(pyb.t[:, gi * 256:(gi + 1) * 256], CT.t[:, g, :], Hbc.t[:, g * 256:(g + 1) * 256], True, True,
                        [CT[None], Hbc[None]], [pyb[None]])
                if gi == 1:
                    csl = slice(gp * 512, (gp + 1) * 512)
                    hs = slice(gp * 8, gp * 8 + 8)
                    t1, t2 = t12
                    self.tt("vector", v3(t1.t[:]), v3(pyf.t[:]), bc8(ea.t[:, hs]), ALU.mult, [pyf[None], ea[None]], [t1[None]])
                    self.tt("vector", v3(t2.t[:]), v3(pyb.t[:]), bc8(ea.t[:, 32 + gp * 8:32 + gp * 8 + 8]), ALU.mult, [pyb[None], ea[None]], [t2[None]])
                    self.tt("vector", yacc.t[:, csl], pyd.t[:], t1.t[:], ALU.add, [pyd[None], t1[None]], [yacc[gp]])
                    self.tt("gpsimd", yacc.t[:, csl], yacc.t[:, csl], t2.t[:], ALU.add, [yacc[gp], t2[None]], [yacc[gp]])
                    self.tt("gpsimd", yacc.t[:, csl], yacc.t[:, csl], zt.t[:, csl], ALU.mult, [yacc[gp], zt[None]], [yacc[gp]])

            pending = None
            for g in range(8):
                pG = self.ps[0]
                GT = GTs[nrt["g"] % 2]
                nrt["g"] += 1
                self.mm(pG.t[:, 0:128], BT.t[:, g, :], CT.t[:, g, :], True, True, [BT[None], CT[None]], [pG[None]])
                self.cp("scalar", GT.t[:], pG.t[:, 0:128], [pG[None]], [GT[None]])
                Mg = []
                for d_ in range(2):
                    n = nrt["n"]
                    nrt["n"] += 1
                    E_ = Eb[n % 4]
                    M_ = Mb[nrt["m"] % 4]
                    nrt["m"] += 1
                    pc = self.ps[1 + n % 2]
                    h0 = g * 4
                    bc4 = lambda b: b.t[:, h0:h0 + 4].rearrange("p (h o) -> p h o", o=1).to_broadcast([128, 4, 128])
                    self.mm(pc.t[:], tri_bf[d_].t[:], bc4(nahi[d_]), True, False, [tri_bf[d_][None], nahi[d_][None]], [pc[None]], inc=False)
                    self.mm(pc.t[:], tri_bf[d_].t[:], bc4(nalo[d_]), False, False, [tri_bf[d_][None], nalo[d_][None]], [pc[None]], inc=False)
                    self.mm(pc.t[:], self.ident.t[:], neg4[d_].t[:].rearrange("p a b -> p (a b)"), False, False,
                            [self.ident[None], neg4[d_][None]], [pc[None]], inc=False)
                    for hh in range(4):
                        self.mm(pc.t[:, hh * 128:(hh + 1) * 128], ahi[d_].t[:, h0 + hh:h0 + hh + 1].to_broadcast([128, 128]), tri_bf[d_].t[:],
                                False, False, [ahi[d_][None], tri_bf[d_][None]], [pc[None]], inc=False)
                        self.mm(pc.t[:, hh * 128:(hh + 1) * 128], alo[d_].t[:, h0 + hh:h0 + hh + 1].to_broadcast([128, 128]), tri_bf[d_].t[:],
                                False, hh == 3, [alo[d_][None], tri_bf[d_][None]], [pc[None]], inc=(hh == 3))
                    self.act(E_.t[:], pc.t[:].rearrange("p (a b) -> p a b", a=4), AF.Exp, [pc[None]], [E_[None]])
                    self.tt("vector" if n % 2 == 0 else "gpsimd", M_.t[:], E_.t[:],
                            GT.t[:].rearrange("p (o l) -> p o l", o=1).to_broadcast([128, 4, 128]), ALU.mult,
                            [E_[None], GT[None]], [M_[None]])
                    Mg.append(M_)
                if pending is not None:
                    emit_y(*pending)
                pending = (g, Mg)
            emit_y(*pending)
            P.op("vector", lambda e: e.memset(ss.t[:, 0:8], 0.0), writes=[ss[None]])
            for g in range(8):
                P.op("scalar", lambda e, g=g: e.activation(out=junk.t[:], in_=yacc.t[:, g * 256:(g + 1) * 256], func=AF.Square,
                                                           accum_out=ss.t[:, g:g + 1]),
                     reads=[yacc[g // 2], ss[None]], writes=[junk[None], ss[g]])
            self.act(ss.t[:, 8:16], ss.t[:, 0:8], AF.Sqrt, [ss[None]], [ss[None]], bias=EPS, scale=1.0 / 256.0)
            P.op("vector", lambda e: e.reciprocal(out=ss.t[:, 8:16], in_=ss.t[:, 8:16]), reads=[ss[None]], writes=[ss[None]])
            self.tt("vector", yacc.t[:].rearrange("p (g q) -> p g q", g=8), yacc.t[:].rearrange("p (g q) -> p g q", g=8),
                    ss.t[:, 8:16].rearrange("p (g o) -> p g o", o=1).to_broadcast([128, 8, 256]), ALU.mult, [yacc[None], ss[None]], [yacc[None]])
            self.tt("gpsimd", ynb.t[:], yacc.t[:], nw_bc.t[:], ALU.mult, [yacc[None], nw_bc[None]], [ynb[None]])
            yT = ynT[0]
            for q in range(4):
                pT = self.ps[6 + q % 2]
                pTv = pT.t[:].bitcast(BF16)[:, 0:512].rearrange("p (a b) -> p a b", a=4)
                for tt_ in range(4):
                    fc = q * 4 + tt_
                    P.op("tensor", lambda e, pTv=pTv, tt_=tt_, fc=fc: e.transpose(pTv[:, tt_, :], ynb.t[:, fc * 128:(fc + 1) * 128], self.ident.t[:]),
                         reads=[ynb[None], self.ident[None]], writes=[pT[None]])
                self.cp("vector", yT.t[:, q * 4:(q + 1) * 4, :], pTv, [pT[None]], [yT[q]])
            P.dma("sync", ynTd.t.rearrange("(fc p) t -> p fc t", p=128)[:, :, r0:r0 + 128], yT.t[:], reads=[yT[None]], writes=[ynTd[c]])
            if c < 15:
                state_step(0, c, xt, bt)
        P.barrier()
        ar.off = mark0
        ar2.reset()
        wo = ar.alloc("swo", (16, D), BF16)
        yTb = [ar.alloc("syTb%d" % n, (16, 512), BF16) for n in range(2)]
        P.dma("gpsimd", wo.t[:], wod.rearrange("p (k n) -> p k n", k=16), writes=[wo[None]])
        for T in range(4):
            sl = slice(T * 512, (T + 1) * 512)
            yb = yTb[T % 2]
            P.dma("sync", yb.t[:], ynTd.t.rearrange("(fc p) t -> p fc t", p=128)[:, :, sl], reads=[ynTd[None]], writes=[yb[None]])
            for dc in range(8):
                po = self.ps[dc % 2]
                for fc in range(16):
                    self.mm(po.t[:], wo.t[:, fc, dc * 128:(dc + 1) * 128], yb.t[:, fc, :], fc == 0, fc == 15, [wo[None], yb[None]], [po[None]], inc=(fc == 15))
                self.stt("vector", self.h.t[:, dc, sl], po.t[:], self.modv(i, 2, dc), self.h.t[:, dc, sl], ALU.mult, ALU.add,
                         [po[None], self.modb[i], self.h[(dc, T)]], [self.h[(dc, T)]])


def _c(a):
    return np.ascontiguousarray(a, dtype=np.float32)


def _grid_T():
    t = np.arange(L)
    row, col = t // 64, t % 64
    q = D // 4
    omega = (10000.0 ** (-np.arange(q, dtype=np.float32) / q)).astype(np.float32)

    def enc(p):
        ang = p.astype(np.float32)[:, None] * omega
        return np.concatenate([np.sin(ang), np.cos(ang)], axis=-1)
    g = np.concatenate([enc(row), enc(col)], axis=-1).astype(np.float32)
    return _c(g.T)


def _fnet_tables():
    k = np.arange(256)
    ang = 2.0 * np.pi * ((k[:, None] * k[None, :]) % 256) / 256.0
    cw = np.stack([np.cos(ang), np.sin(ang)], axis=1)
    cw = cw.reshape(2, 128, 2, 256).transpose(1, 0, 2, 3)
    t = np.arange(L)
    angl = 2.0 * np.pi * ((t[:, None] * t[None, :]) % L) / float(L)
    sc = 1.0 / math.sqrt(L * 256.0)
    cl = np.stack([np.cos(angl) * sc, -np.sin(angl) * sc], axis=1)
    cl = cl.reshape(16, 128, 2, 4, 512).transpose(3, 1, 0, 2, 4)
    return _c(cw.reshape(128, -1)), _c(cl.reshape(4, 128, -1))


_HY_CACHE = {}


def _hyena_tables():
    if _HY_CACHE:
        return _HY_CACHE
    f32 = np.float32
    pos = np.arange(L, dtype=f32)[:, None]
    t = pos / f32(L - 1)
    freqs = np.linspace(1e-4, 15, 16, dtype=f32)[None, :]
    ang = freqs * (f32(2.0 * math.pi) * pos / f32(L))
    feats = np.concatenate([t, np.cos(ang), -np.sin(ang)], axis=-1).astype(f32)
    max_decay = math.log(1e-2) / 0.3
    min_decay = math.log(1e-2) / 1.5
    deltas = np.linspace(min_decay, max_decay, D, dtype=f32)
    win = np.exp(-t * np.abs(deltas)[None, :]).astype(f32)
    N = 2 * L
    tt_ = np.arange(L, dtype=np.int64)
    ff = np.arange(L, dtype=np.int64)
    th = 2.0 * np.pi * ((tt_[:, None] * ff[None, :]) % N) / float(N)
    cosm = np.cos(th)
    sinm = np.sin(th)
    nyq = np.where(tt_ % 2 == 0, 1.0, -1.0)
    sinm_f = sinm.copy()
    sinm_f[:, 0] = nyq
    fwd = np.stack([cosm, sinm_f], axis=1)
    fwd = fwd.reshape(16, 128, 2, 16, 128).transpose(3, 1, 0, 2, 4)
    wf = np.full(L, 2.0)
    wf[0] = 1.0
    icos = (cosm * wf[None, :] / N).T
    isin = (2.0 * sinm / N).T
    isin[0, :] = nyq / N
    inv = np.concatenate([icos.reshape(16, 128, L), isin.reshape(16, 128, L)], 0)
    inv = inv.reshape(32, 128, 16, 128).transpose(2, 1, 0, 3)
    _HY_CACHE.update({"hy_featsT": _c(feats.T), "hy_win": _c(win), "hy_fwd": _c(fwd.reshape(16, 128, -1)),
                      "hy_inv": _c(inv.reshape(16, 128, -1))})
    return _HY_CACHE


def prep_shared(inp, layers, phases):
    sh = {}
    sh["ada_bT"] = _c(inp["ada_b"].reshape(4, 48, 128).transpose(2, 0, 1))
    sh["nw"] = _c(np.stack([inp["norm_mix_w"], inp["norm_ffn_w"]], 0).reshape(2, 4, 8, 128).transpose(3, 1, 0, 2))
    sh["fnw"] = _c(inp["final_norm_w"].reshape(8, 128).T)
    sh["gridT"] = _grid_T()
    for i in layers:
        sh["ada_w%d" % i] = _c(inp["ada_w"][i])
    for (kind, i) in phases:
        if kind == "ffn":
            w1 = inp["ffn_w1"][i].reshape(8, 128, NF, 128).transpose(2, 1, 0, 3)
            w3 = inp["ffn_w3"][i].reshape(8, 128, NF, 128).transpose(2, 1, 0, 3)
            sh["w13_%d" % i] = _c(np.stack([w1, w3], axis=2).reshape(NF, 128, -1))
            sh["w2_%d" % i] = _c(inp["ffn_w2"][i].reshape(NF, 128, 8, 128).transpose(2, 1, 0, 3).reshape(8, 128, -1))
        elif i % 4 == 0:
            j = i // 4
            w = inp["ssd_w_in"][j]
            sh["ss_wi"] = _c(w[:, 2048:6144].reshape(8, 128, 32, 128).transpose(2, 1, 0, 3).reshape(32, 128, -1))
            sh["ss_wz"] = _c(w[:, :2048].reshape(8, 128, 2048).transpose(1, 0, 2).reshape(128, -1))
            sh["ss_wdt"] = _c(w[:, 6144:].reshape(8, 128, 64).transpose(1, 0, 2).reshape(128, -1))
            cvw = inp["ssd_conv_w"][j].reshape(3, 32, 128).transpose(2, 1, 0)
            cvb = inp["ssd_conv_b"][j].reshape(32, 128).T[:, :, None]
            sh["ss_cv"] = _c(np.concatenate([cvw, cvb], 2).reshape(128, -1))
            sh["ss_small"] = _c(np.concatenate([inp["ssd_a_log"][j].reshape(-1), inp["ssd_dt_bias"][j].reshape(-1), inp["ssd_d"][j].reshape(-1)])[None, :])
            sh["ss_nw"] = _c(inp["ssd_norm_w"][j][None, :])
            sh["ss_wo"] = _c(inp["ssd_w_out"][j].reshape(16, 128, D).transpose(1, 0, 2).reshape(128, -1))
        elif i % 4 == 1:
            j = i // 4
            sh["gm_wu"] = _c(inp["gm_w_in"][j][:, :2048].reshape(8, 128, 2048).transpose(1, 0, 2).reshape(128, -1))
            sh["gm_wv"] = _c(inp["gm_w_in"][j][:, 2048:].reshape(8, 128, 2048).transpose(1, 0, 2).reshape(128, -1))
            sh["gm_wo"] = _c(inp["gm_w_out"][j].reshape(16, 128, D).transpose(1, 0, 2).reshape(128, -1))
            sh["gm_wsT"] = _c(inp["gm_w_s"][j].transpose(2, 0, 1).reshape(128, -1))
            sh["gm_bs"] = _c(inp["gm_b_s"][j].reshape(1, -1))
            sh["gm_ln"] = _c(np.stack([inp["gm_ln_w"][j].reshape(16, 128).T, inp["gm_ln_b"][j].reshape(16, 128).T], 1).reshape(128, -1))
        elif i % 4 == 2:
            j = i // 4
            sh.update(_hyena_tables())
            sh["hy_fw0"] = _c(inp["hy_f_w0"][j])
            sh["hy_fw1"] = _c(inp["hy_f_w1"][j])
            sh["hy_fw2"] = _c(inp["hy_f_w2"][j])
            sh["hy_fb"] = _c(np.stack([inp["hy_f_b0"][j], inp["hy_f_b1"][j], inp["hy_sin_freq"][j][0], inp["hy_sin_freq"][j][1]], 1))
            sh["hy_bias"] = _c(inp["hy_bias"][j])
            sh["hy_wi"] = _c(inp["hy_w_in"][j].reshape(8, 128, 24, 128).transpose(2, 1, 0, 3).reshape(24, 128, -1))
            cvw = inp["hy_conv_w"][j].reshape(3, 24, 128).transpose(2, 1, 0)
            cvb = inp["hy_conv_b"][j].reshape(24, 128).T[:, :, None]
            sh["hy_cv"] = _c(np.concatenate([cvw, cvb], 2).reshape(128, -1))
            sh["hy_wo"] = _c(inp["hy_w_out"][j].reshape(8, 128, D).transpose(1, 0, 2).reshape(128, -1))
        elif i % 4 == 3:
            sh["fn_cw"], sh["fn_cl"] = _fnet_tables()
            sh["fn_wo"] = _c(inp["fn_w_out"][i // 4].reshape(8, 128, D).transpose(1, 0, 2).reshape(128, -1))
            sh["fn_bo"] = _c(inp["fn_b_out"][i // 4].reshape(8, 128).T)
    return sh


def prep_core(inp, b, xb=None):
    x = inp["x"][b] if xb is None else xb
    d = {"xT": _c(x.T)}
    d["cc"] = _c(np.stack([inp["c"][b], inp["c_ctx"]], 0).reshape(2, 8, 128).transpose(2, 1, 0))
    d["ctxT"] = _c(inp["ctx"][b].T)
    return d


ALL_PHASES = [(k, i) for i in range(4) for k in ("mix", "ffn")]


def run(inp, phases, cores, add_grid=True, final_norm=True, xs=None, trace=False):
    kb = KB(phases, add_grid, final_norm)
    nc = kb.build()
    sh = prep_shared(inp, kb.layers, phases)
    in_maps = []
    for n, b in enumerate(cores):
        m = dict(sh)
        m.update(prep_core(inp, b, None if xs is None else xs[n]))
        in_maps.append({k: v for k, v in m.items() if k in kb.din})
    res = run_bass_kernel_spmd(nc, in_maps, core_ids=list(range(len(cores))), trace=trace)
    outs = [np.ascontiguousarray(r["yT"].T) for r in res.results]
    return outs, res


def kernel(**inputs):
    inp = {k: np.asarray(v) for k, v in inputs.items()}
    outs, _ = run(inp, ALL_PHASES, list(range(NCORES)))
    return np.stack(outs, 0).astype(np.float32)
```

```python
import contextlib
import math
import os
import numpy as np
import concourse.bass as bass
import concourse.mybir as mybir
from concourse.bass_utils import run_bass_kernel_spmd

F32 = mybir.dt.float32
BF16 = mybir.dt.bfloat16
AF = mybir.ActivationFunctionType
ALU = mybir.AluOpType
AX = mybir.AxisListType

D = 1024
L = 2048
LC = 256
FF = 2816
NF = 22
EPS = 1e-6
NCORES = 8

ENGS = ["sync", "scalar", "tensor", "vector", "gpsimd"]
DMA_K = 8
SAME_ENGINE_SYNC = True


class Res:
    __slots__ = ("last_w", "readers", "dma_readers")

    def __init__(self):
        self.last_w = None
        self.readers = {}
        self.dma_readers = []


class Buf:
    def __init__(self, name, t):
        self.name = name
        self.t = t
        self.res = {}

    def __getitem__(self, key):
        return (self, key)


class PsBuf(Buf):
    def __getitem__(self, key):
        return (self, None)


def RK(buf, keys):
    return [(buf, k) for k in keys]


class Prog:
    def __init__(self, nc):
        self.nc = nc
        self.stream = {e: [] for e in ENGS}
        self.ops = []
        self.ccount = {e: 0 for e in ENGS}
        self.dcount = {e: 0 for e in ENGS}
        self.seen = {e: {} for e in ENGS}
        self.pending_noinc = {e: [] for e in ENGS}
        self.semkeys = set()

    def _collect(self, reads, writes):
        deps = set()
        self._war = set()
        for (buf, key) in reads:
            keys = list(buf.res.keys()) if key is None else [key, None]
            for k in keys:
                r = buf.res.get(k)
                if r is not None and r.last_w is not None:
                    deps.add(r.last_w)
        for (buf, key) in writes:
            keys = list(buf.res.keys()) if key is None else [key, None]
            for k in keys:
                r = buf.res.get(k)
                if r is None:
                    continue
                if r.last_w is not None:
                    deps.add(r.last_w)
                self._war.update(r.readers.values())
                self._war.update(r.dma_readers)
        self._war -= deps
        return deps | self._war

    def _mark(self, opid, eng, is_dma, reads, writes):
        for (buf, key) in reads:
            r = buf.res.get(key)
            if r is None:
                r = buf.res[key] = Res()
            if is_dma:
                r.dma_readers.append(opid)
            else:
                r.readers[eng] = opid
        for (buf, key) in writes:
            if key is None:
                buf.res = {}
            r = buf.res.get(key)
            if r is None:
                r = buf.res[key] = Res()
            r.last_w = opid
            r.readers = {}
            r.dma_readers = []

    def _emit_waits(self, eng, deps):
        for d in sorted(deps):
            deng, kind, semkey, val = self.ops[d]
            if kind == "c" and deng == eng:
                if eng == "tensor" or not SAME_ENGINE_SYNC or d in self._war:
                    continue
            if val is None:
                raise RuntimeError("dependency on op without completion signal")
            if self.seen[eng].get(semkey, 0) >= val:
                continue
            self.seen[eng][semkey] = val
            self.stream[eng].append(("wait", semkey, val))

    def op(self, eng, fn, reads=(), writes=(), inc=True):
        deps = self._collect(reads, writes)
        self._emit_waits(eng, deps)
        opid = len(self.ops)
        if inc:
            self.ccount[eng] += 1
            semkey = "c_" + eng
            self.semkeys.add(semkey)
            val = self.ccount[eng]
            self.ops.append((eng, "c", semkey, val))
            for pid in self.pending_noinc[eng]:
                self.ops[pid] = (eng, "c", semkey, val)
            self.pending_noinc[eng] = []
            self.stream[eng].append(("op", fn, semkey, 1))
        else:
            self.ops.append((eng, "c", "c_" + eng, None))
            self.pending_noinc[eng].append(opid)
            self.stream[eng].append(("op", fn, None, 0))
        self._mark(opid, eng, False, reads, writes)
        return opid

    def dma(self, eng, out, in_, reads=(), writes=(), **kw):
        deps = self._collect(reads, writes)
        i = self.dcount[eng]
        self.dcount[eng] += 1
        slot = i % DMA_K
        semkey = "d_%s_%d" % (eng, slot)
        self.semkeys.add(semkey)
        prev_val = 16 * (i // DMA_K)
        val = prev_val + 16
        self._emit_waits(eng, deps)
        if prev_val > 0 and self.seen[eng].get(semkey, 0) < prev_val:
            self.seen[eng][semkey] = prev_val
            self.stream[eng].append(("wait", semkey, prev_val))
        opid = len(self.ops)
        self.ops.append((eng, "d", semkey, val))
        fn = lambda e, out=out, in_=in_, kw=kw: e.dma_start(out=out, in_=in_, **kw)
        self.stream[eng].append(("op", fn, semkey, 16))
        self._mark(opid, eng, True, reads, writes)
        return opid

    def barrier(self, engines=None):
        for e in ENGS:
            if self.pending_noinc[e]:
                raise RuntimeError("barrier with trailing no-inc ops on " + e)
        for e in (engines or ENGS):
            for e2 in ENGS:
                n = self.dcount[e2]
                for slot in range(min(n, DMA_K)):
                    cnt = (n - 1 - slot) // DMA_K + 1
                    semkey = "d_%s_%d" % (e2, slot)
                    if self.seen[e].get(semkey, 0) < 16 * cnt:
                        self.seen[e][semkey] = 16 * cnt
                        self.stream[e].append(("wait", semkey, 16 * cnt))
                if self.ccount[e2]:
                    semkey = "c_" + e2
                    if self.seen[e].get(semkey, 0) < self.ccount[e2]:
                        self.seen[e][semkey] = self.ccount[e2]
                        self.stream[e].append(("wait", semkey, self.ccount[e2]))

    def finish(self, final_eng="sync"):
        self.barrier([final_eng])

    def run_block(self):
        nc = self.nc
        with contextlib.ExitStack() as st:
            sems = {}
            for k in sorted(self.semkeys):
                sems[k] = st.enter_context(nc.semaphore(k))
            block = st.enter_context(nc.Block())

            def mk(engname):
                def body(e):
                    for act in self.stream[engname]:
                        if act[0] == "wait":
                            e.wait_ge(sems[act[1]], act[2])
                        else:
                            ins = act[1](e)
                            if act[2] is not None:
                                ins.then_inc(sems[act[2]], act[3])
                return body

            for engname in ENGS:
                if self.stream[engname]:
                    getattr(block, engname)(mk(engname))


class Arena:
    def __init__(self, name, ap, nwords):
        self.name = name
        self.ap = ap
        self.nwords = nwords
        self.off = 0
        self.n = 0

    def reset(self):
        self.off = 0

    def alloc(self, name, free_shape, dt):
        free_shape = tuple(free_shape)
        nel = int(np.prod(free_shape))
        words = nel if dt == F32 else (nel + 1) // 2
        words = (words + 7) // 8 * 8
        if self.off + words > self.nwords:
            raise RuntimeError("arena %s overflow: %s needs %d words at %d / %d" % (self.name, name, words, self.off, self.nwords))
        v = self.ap[:, self.off:self.off + words]
        if dt != F32:
            v = v.bitcast(dt)
        v = v[:, 0:nel]
        if len(free_shape) == 2:
            v = v.rearrange("p (a b) -> p a b", a=free_shape[0])
        elif len(free_shape) == 3:
            v = v.rearrange("p (a b c) -> p a b c", a=free_shape[0], b=free_shape[1])
        elif len(free_shape) == 4:
            v = v.rearrange("p (a b c d) -> p a b c d", a=free_shape[0], b=free_shape[1], c=free_shape[2])
        self.off += words
        self.n += 1
        return Buf("%s.%s.%d" % (self.name, name, self.n), v)


ARENA_WORDS = 26432


class KB:
    def __init__(self, phases, add_grid=True, final_norm=True):
        self.phases = phases
        self.add_grid = add_grid
        self.final_norm = final_norm
        self.nc = bass.Bass("TRN2", target_bir_lowering=False)
        self.P = Prog(self.nc)
        self.din = {}
        self.st = contextlib.ExitStack()
        self.layers = sorted(set(i for (_, i) in phases))

    def inp(self, name, shape):
        if name not in self.din:
            self.din[name] = self.nc.dram_tensor(name, list(shape), F32, kind="ExternalInput").ap()
        return self.din[name]

    def scratch(self, name, shape, dt):
        return Buf(name, self.nc.dram_tensor(name, list(shape), dt, kind="Internal").ap())

    def sb(self, name, shape, dt):
        return Buf(name, self.st.enter_context(self.nc.sbuf_tensor("s_" + name, list(shape), dt)))

    def mm(self, out, lhsT, rhs, start, stop, reads, writes, inc=True):
        self.P.op("tensor", lambda e: e.matmul(out, lhsT=lhsT, rhs=rhs, start=start, stop=stop),
                  reads=reads, writes=writes, inc=inc)

    def act(self, out, in_, func, reads, writes, bias=None, scale=None):
        kw = {}
        if bias is not None:
            kw["bias"] = bias
        if scale is not None:
            kw["scale"] = scale
        self.P.op("scalar", lambda e: e.activation(out=out, in_=in_, func=func, **kw), reads=reads, writes=writes)

    def tt(self, eng, out, in0, in1, op, reads, writes):
        self.P.op(eng, lambda e: e.tensor_tensor(out=out, in0=in0, in1=in1, op=op), reads=reads, writes=writes)

    def ts(self, eng, out, in0, s1, s2, op0, op1, reads, writes):
        self.P.op(eng, lambda e: e.tensor_scalar(out=out, in0=in0, scalar1=s1, scalar2=s2, op0=op0, op1=op1),
                  reads=reads, writes=writes)

    def stt(self, eng, out, in0, scalar, in1, op0, op1, reads, writes):
        self.P.op(eng, lambda e: e.scalar_tensor_tensor(out=out, in0=in0, scalar=scalar, in1=in1, op0=op0, op1=op1),
                  reads=reads, writes=writes)

    def cp(self, eng, out, in_, reads, writes):
        if eng == "scalar":
            self.P.op(eng, lambda e: e.copy(out=out, in_=in_), reads=reads, writes=writes)
        else:
            self.P.op(eng, lambda e: e.tensor_copy(out=out, in_=in_), reads=reads, writes=writes)

    def new_epoch(self, use_hn=False):
        self.P.barrier()
        self.ar.reset()
        self.ar2.reset()

    def build(self):
        nc, P = self.nc, self.P
        with self.st:
            self.h = self.sb("h", [128, 8, L], F32)
            self.hn = self.sb("hn", [128, 8, L], BF16)
            arena_t = self.st.enter_context(nc.sbuf_tensor("arena", [128, ARENA_WORDS], F32))
            self.ar = Arena("ar", arena_t[:], ARENA_WORDS)
            self.ar2 = Arena("ar2", self.hn.t[:].rearrange("p a b -> p (a b)").bitcast(F32), 8 * L // 2)
            self.ident = self.sb("ident", [128, 128], BF16)
            self.ones_bf = self.sb("ones_bf", [128, 128], BF16)
            self.ones_f = self.sb("ones_f", [128, 128], F32)
            self.modb = self.sb("modb", [128, 4, 48, 2], F32)
            self.adab = self.sb("adab", [128, 4, 48], F32)
            self.nw = self.sb("nw", [128, 4, 2, 8], F32)
            self.fnw = self.sb("fnw", [128, 8], F32)
            self.lv = self.sb("lv", [128, 4, 4, 8], F32)
            self.ccs = self.sb("ccs", [128, 8, 2], F32)
            self.cs = self.sb("cs", [128, 8, 2], BF16)
            self.adw = [self.sb("adw%d" % n, [128, 8, 128], BF16) for n in range(2)]
            self.ps = [PsBuf("ps%d" % i, self.st.enter_context(nc.psum_tensor("ps%d" % i, [128, 512], F32))) for i in range(8)]

            P.op("gpsimd", lambda e: e.memset(self.ident.t[:], 1.0), writes=[self.ident[None]])
            P.op("gpsimd", lambda e: e.affine_select(out=self.ident.t[:], in_=self.ident.t[:], pattern=[[-1, 128]],
                                                     compare_op=ALU.is_equal, fill=0.0, base=0, channel_multiplier=1),
                 reads=[self.ident[None]], writes=[self.ident[None]])
            P.op("vector", lambda e: e.memset(self.ones_bf.t[:], 1.0), writes=[self.ones_bf[None]])
            P.op("vector", lambda e: e.memset(self.ones_f.t[:], 1.0), writes=[self.ones_f[None]])
            P.dma("sync", self.adab.t[:], self.inp("ada_bT", [128, 4, 48]), writes=[self.adab[None]])
            P.dma("sync", self.nw.t[:], self.inp("nw", [128, 4, 2, 8]), writes=[self.nw[None]])
            P.dma("sync", self.fnw.t[:], self.inp("fnw", [128, 8]), writes=[self.fnw[None]])
            P.dma("sync", self.ccs.t[:], self.inp("cc", [128, 8, 2]), writes=[self.ccs[None]])
            self.act(self.cs.t[:], self.ccs.t[:], AF.Silu, [self.ccs[None]], [self.cs[None]])

            xT = self.inp("xT", [D, L])
            for c in range(8):
                P.dma("sync", self.h.t[:, c, :], xT[c * 128:(c + 1) * 128, :], writes=RK(self.h, [(c, t) for t in range(4)]))
            if self.add_grid:
                gT = self.inp("gridT", [D, L])
                gb = [self.ar.alloc("grid%d" % n, (L,), F32) for n in range(2)]
                for c in range(8):
                    g = gb[c % 2]
                    hk = RK(self.h, [(c, t) for t in range(4)])
                    P.dma("sync", g.t[:], gT[c * 128:(c + 1) * 128, :], writes=[g[None]])
                    self.tt("vector", self.h.t[:, c, :], self.h.t[:, c, :], g.t[:], ALU.add, hk + [g[None]], hk)

            inter = set()
            for n, (kind, i) in enumerate(self.phases):
                if kind == "ffn" and any(i2 == i + 1 for (_, i2) in self.phases[n + 1:]):
                    inter.add(i + 1)
            for i in self.layers:
                if i not in inter:
                    self.new_epoch()
                    wide = [self.ar.alloc("adawide%d" % n, (8, 768), BF16) for n in range(2)]
                    for _ in self.adaln_gen(i, self.ps[0], wide, 6):
                        pass
            for (kind, i) in self.phases:
                self.new_epoch()
                if kind == "ffn":
                    self.ffn(i, self.adaln_gen(i + 1, self.ps[7]) if (i + 1) in inter else None)
                else:
                    getattr(self, ["mix_ssd", "mix_gmlp", "mix_hyena", "mix_fnet"][i % 4])(i)
            self.new_epoch()
            self.final()
            P.finish("sync")
            P.run_block()
        return self.nc

    def adaln_gen(self, i, psA, wbs=None, nper=1):
        P = self.P
        aw = self.inp("ada_w%d" % i, [D, 6 * D])
        wbs = wbs or self.adw
        for piece in range(48 // nper):
            wb = wbs[piece % 2]
            wcol = nper * 128
            P.dma("gpsimd", wb.t[:], aw[:, piece * wcol:(piece + 1) * wcol].rearrange("(k p) n -> p k n", p=128), writes=[wb[None]])
            for jn in range(nper):
                j = piece * nper + jn
                for k in range(8):
                    self.mm(psA.t[:, 2 * j:2 * j + 2], wb.t[:, k, jn * 128:(jn + 1) * 128], self.cs.t[:, k, :], k == 0, k == 7,
                            [wb[None], self.cs[None]], [psA[None]], inc=(k == 7))
                yield
        self.tt("vector", self.modb.t[:, i, :, :], psA.t[:, 0:96].rearrange("p (j c) -> p j c", c=2),
                self.adab.t[:, i, :].rearrange("p (j o) -> p j o", o=1).to_broadcast([128, 48, 2]), ALU.add,
                [psA[None], self.adab[None]], [self.modb[i]])
        for (slot, which, col, nwi) in ((0, 1, 0, 0), (1, 4, 0, 1), (2, 1, 1, 0)):
            self.stt("vector", self.lv.t[:, i, slot, :], self.modb.t[:, i, which * 8:(which + 1) * 8, col], 1.0,
                     self.nw.t[:, i, nwi, :], ALU.add, ALU.mult, [self.modb[i], self.nw[None]], [self.lv[(i, slot)]])
        yield

    def modv(self, i, which, c, col=0):
        return self.modb.t[:, i, which * 8 + c, col:col + 1]

    def norm_mod(self, src, ntok, Afn, Bfn, dst, extra_reads, ar):
        P = self.P
        TW = min(512, ntok)
        sqb = [ar.alloc("nsq%d" % n, (TW,), BF16) for n in range(2)]
        rsb = [ar.alloc("nrs%d" % n, (TW,), F32) for n in range(2)]
        tmb = [ar.alloc("ntm%d" % n, (TW,), F32) for n in range(2)]
        for t in range(ntok // TW):
            sl = slice(t * TW, (t + 1) * TW)
            pss = self.ps[6 + t % 2]
            rs = rsb[t % 2]
            for c in range(8):
                sq = sqb[c % 2]
                self.tt("gpsimd", sq.t[:], src.t[:, c, sl], src.t[:, c, sl], ALU.mult, [src[(c, t)]], [sq[None]])
                self.mm(pss.t[:, :TW], self.ones_bf.t[:], sq.t[:], c == 0, c == 7, [sq[None], self.ones_bf[None]], [pss[None]])
            self.act(rs.t[:], pss.t[:, :TW], AF.Sqrt, [pss[None]], [rs[None]], bias=EPS, scale=1.0 / D)
            P.op("vector", lambda e, rs=rs: e.reciprocal(out=rs.t[:], in_=rs.t[:]), reads=[rs[None]], writes=[rs[None]])
            for c in range(8):
                tm = tmb[c % 2]
                self.tt("vector", tm.t[:], src.t[:, c, sl], rs.t[:], ALU.mult, [src[(c, t)], rs[None]], [tm[None]])
                self.act(dst.t[:, c, sl], tm.t[:], AF.Identity, [tm[None]] + extra_reads, [dst[(c, t)]],
                         bias=Bfn(c), scale=Afn(c))

    def ffn(self, i, side=None):
        P, ar = self.P, self.ar

        def step():
            if side is not None:
                next(side, None)

        w13d = self.inp("w13_%d" % i, [NF, 128, 2 * 8 * 128])
        w2d = self.inp("w2_%d" % i, [8, 128, NF * 128])
        w13 = [ar.alloc("w13_%d" % n, (2, 8, 128), BF16) for n in range(3)]
        for j in range(3):
            P.dma("gpsimd", w13[j].t[:], w13d[j].rearrange("p (a k f) -> p a k f", a=2, k=8), writes=[w13[j][None]])
        self.norm_mod(self.h, L, lambda c: self.lv.t[:, i, 1, c:c + 1], lambda c: self.modv(i, 3, c), self.hn,
                      [self.lv[(i, 1)], self.modb[i]], ar)
        a = ar.alloc("a", (NF, 1024), BF16)
        w2b = [ar.alloc("w2_%d" % n, (NF, 128), BF16) for n in range(2)]
        stb = [ar.alloc("st%d" % n, (512,), BF16) for n in range(2)]
        n13 = 0
        for half in range(2):
            for j in range(NF):
                wb = w13[j % 3]
                if not (half == 0 and j < 3):
                    P.dma("gpsimd", wb.t[:], w13d[j].rearrange("p (a k f) -> p a k f", a=2, k=8), writes=[wb[None]])
                for tt_ in range(2):
                    T = half * 2 + tt_
                    sl = slice(T * 512, (T + 1) * 512)
                    p1 = self.ps[(n13 % 2) * 2]
                    p3 = self.ps[(n13 % 2) * 2 + 1]
                    stt_ = stb[n13 % 2]
                    n13 += 1
                    hr = RK(self.hn, [(k, T) for k in range(8)])
                    for k in range(8):
                        self.mm(p1.t[:], wb.t[:, 0, k, :], self.hn.t[:, k, sl], k == 0, k == 7, [wb[None], self.hn[(k, T)]], [p1[None]], inc=(k == 7))
                    for k in range(8):
                        self.mm(p3.t[:], wb.t[:, 1, k, :], self.hn.t[:, k, sl], k == 0, k == 7, [wb[None], self.hn[(k, T)]], [p3[None]], inc=(k == 7))
                    self.act(stt_.t[:], p1.t[:], AF.Silu, [p1[None]], [stt_[None]])
                    self.tt("vector", a.t[:, j, tt_ * 512:(tt_ + 1) * 512], stt_.t[:], p3.t[:], ALU.mult,
                            [stt_[None], p3[None]], [a[(j, tt_)]])
                step()
            for dc in range(8):
                w2 = w2b[dc % 2]
                P.dma("gpsimd", w2.t[:], w2d[dc].rearrange("p (j d) -> p j d", j=NF), writes=[w2[None]])
                for tt_ in range(2):
                    T = half * 2 + tt_
                    sl = slice(T * 512, (T + 1) * 512)
                    po = self.ps[4 + (dc * 2 + tt_) % 2]
                    for j in range(NF):
                        self.mm(po.t[:], w2.t[:, j, :], a.t[:, j, tt_ * 512:(tt_ + 1) * 512], j == 0, j == NF - 1,
                                [w2[None], a[(j, tt_)]], [po[None]], inc=(j == NF - 1))
                    self.stt("vector", self.h.t[:, dc, sl], po.t[:], self.modv(i, 5, dc), self.h.t[:, dc, sl], ALU.mult, ALU.add,
                             [po[None], self.modb[i], self.h[(dc, T)]], [self.h[(dc, T)]])
                step()
        if side is not None:
            for _ in side:
                pass

    def final(self):
        P, ar = self.P, self.ar
        yT = self.nc.dram_tensor("yT", [D, L], F32, kind="ExternalOutput").ap()
        ob = [ar.alloc("ob%d" % n, (512,), F32) for n in range(3)]
        if not self.final_norm:
            for c in range(8):
                P.dma("sync", yT[c * 128:(c + 1) * 128, :], self.h.t[:, c, :], reads=RK(self.h, [(c, t) for t in range(4)]))
            return
        sqb = [ar.alloc("fsq%d" % n, (512,), BF16) for n in range(2)]
        rsb = [ar.alloc("frs%d" % n, (512,), F32) for n in range(2)]
        n = 0
        for t in range(4):
            sl = slice(t * 512, (t + 1) * 512)
            pss = self.ps[6 + t % 2]
            rs = rsb[t % 2]
            for c in range(8):
                sq = sqb[c % 2]
                self.tt("gpsimd", sq.t[:], self.h.t[:, c, sl], self.h.t[:, c, sl], ALU.mult, [self.h[(c, t)]], [sq[None]])
                self.mm(pss.t[:], self.ones_bf.t[:], sq.t[:], c == 0, c == 7, [sq[None], self.ones_bf[None]], [pss[None]])
            self.act(rs.t[:], pss.t[:], AF.Sqrt, [pss[None]], [rs[None]], bias=EPS, scale=1.0 / D)
            P.op("vector", lambda e, rs=rs: e.reciprocal(out=rs.t[:], in_=rs.t[:]), reads=[rs[None]], writes=[rs[None]])
            for c in range(8):
                o = ob[n % 3]
                n += 1
                self.stt("vector", o.t[:], self.h.t[:, c, sl], self.fnw.t[:, c:c + 1], rs.t[:], ALU.mult, ALU.mult,
                         [self.h[(c, t)], self.fnw[None], rs[None]], [o[None]])
                P.dma("sync", yT[c * 128:(c + 1) * 128, sl], o.t[:], reads=[o[None]])

    def mix_fnet(self, i):
        P, ar, ar2 = self.P, self.ar, self.ar2
        j = i // 4
        self.norm_phase(i, 0, 0)
        cwd = self.inp("fn_cw", [128, 2 * 2 * 256])
        cld = self.inp("fn_cl", [4, 128, 16 * 2 * 512])
        wod = self.inp("fn_wo", [128, 8 * D])
        bod = self.inp("fn_bo", [128, 8])
        A = ar.alloc("fnA", (16, 2, D), BF16)
        cw = ar.alloc("fncw", (2, 512), BF16)
        wo = ar.alloc("fnwo", (8, D), BF16)
        bo = ar.alloc("fnbo", (8,), F32)
        bg = ar.alloc("fnbg", (8,), F32)
        fT = ar.alloc("fnfT", (8, 512), BF16)
        tmb = [ar.alloc("fntm%d" % n, (512,), F32) for n in range(2)]
        P.dma("gpsimd", cw.t[:], cwd.rearrange("p (k n) -> p k n", k=2), writes=[cw[None]])
        P.dma("gpsimd", wo.t[:], wod.rearrange("p (k n) -> p k n", k=8), writes=[wo[None]])
        P.dma("sync", bo.t[:], bod, writes=[bo[None]])
        self.tt("vector", bg.t[:], bo.t[:], self.modb.t[:, i, 16:24, 0], ALU.mult, [bo[None], self.modb[i]], [bg[None]])
        n = 0
        for tc in range(16):
            T = tc // 4
            for g in range(4):
                pa = self.ps[n % 2]
                n += 1
                for kk in range(2):
                    k = 2 * g + kk
                    self.mm(pa.t[:], self.hn.t[:, k, tc * 128:(tc + 1) * 128], cw.t[:, kk, :], kk == 0, kk == 1,
                            [self.hn[(k, T)], cw[None]], [pa[None]], inc=(kk == 1))
                self.cp("scalar", A.t[:, tc, :, g * 256:(g + 1) * 256], pa.t[:].rearrange("p (a b) -> p a b", a=2),
                        [pa[None]], [A[(tc, g)]])
        P.barrier()
        clb = ar2.alloc("fncl", (16, 2, 512), BF16)
        n = 0
        for T in range(4):
            sl = slice(T * 512, (T + 1) * 512)
            P.dma("gpsimd", clb.t[:], cld[T].rearrange("p (a b c) -> p a b c", a=16, b=2), writes=[clb[None]])
            for dc in range(8):
                pf = self.ps[2 + dc % 2]
                g = dc // 2
                for tc in range(16):
                    for cs_ in range(2):
                        self.mm(pf.t[:], A.t[:, tc, cs_, dc * 128:(dc + 1) * 128], clb.t[:, tc, cs_, :],
                                tc == 0 and cs_ == 0, tc == 15 and cs_ == 1, [A[(tc, g)], clb[None]], [pf[None]],
                                inc=(tc == 15 and cs_ == 1))
                self.cp("scalar", fT.t[:, dc, :], pf.t[:], [pf[None]], [fT[dc]])
            for dc in range(8):
                po = self.ps[4 + dc % 2]
                tm = tmb[dc % 2]
                for k in range(8):
                    self.mm(po.t[:], wo.t[:, k, dc * 128:(dc + 1) * 128], fT.t[:, k, :], k == 0, k == 7,
                            [wo[None], fT[k]], [po[None]], inc=(k == 7))
                self.act(tm.t[:], po.t[:], AF.Identity, [po[None], bg[None], self.modb[i]], [tm[None]],
                         bias=bg.t[:, dc:dc + 1], scale=self.modv(i, 2, dc))
                self.tt("vector", self.h.t[:, dc, sl], self.h.t[:, dc, sl], tm.t[:], ALU.add,
                        [self.h[(dc, T)], tm[None]], [self.h[(dc, T)]])

    def norm_phase(self, i, slot_A, which_B):
        self.norm_mod(self.h, L, lambda c: self.lv.t[:, i, slot_A, c:c + 1], lambda c: self.modv(i, which_B, c), self.hn,
                      [self.lv[(i, slot_A)], self.modb[i]], self.ar)
        self.P.barrier()
        self.ar.reset()

    def mix_gmlp(self, i):
        P, ar = self.P, self.ar
        self.norm_phase(i, 0, 0)
        wud = self.inp("gm_wu", [128, 8 * 2048]).rearrange("p (k n) -> p k n", k=8)
        wvd = self.inp("gm_wv", [128, 8 * 2048]).rearrange("p (k n) -> p k n", k=8)
        wod = self.inp("gm_wo", [128, 16 * D]).rearrange("p (k n) -> p k n", k=16)
        wsd = self.inp("gm_wsT", [128, 8 * 128])
        bsd = self.inp("gm_bs", [1, 8 * 128])
        lnd = self.inp("gm_ln", [128, 2 * 16])
        wv = ar.alloc("gwv", (8, 2048), BF16)
        v32 = ar.alloc("gv32", (2048,), F32)
        vln = ar.alloc("gvln", (2048,), BF16)
        wsT = ar.alloc("gwsT", (8, 128), BF16)
        extra = ar.alloc("gextra", (16, 128), F32)
        sT = ar.alloc("gsT", (16, 512), BF16)
        wub = [ar.alloc("gwu%d" % n, (8, 128), BF16) for n in range(3)]
        ugb = [ar.alloc("gug%d" % n, (512,), BF16) for n in range(2)]
        prod = ar.alloc("gprod", (16, 512), BF16)
        wob = [ar.alloc("gwo%d" % n, (16, 128), BF16) for n in range(2)]
        ln = ar.alloc("gln", (2, 16), F32)
        stats = ar.alloc("gstats", (4, 6), F32)
        mv = ar.alloc("gmv", (4,), F32)
        for q in range(4):
            P.dma("gpsimd", wv.t[:, :, q * 512:(q + 1) * 512], wvd[:, :, q * 512:(q + 1) * 512], writes=[wv[q]])
        P.dma("gpsimd", wsT.t[:], wsd.rearrange("p (g q) -> p g q", g=8), writes=[wsT[None]])
        P.dma("sync", ln.t[:], lnd.rearrange("p (a b) -> p a b", a=2), writes=[ln[None]])
        rs_bc = v32.t[:, 0:1024]
        bs_bc = v32.t[:, 1024:2048]
        P.dma("sync", bs_bc, bsd[0:1, :].to_broadcast([128, 1024]), writes=[v32[None]])
        for hh in range(2):
            pr = self.ps[hh]
            self.mm(pr.t[:], self.ones_bf.t[:], wsT.t[:, hh * 4:(hh + 1) * 4, :].rearrange("p a b -> p (a b)"), True, True,
                    [self.ones_bf[None], wsT[None]], [pr[None]])
            self.cp("vector", rs_bc[:, hh * 512:(hh + 1) * 512], pr.t[:], [pr[None]], [v32[None]])
        for dcx in range(16):
            g = dcx // 2
            self.stt("vector", extra.t[:, dcx, :], rs_bc[:, g * 128:(g + 1) * 128], ln.t[:, 1, dcx:dcx + 1],
                     bs_bc[:, g * 128:(g + 1) * 128], ALU.mult, ALU.add, [v32[None], ln[None]], [extra[dcx]])
        nwu = 0
        STOP = int(os.environ.get('GM_STOP', '99'))
        if STOP <= 1:
            return
        for T in range(4):
            sl = slice(T * 512, (T + 1) * 512)
            for tcc in range(4):
                tc = T * 4 + tcc
                tsl = slice(tc * 128, (tc + 1) * 128)
                for ct in range(4):
                    pv = self.ps[ct % 2]
                    for k in range(8):
                        self.mm(pv.t[:], self.hn.t[:, k, tsl], wv.t[:, k, ct * 512:(ct + 1) * 512], k == 0, k == 7,
                                [self.hn[(k, T)], wv[ct]], [pv[None]], inc=(k == 7))
                    self.act(v32.t[:, ct * 512:(ct + 1) * 512], pv.t[:], AF.Gelu, [pv[None]], [v32[ct]])
                    P.op("vector", lambda e, ct=ct: e.bn_stats(out=stats.t[:, ct, :], in_=v32.t[:, ct * 512:(ct + 1) * 512]),
                         reads=[v32[ct]], writes=[stats[ct]])
                P.op("vector", lambda e: e.bn_aggr(out=mv.t[:, 0:2], in_=stats.t[:].rearrange("p a b -> p (a b)")),
                     reads=[stats[None]], writes=[mv[None]])
                self.act(mv.t[:, 2:3], mv.t[:, 1:2], AF.Sqrt, [mv[None]], [mv[None]], bias=EPS, scale=1.0)
                P.op("vector", lambda e: e.reciprocal(out=mv.t[:, 3:4], in_=mv.t[:, 2:3]), reads=[mv[None]], writes=[mv[None]])
                self.ts("vector", vln.t[:], v32.t[:], mv.t[:, 0:1], mv.t[:, 3:4], ALU.subtract, ALU.mult,
                        [v32[None], mv[None]], [vln[None]])
                if STOP <= 2:
                    return
                for q4 in range(4):
                    pS = self.ps[2 + q4 % 2]
                    for dd in range(4):
                        dcx = q4 * 4 + dd
                        self.mm(pS.t[:, dd * 128:(dd + 1) * 128], vln.t[:, dcx * 128:(dcx + 1) * 128], wsT.t[:, dcx // 2, :], True, True,
                                [vln[None], wsT[None]], [pS[dd]])
                    for dd in range(4):
                        dcx = q4 * 4 + dd
                        self.stt("vector", sT.t[:, dcx, tcc * 128:(tcc + 1) * 128], pS.t[:, dd * 128:(dd + 1) * 128],
                                 ln.t[:, 0, dcx:dcx + 1], extra.t[:, dcx, :], ALU.mult, ALU.add,
                                 [pS[dd], ln[None], extra[dcx]], [sT[(dcx, tcc)]])
                if STOP == 25:
                    return
            if STOP <= 3:
                return
            for fc in range(16):
                wu = wub[nwu % 3]
                ug = ugb[nwu % 2]
                nwu += 1
                P.dma("gpsimd", wu.t[:], wud[:, :, fc * 128:(fc + 1) * 128], writes=[wu[None]])
                pu = self.ps[4 + fc % 2]
                for k in range(8):
                    self.mm(pu.t[:], wu.t[:, k, :], self.hn.t[:, k, sl], k == 0, k == 7, [wu[None], self.hn[(k, T)]], [pu[None]], inc=(k == 7))
                self.act(ug.t[:], pu.t[:], AF.Gelu, [pu[None]], [ug[None]])
                self.tt("vector", prod.t[:, fc, :], ug.t[:], sT.t[:, fc, :], ALU.mult,
                        [ug[None]] + RK(sT, [(fc, q) for q in range(4)]), [prod[fc]])
            if STOP <= 4:
                return
            for dc in range(8):
                wo = wob[dc % 2]
                P.dma("gpsimd", wo.t[:], wod[:, :, dc * 128:(dc + 1) * 128], writes=[wo[None]])
                po = self.ps[6 + dc % 2]
                for fc in range(16):
                    self.mm(po.t[:], wo.t[:, fc, :], prod.t[:, fc, :], fc == 0, fc == 15, [wo[None], prod[fc]], [po[None]], inc=(fc == 15))
                self.stt("vector", self.h.t[:, dc, sl], po.t[:], self.modv(i, 2, dc), self.h.t[:, dc, sl], ALU.mult, ALU.add,
                         [po[None], self.modb[i], self.h[(dc, T)]], [self.h[(dc, T)]])

    def sin_mlp(self, out, ps_in, b_ap, f_ap, tmps, reads, writes):
        P = self.P
        arg, m, ki = tmps
        I32 = mybir.dt.int32
        kiv = ki.t[:].bitcast(I32)
        self.ts("vector", arg.t[:], ps_in, b_ap, f_ap, ALU.add, ALU.mult, reads, [arg[None]])
        self.ts("vector", m.t[:], arg.t[:], 1.0 / (2 * math.pi), 64.0, ALU.mult, ALU.add, [arg[None]], [m[None]])
        self.cp("vector", kiv, m.t[:], [m[None]], [ki[None]])
        self.cp("vector", m.t[:], kiv, [ki[None]], [m[None]])
        self.ts("vector", m.t[:], m.t[:], -64.0, -2 * math.pi, ALU.add, ALU.mult, [m[None]], [m[None]])
        self.tt("vector", arg.t[:], m.t[:], arg.t[:], ALU.add, [m[None], arg[None]], [arg[None]])
        self.ts("vector", m.t[:], arg.t[:], math.pi, -2 * math.pi, ALU.is_gt, ALU.mult, [arg[None]], [m[None]])
        self.tt("vector", arg.t[:], m.t[:], arg.t[:], ALU.add, [m[None], arg[None]], [arg[None]])
        self.act(out, arg.t[:], AF.Sin, [arg[None]], writes)

    def dft_fwd(self, src, ftab, cs_, banks, extra_reads=()):
        for dh in range(2):
            pb = banks[dh]
            for tc in range(16):
                self.mm(pb.t[:], ftab.t[:, tc, cs_, :], src.t[:, tc, dh * 512:(dh + 1) * 512], tc == 0, tc == 15,
                        [ftab[None], src[tc]] + list(extra_reads), [pb[None]], inc=(tc == 15))

    def mix_hyena(self, i):
        P, ar, ar2 = self.P, self.ar, self.ar2
        j = i // 4
        featd = self.inp("hy_featsT", [33, L])
        w0d = self.inp("hy_fw0", [33, 64])
        w1d = self.inp("hy_fw1", [64, 64])
        w2d = self.inp("hy_fw2", [64, 4 * D])
        fbd = self.inp("hy_fb", [64, 4])
        wind = self.inp("hy_win", [L, D])
        fwdd = self.inp("hy_fwd", [16, 128, 16 * 2 * 128])
        invd = self.inp("hy_inv", [16, 128, 32 * 128])
        biasd = self.inp("hy_bias", [2, D])
        wid = self.inp("hy_wi", [24, 128, 8 * 128])
        cvd = self.inp("hy_cv", [128, 24 * 4])
        wod = self.inp("hy_wo", [128, 8 * D])
        Gd = self.scratch("hy_G", [2, 16, 128, 3 * D], F32)
        xd = [self.scratch("hy_x%d" % o, [L, D], BF16) for o in range(2)]

        hid2T = ar.alloc("hid2T", (L,), BF16)
        w2 = ar.alloc("hw2", (4 * D,), BF16)
        acc = ar.alloc("hacc", (4 * D,), F32)
        hp = ar.alloc("hhp", (16, D), BF16)
        hm = ar.alloc("hhm", (16, D), BF16)
        tmps = [ar.alloc("htmp%d" % n, (512,), F32) for n in range(4)]
        featsT = ar2.alloc("featsT", (L,), F32)
        hid1T = ar2.alloc("hid1T", (L,), F32)
        hid2f = ar2.alloc("hid2f", (L,), F32)
        w0 = ar2.alloc("hw0", (64,), F32)
        w1 = ar2.alloc("hw1", (64,), F32)
        fb = ar2.alloc("hfb", (4,), F32)
        st_ = [ar2.alloc("hst%d" % n, (512,), F32) for n in range(3)]
        st64 = [Buf("hst64_%d" % n, b.t[0:64, :]) for n, b in enumerate(st_)]
        P.dma("sync", featsT.t[0:33, :], featd, writes=[featsT[None]])
        P.dma("sync", w0.t[0:33, :], w0d, writes=[w0[None]])
        P.dma("sync", w1.t[0:64, :], w1d, writes=[w1[None]])
        P.dma("sync", fb.t[0:64, :], fbd, writes=[fb[None]])
        P.dma("gpsimd", w2.t[0:64, :], w2d, writes=[w2[None]])
        self.P.op("vector", lambda e: e.memset(acc.t[:], 0.0), writes=[acc[None]])
        for t in range(4):
            sl = slice(t * 512, (t + 1) * 512)
            pp = self.ps[t % 2]
            self.mm(pp.t[0:64, :], w0.t[0:33, :], featsT.t[0:33, sl], True, True, [w0[None], featsT[None]], [pp[None]])
            self.sin_mlp(hid1T.t[0:64, sl], pp.t[0:64, :], fb.t[0:64, 0:1], fb.t[0:64, 2:3],
                         st64,
                         [pp[None], fb[None]], [hid1T[t]])
        for t in range(4):
            sl = slice(t * 512, (t + 1) * 512)
            pp = self.ps[2 + t % 2]
            self.mm(pp.t[0:64, :], w1.t[0:64, :], hid1T.t[0:64, sl], True, True, [w1[None], hid1T[t]], [pp[None]])
            self.sin_mlp(hid2f.t[0:64, sl], pp.t[0:64, :], fb.t[0:64, 1:2], fb.t[0:64, 3:4],
                         st64,
                         [pp[None], fb[None]], [hid2f[t]])
            self.cp("vector", hid2T.t[0:64, sl], hid2f.t[0:64, sl], [hid2f[t]], [hid2T[t]])
        P.barrier()
        ar2.reset()
        winb = [ar2.alloc("hwin%d" % n, (D,), F32) for n in range(2)]

        def filt_tile(lc, ct, pb):
            self.mm(pb.t[:], hid2T.t[0:64, lc * 128:(lc + 1) * 128], w2.t[0:64, ct * 512:(ct + 1) * 512], True, True,
                    [hid2T[None], w2[None]], [pb[None]])

        for lc in range(16):
            wn = winb[lc % 2]
            P.dma("sync", wn.t[:], wind[lc * 128:(lc + 1) * 128, :], writes=[wn[None]])
            for ct in range(8):
                pb = self.ps[ct % 4]
                tm = tmps[ct % 2]
                dh = ct % 2
                filt_tile(lc, ct, pb)
                self.tt("vector", tm.t[:], pb.t[:], wn.t[:, dh * 512:(dh + 1) * 512], ALU.mult, [pb[None], wn[None]], [tm[None]])
                tm2 = tmps[2 + ct % 2]
                self.act(tm2.t[:], tm.t[:], AF.Abs, [tm[None]], [tm2[None]])
                self.tt("gpsimd" if ct % 4 < 3 else "vector", acc.t[:, ct * 512:(ct + 1) * 512], tm2.t[:], acc.t[:, ct * 512:(ct + 1) * 512], ALU.add,
                        [tm2[None], acc[ct]], [acc[ct]])
        for ct in range(8):
            pb = self.ps[4 + ct % 2]
            self.mm(pb.t[:], self.ones_f.t[:], acc.t[:, ct * 512:(ct + 1) * 512], True, True, [self.ones_f[None], acc[ct]], [pb[None]])
            P.op("vector", lambda e, ct=ct, pb=pb: e.reciprocal(out=acc.t[:, ct * 512:(ct + 1) * 512], in_=pb.t[:]),
                 reads=[pb[None]], writes=[acc[ct]])
        w2pm = ar2.alloc("hw2pm", (2, D), BF16)
        for o in range(2):
            for dh in range(2):
                c0 = (0 * 4 + o * 2 + dh) * 512
                c1 = (1 * 4 + o * 2 + dh) * 512
                ta, tb = tmps[0], tmps[1]
                self.tt("vector", ta.t[0:64, :], w2.t[0:64, c0:c0 + 512], acc.t[0:64, c0:c0 + 512], ALU.mult, [w2[None], acc[None]], [ta[None]])
                self.tt("vector", tb.t[0:64, :], w2.t[0:64, c1:c1 + 512], acc.t[0:64, c1:c1 + 512], ALU.mult, [w2[None], acc[None]], [tb[None]])
                self.tt("vector", w2pm.t[0:64, 0, dh * 512:(dh + 1) * 512], ta.t[0:64, :], tb.t[0:64, :], ALU.add, [ta[None], tb[None]], [w2pm[(0, dh)]])
                self.tt("vector", w2pm.t[0:64, 1, dh * 512:(dh + 1) * 512], ta.t[0:64, :], tb.t[0:64, :], ALU.subtract, [ta[None], tb[None]], [w2pm[(1, dh)]])
            nb = 0
            for lc in range(16):
                wn = winb[lc % 2]
                P.dma("sync", wn.t[:], wind[lc * 128:(lc + 1) * 128, :], writes=[wn[None]])
                for dh in range(2):
                    dsl = slice(dh * 512, (dh + 1) * 512)
                    for pm, dst in ((0, hp), (1, hm)):
                        pb = self.ps[nb % 4]
                        nb += 1
                        self.mm(pb.t[:], hid2T.t[0:64, lc * 128:(lc + 1) * 128], w2pm.t[0:64, pm, dsl], True, True,
                                [hid2T[None], w2pm[(pm, dh)]], [pb[None]])
                        self.tt("vector", dst.t[:, lc, dsl], pb.t[:], wn.t[:, dsl], ALU.mult, [pb[None], wn[None]], [dst[lc]])
            P.barrier()
            ar2.reset()
            ftb = [ar2.alloc("hft%d" % n, (16, 2, 128), BF16) for n in range(2)]
            stage = ar2.alloc("hstage", (3, D), F32)
            bias_bc = ar2.alloc("hbias", (D,), F32)
            P.dma("sync", bias_bc.t[:], biasd[o:o + 1, :].to_broadcast([128, D]), writes=[bias_bc[None]])
            for fc in range(16):
                ft = ftb[fc % 2]
                P.dma("gpsimd", ft.t[:], fwdd[fc].rearrange("p (a b c) -> p a b c", a=16, b=2), writes=[ft[None]])
                b0 = 4 * (fc % 2)
                self.dft_fwd(hp, ft, 0, [self.ps[b0], self.ps[b0 + 1]])
                self.dft_fwd(hm, ft, 1, [self.ps[b0 + 2], self.ps[b0 + 3]])
                if fc == 0:
                    self.dft_fwd(hp, ft, 1, [self.ps[4], self.ps[5]])
                for dh in range(2):
                    dsl = slice(dh * 512, (dh + 1) * 512)
                    self.tt("vector", stage.t[:, 0, dsl], self.ps[b0 + dh].t[:], bias_bc.t[:, dsl], ALU.add,
                            [self.ps[b0 + dh][None], bias_bc[None]], [stage[(0, dh)]])
                    self.cp("scalar", stage.t[:, 2, dsl], stage.t[:, 0, dsl], [stage[(0, dh)]], [stage[(2, dh)]])
                    self.cp("scalar", stage.t[:, 1, dsl], self.ps[b0 + 2 + dh].t[:], [self.ps[b0 + 2 + dh][None]], [stage[(1, dh)]])
                    if fc == 0:
                        P.op("vector", lambda e, dsl=dsl: e.memset(stage.t[0:1, 1, dsl], 0.0), reads=[], writes=[stage[(1, dh)]])
                        self.tt("vector", stage.t[0:1, 2, dsl], self.ps[4 + dh].t[0:1, :], bias_bc.t[0:1, dsl], ALU.add,
                                [self.ps[4 + dh][None], bias_bc[None]], [stage[(2, dh)]])
                P.dma("sync", Gd.t[o, fc], stage.t[:].rearrange("p a b -> p (a b)"), reads=[stage[None]], writes=[Gd[(o, fc)]])
            P.barrier()
            ar2.reset()
            winb = [ar2.alloc("hwin%d_%d" % (n, o), (D,), F32) for n in range(2)]
            w2pm = ar2.alloc("hw2pm_%d" % o, (2, D), BF16)

        P.barrier()
        ar.reset()
        ar2.reset()
        self.norm_phase(i, 0, 0)
        z_tok = ar.alloc("hz", (16, D), BF16)
        mark = ar.off
        wib = [ar.alloc("hwi%d" % n, (8, 128), BF16) for n in range(3)]
        raw = [ar.alloc("hraw%d" % n, (L + 2,), F32) for n in range(2)]
        cacc = [ar.alloc("hcacc%d" % n, (L,), F32) for n in range(2)]
        obf = [ar.alloc("hobf%d" % n, (L,), BF16) for n in range(2)]
        xst = [ar.alloc("hxst%d" % n, (16, 128), BF16) for n in range(2)]
        cv = ar.alloc("hcv", (24, 4), F32)
        P.dma("sync", cv.t[:], cvd.rearrange("p (a b) -> p a b", b=4), writes=[cv[None]])
        for n in range(2):
            P.op("vector", lambda e, n=n: e.memset(raw[n].t[:, 0:1], 0.0), writes=[raw[n]["pad"]])
            P.op("vector", lambda e, n=n: e.memset(raw[n].t[:, L + 1:L + 2], 0.0), writes=[raw[n]["pad"]])
        ntrc = {"n": 0}

        def hy_part_b(fcx, ob, ca):
            self.act(ob.t[:], ca.t[:], AF.Identity, [ca[None], cv[None]], [ob[None]], bias=cv.t[:, fcx, 3:4], scale=1.0)
            kind = fcx // 8
            fcol = fcx % 8
            xs_ = xst[fcx % 2]
            for q in range(4):
                pT = self.ps[2 + ntrc["n"] % 2]
                ntrc["n"] += 1
                pTv = pT.t[:].bitcast(BF16)[:, 0:512].rearrange("p (a b) -> p a b", a=4)
                for tt_ in range(4):
                    tc = q * 4 + tt_
                    P.op("tensor", lambda e, pTv=pTv, tt_=tt_, ob=ob, tc=tc: e.transpose(pTv[:, tt_, :], ob.t[:, tc * 128:(tc + 1) * 128], self.ident.t[:]),
                         reads=[ob[None], self.ident[None]], writes=[pT[None]])
                if kind == 0:
                    self.cp("vector", z_tok.t[:, q * 4:(q + 1) * 4, fcol * 128:(fcol + 1) * 128], pTv, [pT[None]],
                            RK(z_tok, [q * 4 + a for a in range(4)]))
                else:
                    self.cp("vector", xs_.t[:, q * 4:(q + 1) * 4, :], pTv, [pT[None]], [xs_[q]])
            if kind > 0:
                xdd = xd[kind - 1]
                P.dma("sync", xdd.t.rearrange("(tc p) f -> p tc f", p=128)[:, :, fcol * 128:(fcol + 1) * 128], xs_.t[:],
                      reads=[xs_[None]], writes=[xdd[fcol]])

        hy_def = [None]
        for fcx in range(24):
            wi = wib[fcx % 3]
            rw = raw[fcx % 2]
            ca = cacc[fcx % 2]
            ob = obf[fcx % 2]
            P.dma("gpsimd", wi.t[:], wid[fcx].rearrange("p (k f) -> p k f", k=8), writes=[wi[None]])
            for T in range(4):
                pp = self.ps[T % 2]
                for k in range(8):
                    self.mm(pp.t[:], wi.t[:, k, :], self.hn.t[:, k, T * 512:(T + 1) * 512], k == 0, k == 7,
                            [wi[None], self.hn[(k, T)]], [pp[None]], inc=(k == 7))
                self.cp("scalar", rw.t[:, 1 + T * 512:1 + (T + 1) * 512], pp.t[:], [pp[None]], [rw[T]])
            rall = [rw[T] for T in range(4)] + [rw["pad"]]
            self.ts("vector", ca.t[:], rw.t[:, 0:L], cv.t[:, fcx, 0:1], 0.0, ALU.mult, ALU.add, rall + [cv[None]], [ca[None]])
            self.stt("vector", ca.t[:], rw.t[:, 1:L + 1], cv.t[:, fcx, 1:2], ca.t[:], ALU.mult, ALU.add, rall + [cv[None], ca[None]], [ca[None]])
            self.stt("vector", ca.t[:], rw.t[:, 2:L + 2], cv.t[:, fcx, 2:3], ca.t[:], ALU.mult, ALU.add, rall + [cv[None], ca[None]], [ca[None]])
            if hy_def[0] is not None:
                hy_def[0]()
            hy_def[0] = (lambda fcx=fcx, ob=ob, ca=ca: hy_part_b(fcx, ob, ca))
        hy_def[0]()
        P.barrier()
        ar.off = mark
        RQ = ar.alloc("hRQ", (32, D), BF16)
        tqa = [ar.alloc("htq%d" % n, (512,), F32) for n in range(2)]
        for o in range(2):
            ar2.reset()
            ftb = [ar2.alloc("hcft%d_%d" % (n, o), (16, 2, 128), BF16) for n in range(2)]
            gtb = [ar2.alloc("hcg%d_%d" % (n, o), (3, D), BF16) for n in range(2)]
            tq = tqa + [ar2.alloc("htqb%d_%d" % (n, o), (512,), F32) for n in range(2)]
            for fc in range(16):
                ft = ftb[fc % 2]
                gt = gtb[fc % 2]
                P.dma("gpsimd", ft.t[:], fwdd[fc].rearrange("p (a b c) -> p a b c", a=16, b=2), writes=[ft[None]])
                P.dma("gpsimd", gt.t[:], Gd.t[o, fc].rearrange("p (a b) -> p a b", a=3), reads=[Gd[(o, fc)]], writes=[gt[None]])
                b0 = 4 * (fc % 2)
                self.dft_fwd(z_tok, ft, 0, [self.ps[b0], self.ps[b0 + 1]])
                self.dft_fwd(z_tok, ft, 1, [self.ps[b0 + 2], self.ps[b0 + 3]])
                for dh in range(2):
                    dsl = slice(dh * 512, (dh + 1) * 512)
                    pA, pB = self.ps[b0 + dh], self.ps[b0 + 2 + dh]
                    t1, t2, t3, t4 = tq
                    self.tt("vector", t1.t[:], pA.t[:], gt.t[:, 0, dsl], ALU.mult, [pA[None], gt[None]], [t1[None]])
                    self.tt("vector", t2.t[:], pB.t[:], gt.t[:, 1, dsl], ALU.mult, [pB[None], gt[None]], [t2[None]])
                    self.tt("vector", RQ.t[:, fc, dsl], t1.t[:], t2.t[:], ALU.subtract, [t1[None], t2[None]], [RQ[fc]])
                    self.tt("vector", t3.t[:], pA.t[:], gt.t[:, 1, dsl], ALU.mult, [pA[None], gt[None]], [t3[None]])
                    self.tt("vector", t4.t[:], pB.t[:], gt.t[:, 2, dsl], ALU.mult, [pB[None], gt[None]], [t4[None]])
                    self.tt("vector", RQ.t[:, 16 + fc, dsl], t3.t[:], t4.t[:], ALU.add, [t3[None], t4[None]], [RQ[16 + fc]])
            P.barrier()
            ar2.reset()
            ivb = [ar2.alloc("hiv%d_%d" % (n, o), (32, 128), BF16) for n in range(2)]
            xcb = [ar2.alloc("hxc%d_%d" % (n, o), (D,), BF16) for n in range(2)]
            for tc in range(16):
                iv = ivb[tc % 2]
                xc = xcb[tc % 2]
                P.dma("gpsimd", iv.t[:], invd[tc].rearrange("p (a b) -> p a b", a=32), writes=[iv[None]])
                P.dma("sync", xc.t[:], xd[o].t[tc * 128:(tc + 1) * 128, :], reads=[xd[o][None]], writes=[xc[None]])
                for dh in range(2):
                    dsl = slice(dh * 512, (dh + 1) * 512)
                    py = self.ps[4 + dh]
                    for fcx in range(32):
                        self.mm(py.t[:], iv.t[:, fcx, :], RQ.t[:, fcx, dsl], fcx == 0, fcx == 31, [iv[None], RQ[fcx]], [py[None]], inc=(fcx == 31))
                    self.tt("vector", z_tok.t[:, tc, dsl], py.t[:], xc.t[:, dsl], ALU.mult, [py[None], xc[None]], [z_tok[tc]])
            P.barrier()
        ar2.reset()
        wo = ar2.alloc("hwo", (8, D), BF16)
        z3T = ar2.alloc("hz3T", (8, 512), BF16)
        P.dma("gpsimd", wo.t[:], wod.rearrange("p (k n) -> p k n", k=8), writes=[wo[None]])
        ntr = 0
        for T in range(4):
            sl = slice(T * 512, (T + 1) * 512)
            for tcc in range(4):
                tc = T * 4 + tcc
                for q in range(2):
                    pT = self.ps[ntr % 2]
                    ntr += 1
                    pTv = pT.t[:].bitcast(BF16)[:, 0:512].rearrange("p (a b) -> p a b", a=4)
                    for dd in range(4):
                        dc = q * 4 + dd
                        P.op("tensor", lambda e, pTv=pTv, dd=dd, tc=tc, dc=dc: e.transpose(pTv[:, dd, :], z_tok.t[:, tc, dc * 128:(dc + 1) * 128], self.ident.t[:]),
                             reads=[z_tok[tc], self.ident[None]], writes=[pT[None]])
                    self.cp("vector", z3T.t[:, q * 4:(q + 1) * 4, tcc * 128:(tcc + 1) * 128], pTv, [pT[None]],
                            RK(z3T, [q * 4 + a for a in range(4)]))
            for dc in range(8):
                po = self.ps[2 + dc % 2]
                for k in range(8):
                    self.mm(po.t[:], wo.t[:, k, dc * 128:(dc + 1) * 128], z3T.t[:, k, :], k == 0, k == 7, [wo[None], z3T[k]], [po[None]], inc=(k == 7))
                self.stt("vector", self.h.t[:, dc, sl], po.t[:], self.modv(i, 2, dc), self.h.t[:, dc, sl], ALU.mult, ALU.add,
                         [po[None], self.modb[i], self.h[(dc, T)]], [self.h[(dc, T)]])

    def mix_ssd(self, i):
        P, ar, ar2 = self.P, self.ar, self.ar2
        j = i // 4
        wid = self.inp("ss_wi", [32, 128, 8 * 128])
        wdtd = self.inp("ss_wdt", [128, 8 * 64])
        cvd = self.inp("ss_cv", [128, 32 * 4])
        smalld = self.inp("ss_small", [1, 64 + 64 + 32])
        nwd = self.inp("ss_nw", [1, 2048])
        wod = self.inp("ss_wo", [128, 16 * D])
        ctxd = self.inp("ctxT", [D, LC])
        zd = self.scratch("ss_z", [L, 2048], BF16)
        xd = self.scratch("ss_x", [L, 2048], BF16)
        btokd = self.scratch("ss_btok", [L, 1024], BF16)
        BTd = self.scratch("ss_BT", [1024, L], BF16)
        CTd = self.scratch("ss_CT", [1024, L], BF16)
        xcd = self.scratch("ss_xc", [LC, 2048], BF16)
        bcd = self.scratch("ss_bc", [LC, 1024], BF16)
        Hbd = self.scratch("ss_Hb", [16, 128, 2048], BF16)
        ynTd = self.scratch("ss_ynT", [2048, L], BF16)

        self.norm_phase(i, 0, 0)
        dt_tok = ar.alloc("sdt", (18, 64), F32)
        small = ar.alloc("ssmall", (160,), F32)
        a_bc = ar.alloc("sabc", (64,), F32)
        mark0 = ar.off
        P.dma("sync", small.t[:], smalld[0:1, :].to_broadcast([128, 160]), writes=[small[None]])
        self.act(a_bc.t[:], small.t[:, 0:64], AF.Exp, [small[None]], [a_bc[None]])
        self.ts("vector", a_bc.t[:], a_bc.t[:], -1.0, 0.0, ALU.mult, ALU.add, [a_bc[None]], [a_bc[None]])
        dtb_bc = small.t[:, 64:128]
        dsk_bc = small.t[:, 128:160]
        hc = ar.alloc("shc", (8, LC), F32)
        hcn = ar.alloc("shcn", (8, LC), BF16)
        for c in range(8):
            P.dma("sync", hc.t[:, c, :], ctxd[c * 128:(c + 1) * 128, :], writes=[hc[(c, 0)]])
        self.norm_mod(hc, LC, lambda c: self.lv.t[:, i, 2, c:c + 1], lambda c: self.modv(i, 0, c, 1), hcn,
                      [self.lv[(i, 2)], self.modb[i]], ar)
        wib = [ar.alloc("swi%d" % n, (8, 128), BF16) for n in range(3)]
        raw = [ar.alloc("sraw%d" % n, (L + 2,), F32) for n in range(2)]
        cacc = [ar.alloc("scacc%d" % n, (L,), F32) for n in range(2)]
        obf = [ar.alloc("sobf%d" % n, (L,), BF16) for n in range(2)]
        xst = [ar.alloc("sxst%d" % n, (16, 128), BF16) for n in range(2)]
        cv = ar.alloc("scv", (32, 4), F32)
        wdt = ar.alloc("swdt", (8, 64), BF16)
        sp = [ar.alloc("ssp%d" % n, (64,), F32) for n in range(4)]
        P.dma("sync", cv.t[:], cvd.rearrange("p (a b) -> p a b", b=4), writes=[cv[None]])
        P.dma("gpsimd", wdt.t[:], wdtd.rearrange("p (k f) -> p k f", k=8), writes=[wdt[None]])
        for n in range(2):
            P.op("vector", lambda e, n=n: e.memset(raw[n].t[:, 0:1], 0.0), writes=[raw[n]["pad"]])
        cnt = {"n": 0, "tr": 0}

        def proj_chunk(fcx, src, srckeys, ntok, dst_tok, dst_fm):
            n = cnt["n"]
            cnt["n"] += 1
            wi, rw, ca, ob, xs_ = wib[n % 3], raw[n % 2], cacc[n % 2], obf[n % 2], xst[n % 2]
            TW = min(512, ntok)
            NT = ntok // TW
            P.dma("gpsimd", wi.t[:], wid[fcx - 16].rearrange("p (k f) -> p k f", k=8), writes=[wi[None]])
            is_z = fcx < 16
            for T in range(NT):
                pp = self.ps[T % 2]
                for k in range(8):
                    self.mm(pp.t[:, 0:TW], wi.t[:, k, :], src.t[:, k, T * TW:(T + 1) * TW], k == 0, k == 7,
                            [wi[None], srckeys(k, T)], [pp[None]], inc=(k == 7))
                if is_z:
                    self.act(ob.t[:, T * TW:(T + 1) * TW], pp.t[:, 0:TW], AF.Silu, [pp[None]], [ob[T]])
                else:
                    self.cp("scalar", rw.t[:, 1 + T * TW:1 + (T + 1) * TW], pp.t[:, 0:TW], [pp[None]], [rw[T]])
            if not is_z:
                cx = fcx - 16
                P.op("vector", lambda e, rw=rw: e.memset(rw.t[:, ntok + 1:ntok + 2], 0.0), writes=[rw["pad2"]])
                rall = [rw[T] for T in range(NT)] + [rw["pad"], rw["pad2"]]
                self.ts("vector", ca.t[:, 0:ntok], rw.t[:, 0:ntok], cv.t[:, cx, 0:1], 0.0, ALU.mult, ALU.add, rall + [cv[None]], [ca[None]])
                self.stt("vector", ca.t[:, 0:ntok], rw.t[:, 1:ntok + 1], cv.t[:, cx, 1:2], ca.t[:, 0:ntok], ALU.mult, ALU.add, rall + [cv[None], ca[None]], [ca[None]])
                self.stt("vector", ca.t[:, 0:ntok], rw.t[:, 2:ntok + 2], cv.t[:, cx, 2:3], ca.t[:, 0:ntok], ALU.mult, ALU.add, rall + [cv[None], ca[None]], [ca[None]])
            prev = cnt.get("deferred")
            cnt["deferred"] = None
            if prev is not None:
                prev()
            cnt["deferred"] = lambda: part_b(dst_tok, dst_fm, ntok, ob, xs_, ca, fcx - 16)

        def part_b(dst_tok, dst_fm, ntok, ob, xs_, ca, cx):
            self.act(ob.t[:, 0:ntok], ca.t[:, 0:ntok], AF.Silu, [ca[None], cv[None]], [ob[None]], bias=cv.t[:, cx, 3:4], scale=1.0)
            if dst_fm is not None:
                dbuf, row0 = dst_fm
                P.dma("sync", dbuf.t[row0:row0 + 128, 0:ntok], ob.t[:, 0:ntok], reads=[ob[None]], writes=[dbuf[row0]])
            if dst_tok is not None:
                dbuf, fcol = dst_tok
                ntc = ntok // 128
                for q in range((ntc + 3) // 4):
                    nq = min(4, ntc - q * 4)
                    pT = self.ps[2 + cnt["tr"] % 2]
                    cnt["tr"] += 1
                    pTv = pT.t[:].bitcast(BF16)[:, 0:512].rearrange("p (a b) -> p a b", a=4)
                    for tt_ in range(nq):
                        tc = q * 4 + tt_
                        P.op("tensor", lambda e, pTv=pTv, tt_=tt_, ob=ob, tc=tc: e.transpose(pTv[:, tt_, :], ob.t[:, tc * 128:(tc + 1) * 128], self.ident.t[:]),
                             reads=[ob[None], self.ident[None]], writes=[pT[None]])
                    self.cp("vector", xs_.t[:, q * 4:q * 4 + nq, :], pTv[:, 0:nq, :], [pT[None]], [xs_[q]])
                P.dma("sync", dbuf.t.rearrange("(tc p) f -> p tc f", p=128)[:, :, fcol * 128:(fcol + 1) * 128], xs_.t[:, 0:ntc, :],
                      reads=[xs_[None]], writes=[dbuf[fcol]])

        hk = lambda k, T: self.hn[(k, T)]
        ck = lambda k, T: hcn[(k, 0)]
        wzd = self.inp("ss_wz", [128, 8 * 2048]).rearrange("p (k n) -> p k n", k=8)
        wzb = [ar.alloc("swz%d" % n, (8, 512), BF16) for n in range(2)]
        zst = [ar.alloc("szst%d" % n, (512,), BF16) for n in range(3)]
        nz = 0
        for piece in range(4):
            wz = wzb[piece % 2]
            P.dma("gpsimd", wz.t[:], wzd[:, :, piece * 512:(piece + 1) * 512], writes=[wz[None]])
            for tc in range(16):
                pp = self.ps[4 + nz % 2]
                zs_ = zst[nz % 3]
                nz += 1
                for k in range(8):
                    self.mm(pp.t[:], self.hn.t[:, k, tc * 128:(tc + 1) * 128], wz.t[:, k, :], k == 0, k == 7,
                            [self.hn[(k, tc // 4)], wz[None]], [pp[None]], inc=(k == 7))
                self.act(zs_.t[:], pp.t[:], AF.Silu, [pp[None]], [zs_[None]])
                P.dma("sync", zd.t[tc * 128:(tc + 1) * 128, piece * 512:(piece + 1) * 512], zs_.t[:], reads=[zs_[None]], writes=[zd[(tc, piece)]])
        for fcx in range(16, 32):
            proj_chunk(fcx, self.hn, hk, L, (xd, fcx - 16), None)
            proj_chunk(fcx, hcn, ck, LC, (xcd, fcx - 16), None)
        for fcx in range(32, 40):
            proj_chunk(fcx, self.hn, hk, L, (btokd, fcx - 32), (BTd, (fcx - 32) * 128))
            proj_chunk(fcx, hcn, ck, LC, (bcd, fcx - 32), None)
        for fcx in range(40, 48):
            proj_chunk(fcx, self.hn, hk, L, None, (CTd, (fcx - 40) * 128))
        if cnt.get("deferred") is not None:
            cnt["deferred"]()
            cnt["deferred"] = None
        for c in range(18):
            pp = self.ps[4 + c % 2]
            for k in range(8):
                if c < 16:
                    lhs, rk = self.hn.t[:, k, c * 128:(c + 1) * 128], self.hn[(k, c // 4)]
                else:
                    lhs, rk = hcn.t[:, k, (c - 16) * 128:(c - 15) * 128], hcn[(k, 0)]
                self.mm(pp.t[:, 0:64], lhs, wdt.t[:, k, :], k == 0, k == 7, [rk, wdt[None]], [pp[None]], inc=(k == 7))
            x_, ax, e_, l_ = sp
            self.tt("vector", x_.t[:], pp.t[:, 0:64], dtb_bc, ALU.add, [pp[None], small[None]], [x_[None]])
            self.act(ax.t[:], x_.t[:], AF.Abs, [x_[None]], [ax[None]])
            self.act(e_.t[:], ax.t[:], AF.Exp, [ax[None]], [e_[None]], scale=-1.0)
            self.act(l_.t[:], e_.t[:], AF.Ln, [e_[None]], [l_[None]], bias=1.0, scale=1.0)
            self.ts("vector", ax.t[:], x_.t[:], 0.0, 0.0, ALU.max, ALU.add, [x_[None]], [ax[None]])
            self.tt("vector", dt_tok.t[:, c, :], ax.t[:], l_.t[:], ALU.add, [ax[None], l_[None]], [dt_tok[c]])
        P.barrier()
        ar.off = mark0
        ar2.reset()

        tri = [ar.alloc("stri%d" % n, (128,), F32) for n in range(2)]
        neg = [ar.alloc("sneg%d" % n, (128,), F32) for n in range(2)]
        for d_, (pat, cm) in enumerate((([[1, 128]], -1), ([[-1, 128]], 1))):
            P.op("gpsimd", lambda e, d_=d_: e.memset(tri[d_].t[:], 1.0), writes=[tri[d_][None]])
            P.op("gpsimd", lambda e, d_=d_, pat=pat, cm=cm: e.affine_select(out=tri[d_].t[:], in_=tri[d_].t[:], pattern=pat,
                                                                             compare_op=ALU.is_ge, fill=0.0, base=0, channel_multiplier=cm),
                 reads=[tri[d_][None]], writes=[tri[d_][None]])
        for d_, (pat, cm) in enumerate((([[-1, 128]], 1), ([[1, 128]], -1))):
            P.op("gpsimd", lambda e, d_=d_: e.memset(neg[d_].t[:], -30000.0), writes=[neg[d_][None]])
            P.op("gpsimd", lambda e, d_=d_, pat=pat, cm=cm: e.affine_select(out=neg[d_].t[:], in_=neg[d_].t[:], pattern=pat,
                                                                             compare_op=ALU.is_gt, fill=0.0, base=0, channel_multiplier=cm),
                 reads=[neg[d_][None]], writes=[neg[d_][None]])
        neg4 = [ar.alloc("sneg4_%d" % n, (4, 128), BF16) for n in range(2)]
        for d_ in range(2):
            self.cp("vector", neg4[d_].t[:], neg[d_].t[:].rearrange("p (o l) -> p o l", o=1).to_broadcast([128, 4, 128]),
                    [neg[d_][None]], [neg4[d_][None]])
        nadt = [ar.alloc("snadt%d" % n, (32,), F32) for n in range(2)]
        ahi = [ar.alloc("sahi%d" % n, (32,), BF16) for n in range(2)]
        alo = [ar.alloc("salo%d" % n, (32,), BF16) for n in range(2)]
        nahi = [ar.alloc("snahi%d" % n, (32,), BF16) for n in range(2)]
        nalo = [ar.alloc("snalo%d" % n, (32,), BF16) for n in range(2)]
        tri_bf = [ar.alloc("stribf%d" % n, (128,), BF16) for n in range(2)]
        for d_ in range(2):
            self.cp("vector", tri_bf[d_].t[:], tri[d_].t[:], [tri[d_][None]], [tri_bf[d_][None]])
        H = [ar.alloc("sH%d" % n, (2048,), F32) for n in range(2)]
        for d_ in range(2):
            P.op("gpsimd", lambda e, d_=d_: e.memset(H[d_].t[:], 0.0), writes=[H[d_][None]])
        xtb = [ar.alloc("sxt%d" % n, (2048,), BF16) for n in range(2)]
        btb = [ar.alloc("sbt%d" % n, (8, 128), BF16) for n in range(2)]
        dxs = [ar.alloc("sdxs%d" % n, (2048,), BF16) for n in range(2)]
        cst = [ar.alloc("scst%d" % n, (64,), F32) for n in range(2)]
        adt = [ar.alloc("sadt%d" % n, (32,), F32) for n in range(2)]
        sw = [ar.alloc("ssw%d" % n, (32,), F32) for n in range(4)]
        edec = [ar.alloc("sedec%d" % n, (32,), F32) for n in range(2)]
        nld = {"n": 0}

        def load_xb(xsrc, bsrc, r0):
            n = nld["n"]
            nld["n"] += 1
            xt, bt = xtb[n % 2], btb[n % 2]
            P.dma("sync", xt.t[:], xsrc.t[r0:r0 + 128, :], reads=[xsrc[None]], writes=[xt[None]])
            P.dma("sync", bt.t[:], bsrc.t[r0:r0 + 128, :].rearrange("p (g n) -> p g n", g=8), reads=[bsrc[None]], writes=[bt[None]])
            return xt, bt

        def cums(d_, c):
            dtc = dt_tok.t[:, c, d_ * 32:(d_ + 1) * 32]
            self.tt("vector", adt[d_].t[:], dtc, a_bc.t[:, d_ * 32:(d_ + 1) * 32], ALU.mult, [dt_tok[c], a_bc[None]], [adt[d_][None]])
            pc = self.ps[6]
            self.mm(pc.t[:, 0:32], tri[d_].t[:], adt[d_].t[:], True, True, [tri[d_][None], adt[d_][None]], [pc[None]])
            self.mm(pc.t[:, 32:64], self.ones_f.t[:], adt[d_].t[:], True, True, [self.ones_f[None], adt[d_][None]], [pc[None]])
            self.cp("scalar", cst[d_].t[:], pc.t[:, 0:64], [pc[None]], [cst[d_][None]])
            self.cp("vector", ahi[d_].t[:], adt[d_].t[:], [adt[d_][None]], [ahi[d_][None]])
            self.tt("vector", alo[d_].t[:], adt[d_].t[:], ahi[d_].t[:], ALU.subtract, [adt[d_][None], ahi[d_][None]], [alo[d_][None]])
            self.ts("gpsimd", nahi[d_].t[:], ahi[d_].t[:], -1.0, 0.0, ALU.mult, ALU.add, [ahi[d_][None]], [nahi[d_][None]])
            self.ts("gpsimd", nalo[d_].t[:], alo[d_].t[:], -1.0, 0.0, ALU.mult, ALU.add, [alo[d_][None]], [nalo[d_][None]])

        def state_step(d_, c, xt, bt):
            dtc = dt_tok.t[:, c, d_ * 32:(d_ + 1) * 32]
            dd, ee, ww = sw[d_ * 2], sw[d_ * 2 + 1], sw[d_ * 2]
            self.tt("vector", dd.t[:], cst[d_].t[:, 32:64], cst[d_].t[:, 0:32], ALU.subtract, [cst[d_][None]], [dd[None]])
            self.act(ee.t[:], dd.t[:], AF.Exp, [dd[None]], [ee[None]])
            self.tt("vector", ww.t[:], ee.t[:], dtc, ALU.mult, [ee[None], dt_tok[c]], [ww[None]])
            self.act(edec[d_].t[:], cst[d_].t[:, 32:64], AF.Exp, [cst[d_][None]], [edec[d_][None]])
            dx = dxs[d_]
            self.tt("vector", dx.t[:].rearrange("p (h q) -> p h q", h=32), xt.t[:].rearrange("p (h q) -> p h q", h=32),
                    ww.t[:].rearrange("p (h o) -> p h o", o=1).to_broadcast([128, 32, 64]), ALU.mult, [xt[None], ww[None]], [dx[None]])
            Hd = H[d_]
            self.tt("gpsimd", Hd.t[:].rearrange("p (h q) -> p h q", h=32), Hd.t[:].rearrange("p (h q) -> p h q", h=32),
                    edec[d_].t[:].rearrange("p (h o) -> p h o", o=1).to_broadcast([128, 32, 64]), ALU.mult, [Hd[None], edec[d_][None]], [Hd[None]])
            for gp in range(4):
                pS = self.ps[7]
                for gi in range(2):
                    g = gp * 2 + gi
                    self.mm(pS.t[:, gi * 256:(gi + 1) * 256], bt.t[:, g, :], dx.t[:, g * 256:(g + 1) * 256], True, True,
                            [bt[None], dx[None]], [pS[None]])
                self.tt("vector", Hd.t[:, gp * 512:(gp + 1) * 512], Hd.t[:, gp * 512:(gp + 1) * 512], pS.t[:], ALU.add,
                        [Hd[None], pS[None]], [Hd[None]])

        for c in (0, 1):
            xt, bt = load_xb(xcd, bcd, c * 128)
            cums(0, 16 + c)
            state_step(0, 16 + c, xt, bt)
        for c in (1, 0):
            xt, bt = load_xb(xcd, bcd, c * 128)
            cums(1, 16 + c)
            state_step(1, 16 + c, xt, bt)
        hbst = [ar.alloc("shbst%d" % n, (2048,), BF16) for n in range(1)]
        for c in range(15, -1, -1):
            hb_ = hbst[0]
            self.cp("scalar", hb_.t[:], H[1].t[:], [H[1][None]], [hb_[None]])
            P.dma("sync", Hbd.t[c], hb_.t[:], reads=[hb_[None]], writes=[Hbd[c]])
            if c > 0:
                xt, bt = load_xb(xd, btokd, c * 128)
                cums(1, c)
                state_step(1, c, xt, bt)

        nw_bc = ar.alloc("snw", (2048,), F32)
        P.dma("sync", nw_bc.t[:], nwd[0:1, :].to_broadcast([128, 2048]), writes=[nw_bc[None]])
        ztb = [ar.alloc("szt%d" % n, (2048,), BF16) for n in range(2)]
        BTb = [ar.alloc("sBT%d" % n, (8, 128), BF16) for n in range(2)]
        CTb = [ar.alloc("sCT%d" % n, (8, 128), BF16) for n in range(2)]
        Hbb = [ar.alloc("sHbb%d" % n, (2048,), BF16) for n in range(2)]
        Hfb = ar.alloc("sHfb", (2048,), BF16)
        xsdt = [ar.alloc("sxsdt%d" % n, (2048,), BF16) for n in range(2)]
        ea = ar.alloc("sea", (64,), F32)
        cscol = ar.alloc("scscol", (64,), F32)
        yacc = ar2.alloc("syacc", (2048,), F32)
        Eb = [ar2.alloc("sE%d" % n, (4, 128), BF16) for n in range(4)]
        xdsk = ar2.alloc("sxdsk", (2048,), BF16)
        Mb = [ar2.alloc("sM%d" % n, (4, 128), BF16) for n in range(4)]
        GTs = [ar.alloc("sGT%d" % n, (128,), BF16) for n in range(2)]
        t12 = [ar2.alloc("st12_%d" % n, (512,), F32) for n in range(2)]
        junk = ar.alloc("sjunk", (256,), BF16)
        ss = ar.alloc("sss", (16,), F32)
        ynb = ar2.alloc("synb", (2048,), BF16)
        ynT = [ar.alloc("synT%d" % n, (16, 128), BF16) for n in range(1)]
        nrt = {"n": 0, "m": 0, "g": 0, "t": 0}
        for c in range(16):
            r0 = c * 128
            xt, bt = load_xb(xd, btokd, r0)
            zt, BT, CT, Hbc = ztb[c % 2], BTb[c % 2], CTb[c % 2], Hbb[c % 2]
            P.dma("sync", zt.t[:], zd.t[r0:r0 + 128, :], reads=[zd[None]], writes=[zt[None]])
            P.dma("sync", BT.t[:], BTd.t.rearrange("(g n) t -> n g t", n=128)[:, :, r0:r0 + 128], reads=[BTd[None]], writes=[BT[None]])
            P.dma("sync", CT.t[:], CTd.t.rearrange("(g n) t -> n g t", n=128)[:, :, r0:r0 + 128], reads=[CTd[None]], writes=[CT[None]])
            P.dma("sync", Hbc.t[:], Hbd.t[c], reads=[Hbd[c]], writes=[Hbc[None]])
            for d_ in range(2):
                cums(d_, c)
                self.cp("vector", cscol.t[:, d_ * 32:(d_ + 1) * 32], cst[d_].t[:, 0:32], [cst[d_][None]], [cscol[d_]])
                dtc = dt_tok.t[:, c, d_ * 32:(d_ + 1) * 32]
                self.tt("vector" if d_ == 0 else "gpsimd", xsdt[d_].t[:].rearrange("p (h q) -> p h q", h=32), xt.t[:].rearrange("p (h q) -> p h q", h=32),
                        dtc.rearrange("p (h o) -> p h o", o=1).to_broadcast([128, 32, 64]), ALU.mult, [xt[None], dt_tok[c]], [xsdt[d_][None]])
            self.tt("vector", xdsk.t[:].rearrange("p (h q) -> p h q", h=32), xt.t[:].rearrange("p (h q) -> p h q", h=32),
                    dsk_bc.rearrange("p (h o) -> p h o", o=1).to_broadcast([128, 32, 64]), ALU.mult, [xt[None], small[None]], [xdsk[None]])
            self.act(ea.t[:], cscol.t[:], AF.Exp, [cscol[None]], [ea[None]])
            self.cp("scalar", Hfb.t[:], H[0].t[:], [H[0][None]], [Hfb[None]])
            pyd, pyf, pyb = self.ps[3], self.ps[4], self.ps[5]
            v3 = lambda ap: ap.rearrange("p (h q) -> p h q", h=8)
            bc8 = lambda ap: ap.rearrange("p (h o) -> p h o", o=1).to_broadcast([128, 8, 64])

            def emit_y(g, Mg):
                gp, gi = divmod(g, 2)
                if gi == 0:
                    self.mm(pyd.t[:], self.ident.t[:], xdsk.t[:, gp * 512:(gp + 1) * 512], True, False,
                            [self.ident[None], xdsk[None]], [pyd[None]])
                for hh in range(4):
                    hcol = slice((g * 4 + hh) * 64, (g * 4 + hh + 1) * 64)
                    ocol = slice((gi * 4 + hh) * 64, (gi * 4 + hh + 1) * 64)
                    self.mm(pyd.t[:, ocol], Mg[0].t[:, hh, :], xsdt[0].t[:, hcol], False, False,
                            [Mg[0][None], xsdt[0][None]], [pyd[None]])
                    self.mm(pyd.t[:, ocol], Mg[1].t[:, hh, :], xsdt[1].t[:, hcol], False, gi == 1 and hh == 3,
                            [Mg[1][None], xsdt[1][None]], [pyd[None]])
                self.mm(pyf.t[:, gi * 256:(gi + 1) * 256], CT.t[:, g, :], Hfb.t[:, g * 256:(g + 1) * 256], True, True,
                        [CT[None], Hfb[None]], [pyf[None]])
                self.mm(pyb.t[:, gi * 256:(gi + 1) * 256], CT.t[:, g, :], Hbc.t[:, g * 256:(g + 1) * 256], True, True,
                        [CT[None], Hbc[None]], [pyb[None]])
                if gi == 1:
                    csl = slice(gp * 512, (gp + 1) * 512)
                    hs = slice(gp * 8, gp * 8 + 8)
                    t1, t2 = t12
                    self.tt("vector", v3(t1.t[:]), v3(pyf.t[:]), bc8(ea.t[:, hs]), ALU.mult, [pyf[None], ea[None]], [t1[None]])
                    self.tt("vector", v3(t2.t[:]), v3(pyb.t[:]), bc8(ea.t[:, 32 + gp * 8:32 + gp * 8 + 8]), ALU.mult, [pyb[None], ea[None]], [t2[None]])
                    self.tt("vector", yacc.t[:, csl], pyd.t[:], t1.t[:], ALU.add, [pyd[None], t1[None]], [yacc[gp]])
                    self.tt("gpsimd", yacc.t[:, csl], yacc.t[:, csl], t2.t[:], ALU.add, [yacc[gp], t2[None]], [yacc[gp]])
                    self.tt("gpsimd", yacc.t[:, csl], yacc.t[:, csl], zt.t[:, csl], ALU.mult, [yacc[gp], zt[None]], [yacc[gp]])

            pending = None
            for g in range(8):
                pG = self.ps[0]
                GT = GTs[nrt["g"] % 2]
                nrt["g"] += 1
                self.mm(pG.t[:, 0:128], BT.t[:, g, :], CT.t[:, g, :], True, True, [BT[None], CT[None]], [pG[None]])
                self.cp("scalar", GT.t[:], pG.t[:, 0:128], [pG[None]], [GT[None]])
                Mg = []
                for d_ in range(2):
                    n = nrt["n"]
                    nrt["n"] += 1
                    E_ = Eb[n % 4]
                    M_ = Mb[nrt["m"] % 4]
                    nrt["m"] += 1
                    pc = self.ps[1 + n % 2]
                    h0 = g * 4
                    bc4 = lambda b: b.t[:, h0:h0 + 4].rearrange("p (h o) -> p h o", o=1).to_broadcast([128, 4, 128])
                    self.mm(pc.t[:], tri_bf[d_].t[:], bc4(nahi[d_]), True, False, [tri_bf[d_][None], nahi[d_][None]], [pc[None]], inc=False)
                    self.mm(pc.t[:], tri_bf[d_].t[:], bc4(nalo[d_]), False, False, [tri_bf[d_][None], nalo[d_][None]], [pc[None]], inc=False)
                    self.mm(pc.t[:], self.ident.t[:], neg4[d_].t[:].rearrange("p a b -> p (a b)"), False, False,
                            [self.ident[None], neg4[d_][None]], [pc[None]], inc=False)
                    for hh in range(4):
                        self.mm(pc.t[:, hh * 128:(hh + 1) * 128], ahi[d_].t[:, h0 + hh:h0 + hh + 1].to_broadcast([128, 128]), tri_bf[d_].t[:],
                                False, False, [ahi[d_][None], tri_bf[d_][None]], [pc[None]], inc=False)
                        self.mm(pc.t[:, hh * 128:(hh + 1) * 128], alo[d_].t[:, h0 + hh:h0 + hh + 1].to_broadcast([128, 128]), tri_bf[d_].t[:],
                                False, hh == 3, [alo[d_][None], tri_bf[d_][None]], [pc[None]], inc=(hh == 3))
                    self.act(E_.t[:], pc.t[:].rearrange("p (a b) -> p a b", a=4), AF.Exp, [pc[None]], [E_[None]])
                    self.tt("vector" if n % 2 == 0 else "gpsimd", M_.t[:], E_.t[:],
                            GT.t[:].rearrange("p (o l) -> p o l", o=1).to_broadcast([128, 4, 128]), ALU.mult,
                            [E_[None], GT[None]], [M_[None]])
                    Mg.append(M_)
                if pending is not None:
                    emit_y(*pending)
                pending = (g, Mg)
            emit_y(*pending)
            P.op("vector", lambda e: e.memset(ss.t[:, 0:8], 0.0), writes=[ss[None]])
            for g in range(8):
                P.op("scalar", lambda e, g=g: e.activation(out=junk.t[:], in_=yacc.t[:, g * 256:(g + 1) * 256], func=AF.Square,
                                                           accum_out=ss.t[:, g:g + 1]),
                     reads=[yacc[g // 2], ss[None]], writes=[junk[None], ss[g]])
            self.act(ss.t[:, 8:16], ss.t[:, 0:8], AF.Sqrt, [ss[None]], [ss[None]], bias=EPS, scale=1.0 / 256.0)
            P.op("vector", lambda e: e.reciprocal(out=ss.t[:, 8:16], in_=ss.t[:, 8:16]), reads=[ss[None]], writes=[ss[None]])
            self.tt("vector", yacc.t[:].rearrange("p (g q) -> p g q", g=8), yacc.t[:].rearrange("p (g q) -> p g q", g=8),
                    ss.t[:, 8:16].rearrange("p (g o) -> p g o", o=1).to_broadcast([128, 8, 256]), ALU.mult, [yacc[None], ss[None]], [yacc[None]])
            self.tt("gpsimd", ynb.t[:], yacc.t[:], nw_bc.t[:], ALU.mult, [yacc[None], nw_bc[None]], [ynb[None]])
            yT = ynT[0]
            for q in range(4):
                pT = self.ps[6 + q % 2]
                pTv = pT.t[:].bitcast(BF16)[:, 0:512].rearrange("p (a b) -> p a b", a=4)
                for tt_ in range(4):
                    fc = q * 4 + tt_
                    P.op("tensor", lambda e, pTv=pTv, tt_=tt_, fc=fc: e.transpose(pTv[:, tt_, :], ynb.t[:, fc * 128:(fc + 1) * 128], self.ident.t[:]),
                         reads=[ynb[None], self.ident[None]], writes=[pT[None]])
                self.cp("vector", yT.t[:, q * 4:(q + 1) * 4, :], pTv, [pT[None]], [yT[q]])
            P.dma("sync", ynTd.t.rearrange("(fc p) t -> p fc t", p=128)[:, :, r0:r0 + 128], yT.t[:], reads=[yT[None]], writes=[ynTd[c]])
            if c < 15:
                state_step(0, c, xt, bt)
        P.barrier()
        ar.off = mark0
        ar2.reset()
        wo = ar.alloc("swo", (16, D), BF16)
        yTb = [ar.alloc("syTb%d" % n, (16, 512), BF16) for n in range(2)]
        P.dma("gpsimd", wo.t[:], wod.rearrange("p (k n) -> p k n", k=16), writes=[wo[None]])
        for T in range(4):
            sl = slice(T * 512, (T + 1) * 512)
            yb = yTb[T % 2]
            P.dma("sync", yb.t[:], ynTd.t.rearrange("(fc p) t -> p fc t", p=128)[:, :, sl], reads=[ynTd[None]], writes=[yb[None]])
            for dc in range(8):
                po = self.ps[dc % 2]
                for fc in range(16):
                    self.mm(po.t[:], wo.t[:, fc, dc * 128:(dc + 1) * 128], yb.t[:, fc, :], fc == 0, fc == 15, [wo[None], yb[None]], [po[None]], inc=(fc == 15))
                self.stt("vector", self.h.t[:, dc, sl], po.t[:], self.modv(i, 2, dc), self.h.t[:, dc, sl], ALU.mult, ALU.add,
                         [po[None], self.modb[i], self.h[(dc, T)]], [self.h[(dc, T)]])


def _c(a):
    return np.ascontiguousarray(a, dtype=np.float32)


def _grid_T():
    t = np.arange(L)
    row, col = t // 64, t % 64
    q = D // 4
    omega = (10000.0 ** (-np.arange(q, dtype=np.float32) / q)).astype(np.float32)

    def enc(p):
        ang = p.astype(np.float32)[:, None] * omega
        return np.concatenate([np.sin(ang), np.cos(ang)], axis=-1)
    g = np.concatenate([enc(row), enc(col)], axis=-1).astype(np.float32)
    return _c(g.T)


def _fnet_tables():
    k = np.arange(256)
    ang = 2.0 * np.pi * ((k[:, None] * k[None, :]) % 256) / 256.0
    cw = np.stack([np.cos(ang), np.sin(ang)], axis=1)
    cw = cw.reshape(2, 128, 2, 256).transpose(1, 0, 2, 3)
    t = np.arange(L)
    angl = 2.0 * np.pi * ((t[:, None] * t[None, :]) % L) / float(L)
    sc = 1.0 / math.sqrt(L * 256.0)
    cl = np.stack([np.cos(angl) * sc, -np.sin(angl) * sc], axis=1)
    cl = cl.reshape(16, 128, 2, 4, 512).transpose(3, 1, 0, 2, 4)
    return _c(cw.reshape(128, -1)), _c(cl.reshape(4, 128, -1))


_HY_CACHE = {}


def _hyena_tables():
    if _HY_CACHE:
        return _HY_CACHE
    f32 = np.float32
    pos = np.arange(L, dtype=f32)[:, None]
    t = pos / f32(L - 1)
    freqs = np.linspace(1e-4, 15, 16, dtype=f32)[None, :]
    ang = freqs * (f32(2.0 * math.pi) * pos / f32(L))
    feats = np.concatenate([t, np.cos(ang), -np.sin(ang)], axis=-1).astype(f32)
    max_decay = math.log(1e-2) / 0.3
    min_decay = math.log(1e-2) / 1.5
    deltas = np.linspace(min_decay, max_decay, D, dtype=f32)
    win = np.exp(-t * np.abs(deltas)[None, :]).astype(f32)
    N = 2 * L
    tt_ = np.arange(L, dtype=np.int64)
    ff = np.arange(L, dtype=np.int64)
    th = 2.0 * np.pi * ((tt_[:, None] * ff[None, :]) % N) / float(N)
    cosm = np.cos(th)
    sinm = np.sin(th)
    nyq = np.where(tt_ % 2 == 0, 1.0, -1.0)
    sinm_f = sinm.copy()
    sinm_f[:, 0] = nyq
    fwd = np.stack([cosm, sinm_f], axis=1)
    fwd = fwd.reshape(16, 128, 2, 16, 128).transpose(3, 1, 0, 2, 4)
    wf = np.full(L, 2.0)
    wf[0] = 1.0
    icos = (cosm * wf[None, :] / N).T
    isin = (2.0 * sinm / N).T
    isin[0, :] = nyq / N
    inv = np.concatenate([icos.reshape(16, 128, L), isin.reshape(16, 128, L)], 0)
    inv = inv.reshape(32, 128, 16, 128).transpose(2, 1, 0, 3)
    _HY_CACHE.update({"hy_featsT": _c(feats.T), "hy_win": _c(win), "hy_fwd": _c(fwd.reshape(16, 128, -1)),
                      "hy_inv": _c(inv.reshape(16, 128, -1))})
    return _HY_CACHE


def prep_shared(inp, layers, phases):
    sh = {}
    sh["ada_bT"] = _c(inp["ada_b"].reshape(4, 48, 128).transpose(2, 0, 1))
    sh["nw"] = _c(np.stack([inp["norm_mix_w"], inp["norm_ffn_w"]], 0).reshape(2, 4, 8, 128).transpose(3, 1, 0, 2))
    sh["fnw"] = _c(inp["final_norm_w"].reshape(8, 128).T)
    sh["gridT"] = _grid_T()
    for i in layers:
        sh["ada_w%d" % i] = _c(inp["ada_w"][i])
    for (kind, i) in phases:
        if kind == "ffn":
            w1 = inp["ffn_w1"][i].reshape(8, 128, NF, 128).transpose(2, 1, 0, 3)
            w3 = inp["ffn_w3"][i].reshape(8, 128, NF, 128).transpose(2, 1, 0, 3)
            sh["w13_%d" % i] = _c(np.stack([w1, w3], axis=2).reshape(NF, 128, -1))
            sh["w2_%d" % i] = _c(inp["ffn_w2"][i].reshape(NF, 128, 8, 128).transpose(2, 1, 0, 3).reshape(8, 128, -1))
        elif i % 4 == 0:
            j = i // 4
            w = inp["ssd_w_in"][j]
            sh["ss_wi"] = _c(w[:, 2048:6144].reshape(8, 128, 32, 128).transpose(2, 1, 0, 3).reshape(32, 128, -1))
            sh["ss_wz"] = _c(w[:, :2048].reshape(8, 128, 2048).transpose(1, 0, 2).reshape(128, -1))
            sh["ss_wdt"] = _c(w[:, 6144:].reshape(8, 128, 64).transpose(1, 0, 2).reshape(128, -1))
            cvw = inp["ssd_conv_w"][j].reshape(3, 32, 128).transpose(2, 1, 0)
            cvb = inp["ssd_conv_b"][j].reshape(32, 128).T[:, :, None]
            sh["ss_cv"] = _c(np.concatenate([cvw, cvb], 2).reshape(128, -1))
            sh["ss_small"] = _c(np.concatenate([inp["ssd_a_log"][j].reshape(-1), inp["ssd_dt_bias"][j].reshape(-1), inp["ssd_d"][j].reshape(-1)])[None, :])
            sh["ss_nw"] = _c(inp["ssd_norm_w"][j][None, :])
            sh["ss_wo"] = _c(inp["ssd_w_out"][j].reshape(16, 128, D).transpose(1, 0, 2).reshape(128, -1))
        elif i % 4 == 1:
            j = i // 4
            sh["gm_wu"] = _c(inp["gm_w_in"][j][:, :2048].reshape(8, 128, 2048).transpose(1, 0, 2).reshape(128, -1))
            sh["gm_wv"] = _c(inp["gm_w_in"][j][:, 2048:].reshape(8, 128, 2048).transpose(1, 0, 2).reshape(128, -1))
            sh["gm_wo"] = _c(inp["gm_w_out"][j].reshape(16, 128, D).transpose(1, 0, 2).reshape(128, -1))
            sh["gm_wsT"] = _c(inp["gm_w_s"][j].transpose(2, 0, 1).reshape(128, -1))
            sh["gm_bs"] = _c(inp["gm_b_s"][j].reshape(1, -1))
            sh["gm_ln"] = _c(np.stack([inp["gm_ln_w"][j].reshape(16, 128).T, inp["gm_ln_b"][j].reshape(16, 128).T], 1).reshape(128, -1))
        elif i % 4 == 2:
            j = i // 4
            sh.update(_hyena_tables())
            sh["hy_fw0"] = _c(inp["hy_f_w0"][j])
            sh["hy_fw1"] = _c(inp["hy_f_w1"][j])
            sh["hy_fw2"] = _c(inp["hy_f_w2"][j])
            sh["hy_fb"] = _c(np.stack([inp["hy_f_b0"][j], inp["hy_f_b1"][j], inp["hy_sin_freq"][j][0], inp["hy_sin_freq"][j][1]], 1))
            sh["hy_bias"] = _c(inp["hy_bias"][j])
            sh["hy_wi"] = _c(inp["hy_w_in"][j].reshape(8, 128, 24, 128).transpose(2, 1, 0, 3).reshape(24, 128, -1))
            cvw = inp["hy_conv_w"][j].reshape(3, 24, 128).transpose(2, 1, 0)
            cvb = inp["hy_conv_b"][j].reshape(24, 128).T[:, :, None]
            sh["hy_cv"] = _c(np.concatenate([cvw, cvb], 2).reshape(128, -1))
            sh["hy_wo"] = _c(inp["hy_w_out"][j].reshape(8, 128, D).transpose(1, 0, 2).reshape(128, -1))
        elif i % 4 == 3:
            sh["fn_cw"], sh["fn_cl"] = _fnet_tables()
            sh["fn_wo"] = _c(inp["fn_w_out"][i // 4].reshape(8, 128, D).transpose(1, 0, 2).reshape(128, -1))
            sh["fn_bo"] = _c(inp["fn_b_out"][i // 4].reshape(8, 128).T)
    return sh


def prep_core(inp, b, xb=None):
    x = inp["x"][b] if xb is None else xb
    d = {"xT": _c(x.T)}
    d["cc"] = _c(np.stack([inp["c"][b], inp["c_ctx"]], 0).reshape(2, 8, 128).transpose(2, 1, 0))
    d["ctxT"] = _c(inp["ctx"][b].T)
    return d


ALL_PHASES = [(k, i) for i in range(4) for k in ("mix", "ffn")]


def run(inp, phases, cores, add_grid=True, final_norm=True, xs=None, trace=False):
    kb = KB(phases, add_grid, final_norm)
    nc = kb.build()
    sh = prep_shared(inp, kb.layers, phases)
    in_maps = []
    for n, b in enumerate(cores):
        m = dict(sh)
        m.update(prep_core(inp, b, None if xs is None else xs[n]))
        in_maps.append({k: v for k, v in m.items() if k in kb.din})
    res = run_bass_kernel_spmd(nc, in_maps, core_ids=list(range(len(cores))), trace=trace)
    outs = [np.ascontiguousarray(r["yT"].T) for r in res.results]
    return outs, res


def kernel(**inputs):
    inp = {k: np.asarray(v) for k, v in inputs.items()}
    outs, _ = run(inp, ALL_PHASES, list(range(NCORES)))
    return np.stack(outs, 0).astype(np.float32)
```

```python
import contextlib
import math
import os
import numpy as np
import concourse.bass as bass
import concourse.mybir as mybir
from concourse.bass_utils import run_bass_kernel_spmd

F32 = mybir.dt.float32
BF16 = mybir.dt.bfloat16
AF = mybir.ActivationFunctionType
ALU = mybir.AluOpType
AX = mybir.AxisListType

D = 1024
L = 2048
LC = 256
FF = 2816
NF = 22
EPS = 1e-6
NCORES = 8

ENGS = ["sync", "scalar", "tensor", "vector", "gpsimd"]
DMA_K = 8
SAME_ENGINE_SYNC = True


class Res:
    __slots__ = ("last_w", "readers", "dma_readers")

    def __init__(self):
        self.last_w = None
        self.readers = {}
        self.dma_readers = []


class Buf:
    def __init__(self, name, t):
        self.name = name
        self.t = t
        self.res = {}

    def __getitem__(self, key):
        return (self, key)


class PsBuf(Buf):
    def __getitem__(self, key):
        return (self, None)


def RK(buf, keys):
    return [(buf, k) for k in keys]


class Prog:
    def __init__(self, nc):
        self.nc = nc
        self.stream = {e: [] for e in ENGS}
        self.ops = []
        self.ccount = {e: 0 for e in ENGS}
        self.dcount = {e: 0 for e in ENGS}
        self.seen = {e: {} for e in ENGS}
        self.pending_noinc = {e: [] for e in ENGS}
        self.semkeys = set()

    def _collect(self, reads, writes):
        deps = set()
        self._war = set()
        for (buf, key) in reads:
            keys = list(buf.res.keys()) if key is None else [key, None]
            for k in keys:
                r = buf.res.get(k)
                if r is not None and r.last_w is not None:
                    deps.add(r.last_w)
        for (buf, key) in writes:
            keys = list(buf.res.keys()) if key is None else [key, None]
            for k in keys:
                r = buf.res.get(k)
                if r is None:
                    continue
                if r.last_w is not None:
                    deps.add(r.last_w)
                self._war.update(r.readers.values())
                self._war.update(r.dma_readers)
        self._war -= deps
        return deps | self._war

    def _mark(self, opid, eng, is_dma, reads, writes):
        for (buf, key) in reads:
            r = buf.res.get(key)
            if r is None:
                r = buf.res[key] = Res()
            if is_dma:
                r.dma_readers.append(opid)
            else:
                r.readers[eng] = opid
        for (buf, key) in writes:
            if key is None:
                buf.res = {}
            r = buf.res.get(key)
            if r is None:
                r = buf.res[key] = Res()
            r.last_w = opid
            r.readers = {}
            r.dma_readers = []

    def _emit_waits(self, eng, deps):
        for d in sorted(deps):
            deng, kind, semkey, val = self.ops[d]
            if kind == "c" and deng == eng:
                if eng == "tensor" or not SAME_ENGINE_SYNC or d in self._war:
                    continue
            if val is None:
                raise RuntimeError("dependency on op without completion signal")
            if self.seen[eng].get(semkey, 0) >= val:
                continue
            self.seen[eng][semkey] = val
            self.stream[eng].append(("wait", semkey, val))

    def op(self, eng, fn, reads=(), writes=(), inc=True):
        deps = self._collect(reads, writes)
        self._emit_waits(eng, deps)
        opid = len(self.ops)
        if inc:
            self.ccount[eng] += 1
            semkey = "c_" + eng
            self.semkeys.add(semkey)
            val = self.ccount[eng]
            self.ops.append((eng, "c", semkey, val))
            for pid in self.pending_noinc[eng]:
                self.ops[pid] = (eng, "c", semkey, val)
            self.pending_noinc[eng] = []
            self.stream[eng].append(("op", fn, semkey, 1))
        else:
            self.ops.append((eng, "c", "c_" + eng, None))
            self.pending_noinc[eng].append(opid)
            self.stream[eng].append(("op", fn, None, 0))
        self._mark(opid, eng, False, reads, writes)
        return opid

    def dma(self, eng, out, in_, reads=(), writes=(), **kw):
        deps = self._collect(reads, writes)
        i = self.dcount[eng]
        self.dcount[eng] += 1
        slot = i % DMA_K
        semkey = "d_%s_%d" % (eng, slot)
        self.semkeys.add(semkey)
        prev_val = 16 * (i // DMA_K)
        val = prev_val + 16
        self._emit_waits(eng, deps)
        if prev_val > 0 and self.seen[eng].get(semkey, 0) < prev_val:
            self.seen[eng][semkey] = prev_val
            self.stream[eng].append(("wait", semkey, prev_val))
        opid = len(self.ops)
        self.ops.append((eng, "d", semkey, val))
        fn = lambda e, out=out, in_=in_, kw=kw: e.dma_start(out=out, in_=in_, **kw)
        self.stream[eng].append(("op", fn, semkey, 16))
        self._mark(opid, eng, True, reads, writes)
        return opid

    def barrier(self, engines=None):
        for e in ENGS:
            if self.pending_noinc[e]:
                raise RuntimeError("barrier with trailing no-inc ops on " + e)
        for e in (engines or ENGS):
            for e2 in ENGS:
                n = self.dcount[e2]
                for slot in range(min(n, DMA_K)):
                    cnt = (n - 1 - slot) // DMA_K + 1
                    semkey = "d_%s_%d" % (e2, slot)
                    if self.seen[e].get(semkey, 0) < 16 * cnt:
                        self.seen[e][semkey] = 16 * cnt
                        self.stream[e].append(("wait", semkey, 16 * cnt))
                if self.ccount[e2]:
                    semkey = "c_" + e2
                    if self.seen[e].get(semkey, 0) < self.ccount[e2]:
                        self.seen[e][semkey] = self.ccount[e2]
                        self.stream[e].append(("wait", semkey, self.ccount[e2]))

    def finish(self, final_eng="sync"):
        self.barrier([final_eng])

    def run_block(self):
        nc = self.nc
        with contextlib.ExitStack() as st:
            sems = {}
            for k in sorted(self.semkeys):
                sems[k] = st.enter_context(nc.semaphore(k))
            block = st.enter_context(nc.Block())

            def mk(engname):
                def body(e):
                    for act in self.stream[engname]:
                        if act[0] == "wait":
                            e.wait_ge(sems[act[1]], act[2])
                        else:
                            ins = act[1](e)
                            if act[2] is not None:
                                ins.then_inc(sems[act[2]], act[3])
                return body

            for engname in ENGS:
                if self.stream[engname]:
                    getattr(block, engname)(mk(engname))


class Arena:
    def __init__(self, name, ap, nwords):
        self.name = name
        self.ap = ap
        self.nwords = nwords
        self.off = 0
        self.n = 0

    def reset(self):
        self.off = 0

    def alloc(self, name, free_shape, dt):
        free_shape = tuple(free_shape)
        nel = int(np.prod(free_shape))
        words = nel if dt == F32 else (nel + 1) // 2
        words = (words + 7) // 8 * 8
        if self.off + words > self.nwords:
            raise RuntimeError("arena %s overflow: %s needs %d words at %d / %d" % (self.name, name, words, self.off, self.nwords))
        v = self.ap[:, self.off:self.off + words]
        if dt != F32:
            v = v.bitcast(dt)
        v = v[:, 0:nel]
        if len(free_shape) == 2:
            v = v.rearrange("p (a b) -> p a b", a=free_shape[0])
        elif len(free_shape) == 3:
            v = v.rearrange("p (a b c) -> p a b c", a=free_shape[0], b=free_shape[1])
        elif len(free_shape) == 4:
            v = v.rearrange("p (a b c d) -> p a b c d", a=free_shape[0], b=free_shape[1], c=free_shape[2])
        self.off += words
        self.n += 1
        return Buf("%s.%s.%d" % (self.name, name, self.n), v)


ARENA_WORDS = 26432


class KB:
    def __init__(self, phases, add_grid=True, final_norm=True):
        self.phases = phases
        self.add_grid = add_grid
        self.final_norm = final_norm
        self.nc = bass.Bass("TRN2", target_bir_lowering=False)
        self.P = Prog(self.nc)
        self.din = {}
        self.st = contextlib.ExitStack()
        self.layers = sorted(set(i for (_, i) in phases))

    def inp(self, name, shape):
        if name not in self.din:
            self.din[name] = self.nc.dram_tensor(name, list(shape), F32, kind="ExternalInput").ap()
        return self.din[name]

    def scratch(self, name, shape, dt):
        return Buf(name, self.nc.dram_tensor(name, list(shape), dt, kind="Internal").ap())

    def sb(self, name, shape, dt):
        return Buf(name, self.st.enter_context(self.nc.sbuf_tensor("s_" + name, list(shape), dt)))

    def mm(self, out, lhsT, rhs, start, stop, reads, writes, inc=True):
        self.P.op("tensor", lambda e: e.matmul(out, lhsT=lhsT, rhs=rhs, start=start, stop=stop),
                  reads=reads, writes=writes, inc=inc)

    def act(self, out, in_, func, reads, writes, bias=None, scale=None):
        kw = {}
        if bias is not None:
            kw["bias"] = bias
        if scale is not None:
            kw["scale"] = scale
        self.P.op("scalar", lambda e: e.activation(out=out, in_=in_, func=func, **kw), reads=reads, writes=writes)

    def tt(self, eng, out, in0, in1, op, reads, writes):
        self.P.op(eng, lambda e: e.tensor_tensor(out=out, in0=in0, in1=in1, op=op), reads=reads, writes=writes)

    def ts(self, eng, out, in0, s1, s2, op0, op1, reads, writes):
        self.P.op(eng, lambda e: e.tensor_scalar(out=out, in0=in0, scalar1=s1, scalar2=s2, op0=op0, op1=op1),
                  reads=reads, writes=writes)

    def stt(self, eng, out, in0, scalar, in1, op0, op1, reads, writes):
        self.P.op(eng, lambda e: e.scalar_tensor_tensor(out=out, in0=in0, scalar=scalar, in1=in1, op0=op0, op1=op1),
                  reads=reads, writes=writes)

    def cp(self, eng, out, in_, reads, writes):
        if eng == "scalar":
            self.P.op(eng, lambda e: e.copy(out=out, in_=in_), reads=reads, writes=writes)
        else:
            self.P.op(eng, lambda e: e.tensor_copy(out=out, in_=in_), reads=reads, writes=writes)

    def new_epoch(self, use_hn=False):
        self.P.barrier()
        self.ar.reset()
        self.ar2.reset()

    def build(self):
        nc, P = self.nc, self.P
        with self.st:
            self.h = self.sb("h", [128, 8, L], F32)
            self.hn = self.sb("hn", [128, 8, L], BF16)
            arena_t = self.st.enter_context(nc.sbuf_tensor("arena", [128, ARENA_WORDS], F32))
            self.ar = Arena("ar", arena_t[:], ARENA_WORDS)
            self.ar2 = Arena("ar2", self.hn.t[:].rearrange("p a b -> p (a b)").bitcast(F32), 8 * L // 2)
            self.ident = self.sb("ident", [128, 128], BF16)
            self.ones_bf = self.sb("ones_bf", [128, 128], BF16)
            self.ones_f = self.sb("ones_f", [128, 128], F32)
            self.modb = self.sb("modb", [128, 4, 48, 2], F32)
            self.adab = self.sb("adab", [128, 4, 48], F32)
            self.nw = self.sb("nw", [128, 4, 2, 8], F32)
            self.fnw = self.sb("fnw", [128, 8], F32)
            self.lv = self.sb("lv", [128, 4, 4, 8], F32)
            self.ccs = self.sb("ccs", [128, 8, 2], F32)
            self.cs = self.sb("cs", [128, 8, 2], BF16)
            self.adw = [self.sb("adw%d" % n, [128, 8, 128], BF16) for n in range(2)]
            self.ps = [PsBuf("ps%d" % i, self.st.enter_context(nc.psum_tensor("ps%d" % i, [128, 512], F32))) for i in range(8)]

            P.op("gpsimd", lambda e: e.memset(self.ident.t[:], 1.0), writes=[self.ident[None]])
            P.op("gpsimd", lambda e: e.affine_select(out=self.ident.t[:], in_=self.ident.t[:], pattern=[[-1, 128]],
                                                     compare_op=ALU.is_equal, fill=0.0, base=0, channel_multiplier=1),
                 reads=[self.ident[None]], writes=[self.ident[None]])
            P.op("vector", lambda e: e.memset(self.ones_bf.t[:], 1.0), writes=[self.ones_bf[None]])
            P.op("vector", lambda e: e.memset(self.ones_f.t[:], 1.0), writes=[self.ones_f[None]])
            P.dma("sync", self.adab.t[:], self.inp("ada_bT", [128, 4, 48]), writes=[self.adab[None]])
            P.dma("sync", self.nw.t[:], self.inp("nw", [128, 4, 2, 8]), writes=[self.nw[None]])
            P.dma("sync", self.fnw.t[:], self.inp("fnw", [128, 8]), writes=[self.fnw[None]])
            P.dma("sync", self.ccs.t[:], self.inp("cc", [128, 8, 2]), writes=[self.ccs[None]])
            self.act(self.cs.t[:], self.ccs.t[:], AF.Silu, [self.ccs[None]], [self.cs[None]])

            xT = self.inp("xT", [D, L])
            for c in range(8):
                P.dma("sync", self.h.t[:, c, :], xT[c * 128:(c + 1) * 128, :], writes=RK(self.h, [(c, t) for t in range(4)]))
            if self.add_grid:
                gT = self.inp("gridT", [D, L])
                gb = [self.ar.alloc("grid%d" % n, (L,), F32) for n in range(2)]
                for c in range(8):
                    g = gb[c % 2]
                    hk = RK(self.h, [(c, t) for t in range(4)])
                    P.dma("sync", g.t[:], gT[c * 128:(c + 1) * 128, :], writes=[g[None]])
                    self.tt("vector", self.h.t[:, c, :], self.h.t[:, c, :], g.t[:], ALU.add, hk + [g[None]], hk)

            inter = set()
            for n, (kind, i) in enumerate(self.phases):
                if kind == "ffn" and any(i2 == i + 1 for (_, i2) in self.phases[n + 1:]):
                    inter.add(i + 1)
            first_ada = True
            for i in self.layers:
                if i not in inter:
                    if not first_ada:
                        self.new_epoch()
                    first_ada = False
                    wide = [self.ar.alloc("adawide%d" % n, (8, 768), BF16) for n in range(2)]
                    for _ in self.adaln_gen(i, self.ps[0], wide, 6):
                        pass
            for (kind, i) in self.phases:
                self.new_epoch()
                if kind == "ffn":
                    self.ffn(i, self.adaln_gen(i + 1, self.ps[7]) if (i + 1) in inter else None)
                else:
                    getattr(self, ["mix_ssd", "mix_gmlp", "mix_hyena", "mix_fnet"][i % 4])(i)
            self.new_epoch()
            self.final()
            P.finish("sync")
            P.run_block()
        return self.nc

    def adaln_gen(self, i, psA, wbs=None, nper=1):
        P = self.P
        aw = self.inp("ada_w%d" % i, [D, 6 * D])
        wbs = wbs or self.adw
        for piece in range(48 // nper):
            wb = wbs[piece % 2]
            wcol = nper * 128
            P.dma("gpsimd", wb.t[:], aw[:, piece * wcol:(piece + 1) * wcol].rearrange("(k p) n -> p k n", p=128), writes=[wb[None]])
            for jn in range(nper):
                j = piece * nper + jn
                for k in range(8):
                    self.mm(psA.t[:, 2 * j:2 * j + 2], wb.t[:, k, jn * 128:(jn + 1) * 128], self.cs.t[:, k, :], k == 0, k == 7,
                            [wb[None], self.cs[None]], [psA[None]], inc=(k == 7))
                yield
        self.tt("vector", self.modb.t[:, i, :, :], psA.t[:, 0:96].rearrange("p (j c) -> p j c", c=2),
                self.adab.t[:, i, :].rearrange("p (j o) -> p j o", o=1).to_broadcast([128, 48, 2]), ALU.add,
                [psA[None], self.adab[None]], [self.modb[i]])
        for (slot, which, col, nwi) in ((0, 1, 0, 0), (1, 4, 0, 1), (2, 1, 1, 0)):
            self.stt("vector", self.lv.t[:, i, slot, :], self.modb.t[:, i, which * 8:(which + 1) * 8, col], 1.0,
                     self.nw.t[:, i, nwi, :], ALU.add, ALU.mult, [self.modb[i], self.nw[None]], [self.lv[(i, slot)]])
        yield

    def modv(self, i, which, c, col=0):
        return self.modb.t[:, i, which * 8 + c, col:col + 1]

    def norm_mod(self, src, ntok, Afn, Bfn, dst, extra_reads, ar):
        P = self.P
        TW = min(512, ntok)
        sqb = [ar.alloc("nsq%d" % n, (TW,), BF16) for n in range(2)]
        rsb = [ar.alloc("nrs%d" % n, (TW,), F32) for n in range(2)]
        tmb = [ar.alloc("ntm%d" % n, (TW,), F32) for n in range(2)]
        for t in range(ntok // TW):
            sl = slice(t * TW, (t + 1) * TW)
            pss = self.ps[6 + t % 2]
            rs = rsb[t % 2]
            for c in range(8):
                sq = sqb[c % 2]
                self.tt("gpsimd", sq.t[:], src.t[:, c, sl], src.t[:, c, sl], ALU.mult, [src[(c, t)]], [sq[None]])
                self.mm(pss.t[:, :TW], self.ones_bf.t[:], sq.t[:], c == 0, c == 7, [sq[None], self.ones_bf[None]], [pss[None]])
            self.act(rs.t[:], pss.t[:, :TW], AF.Sqrt, [pss[None]], [rs[None]], bias=EPS, scale=1.0 / D)
            P.op("vector", lambda e, rs=rs: e.reciprocal(out=rs.t[:], in_=rs.t[:]), reads=[rs[None]], writes=[rs[None]])
            for c in range(8):
                tm = tmb[c % 2]
                self.tt("vector", tm.t[:], src.t[:, c, sl], rs.t[:], ALU.mult, [src[(c, t)], rs[None]], [tm[None]])
                self.act(dst.t[:, c, sl], tm.t[:], AF.Identity, [tm[None]] + extra_reads, [dst[(c, t)]],
                         bias=Bfn(c), scale=Afn(c))

    def ffn(self, i, side=None):
        P, ar = self.P, self.ar

        def step():
            if side is not None:
                next(side, None)

        w13d = self.inp("w13_%d" % i, [NF, 128, 2 * 8 * 128])
        w2d = self.inp("w2_%d" % i, [8, 128, NF * 128])
        w13 = [ar.alloc("w13_%d" % n, (2, 8, 128), BF16) for n in range(3)]
        for j in range(3):
            P.dma("gpsimd", w13[j].t[:], w13d[j].rearrange("p (a k f) -> p a k f", a=2, k=8), writes=[w13[j][None]])
        self.norm_mod(self.h, L, lambda c: self.lv.t[:, i, 1, c:c + 1], lambda c: self.modv(i, 3, c), self.hn,
                      [self.lv[(i, 1)], self.modb[i]], ar)
        a = ar.alloc("a", (NF, 1024), BF16)
        w2b = [ar.alloc("w2_%d" % n, (NF, 128), BF16) for n in range(2)]
        stb = [ar.alloc("st%d" % n, (512,), BF16) for n in range(2)]
        n13 = 0
        for half in range(2):
            for j in range(NF):
                wb = w13[j % 3]
                if not (half == 0 and j < 3):
                    P.dma("gpsimd", wb.t[:], w13d[j].rearrange("p (a k f) -> p a k f", a=2, k=8), writes=[wb[None]])
                for tt_ in range(2):
                    T = half * 2 + tt_
                    sl = slice(T * 512, (T + 1) * 512)
                    p1 = self.ps[(n13 % 2) * 2]
                    p3 = self.ps[(n13 % 2) * 2 + 1]
                    stt_ = stb[n13 % 2]
                    n13 += 1
                    hr = RK(self.hn, [(k, T) for k in range(8)])
                    for k in range(8):
                        self.mm(p1.t[:], wb.t[:, 0, k, :], self.hn.t[:, k, sl], k == 0, k == 7, [wb[None], self.hn[(k, T)]], [p1[None]], inc=(k == 7))
                    for k in range(8):
                        self.mm(p3.t[:], wb.t[:, 1, k, :], self.hn.t[:, k, sl], k == 0, k == 7, [wb[None], self.hn[(k, T)]], [p3[None]], inc=(k == 7))
                    self.act(stt_.t[:], p1.t[:], AF.Silu, [p1[None]], [stt_[None]])
                    self.tt("vector", a.t[:, j, tt_ * 512:(tt_ + 1) * 512], stt_.t[:], p3.t[:], ALU.mult,
                            [stt_[None], p3[None]], [a[(j, tt_)]])
                step()
            for dc in range(8):
                w2 = w2b[dc % 2]
                P.dma("gpsimd", w2.t[:], w2d[dc].rearrange("p (j d) -> p j d", j=NF), writes=[w2[None]])
                for tt_ in range(2):
                    T = half * 2 + tt_
                    sl = slice(T * 512, (T + 1) * 512)
                    po = self.ps[4 + (dc * 2 + tt_) % 2]
                    for j in range(NF):
                        self.mm(po.t[:], w2.t[:, j, :], a.t[:, j, tt_ * 512:(tt_ + 1) * 512], j == 0, j == NF - 1,
                                [w2[None], a[(j, tt_)]], [po[None]], inc=(j == NF - 1))
                    self.stt("vector", self.h.t[:, dc, sl], po.t[:], self.modv(i, 5, dc), self.h.t[:, dc, sl], ALU.mult, ALU.add,
                             [po[None], self.modb[i], self.h[(dc, T)]], [self.h[(dc, T)]])
                step()
        if side is not None:
            for _ in side:
                pass

    def final(self):
        P, ar = self.P, self.ar
        yT = self.nc.dram_tensor("yT", [D, L], F32, kind="ExternalOutput").ap()
        ob = [ar.alloc("ob%d" % n, (512,), F32) for n in range(3)]
        if not self.final_norm:
            for c in range(8):
                P.dma("sync", yT[c * 128:(c + 1) * 128, :], self.h.t[:, c, :], reads=RK(self.h, [(c, t) for t in range(4)]))
            return
        sqb = [ar.alloc("fsq%d" % n, (512,), BF16) for n in range(2)]
        rsb = [ar.alloc("frs%d" % n, (512,), F32) for n in range(2)]
        n = 0
        for t in range(4):
            sl = slice(t * 512, (t + 1) * 512)
            pss = self.ps[6 + t % 2]
            rs = rsb[t % 2]
            for c in range(8):
                sq = sqb[c % 2]
                self.tt("gpsimd", sq.t[:], self.h.t[:, c, sl], self.h.t[:, c, sl], ALU.mult, [self.h[(c, t)]], [sq[None]])
                self.mm(pss.t[:], self.ones_bf.t[:], sq.t[:], c == 0, c == 7, [sq[None], self.ones_bf[None]], [pss[None]])
            self.act(rs.t[:], pss.t[:], AF.Sqrt, [pss[None]], [rs[None]], bias=EPS, scale=1.0 / D)
            P.op("vector", lambda e, rs=rs: e.reciprocal(out=rs.t[:], in_=rs.t[:]), reads=[rs[None]], writes=[rs[None]])
            for c in range(8):
                o = ob[n % 3]
                n += 1
                self.stt("vector", o.t[:], self.h.t[:, c, sl], self.fnw.t[:, c:c + 1], rs.t[:], ALU.mult, ALU.mult,
                         [self.h[(c, t)], self.fnw[None], rs[None]], [o[None]])
                P.dma("sync", yT[c * 128:(c + 1) * 128, sl], o.t[:], reads=[o[None]])

    def mix_fnet(self, i):
        P, ar, ar2 = self.P, self.ar, self.ar2
        j = i // 4
        self.norm_phase(i, 0, 0)
        cwd = self.inp("fn_cw", [128, 2 * 2 * 256])
        cld = self.inp("fn_cl", [4, 128, 16 * 2 * 512])
        wod = self.inp("fn_wo", [128, 8 * D])
        bod = self.inp("fn_bo", [128, 8])
        A = ar.alloc("fnA", (16, 2, D), BF16)
        cw = ar.alloc("fncw", (2, 512), BF16)
        wo = ar.alloc("fnwo", (8, D), BF16)
        bo = ar.alloc("fnbo", (8,), F32)
        bg = ar.alloc("fnbg", (8,), F32)
        fT = ar.alloc("fnfT", (8, 512), BF16)
        tmb = [ar.alloc("fntm%d" % n, (512,), F32) for n in range(2)]
        P.dma("gpsimd", cw.t[:], cwd.rearrange("p (k n) -> p k n", k=2), writes=[cw[None]])
        P.dma("gpsimd", wo.t[:], wod.rearrange("p (k n) -> p k n", k=8), writes=[wo[None]])
        P.dma("sync", bo.t[:], bod, writes=[bo[None]])
        self.tt("vector", bg.t[:], bo.t[:], self.modb.t[:, i, 16:24, 0], ALU.mult, [bo[None], self.modb[i]], [bg[None]])
        n = 0
        for tc in range(16):
            T = tc // 4
            for g in range(4):
                pa = self.ps[n % 2]
                n += 1
                for kk in range(2):
                    k = 2 * g + kk
                    self.mm(pa.t[:], self.hn.t[:, k, tc * 128:(tc + 1) * 128], cw.t[:, kk, :], kk == 0, kk == 1,
                            [self.hn[(k, T)], cw[None]], [pa[None]], inc=(kk == 1))
                self.cp("scalar", A.t[:, tc, :, g * 256:(g + 1) * 256], pa.t[:].rearrange("p (a b) -> p a b", a=2),
                        [pa[None]], [A[(tc, g)]])
        P.barrier()
        clb = ar2.alloc("fncl", (16, 2, 512), BF16)
        n = 0
        for T in range(4):
            sl = slice(T * 512, (T + 1) * 512)
            P.dma("gpsimd", clb.t[:], cld[T].rearrange("p (a b c) -> p a b c", a=16, b=2), writes=[clb[None]])
            for dc in range(8):
                pf = self.ps[2 + dc % 2]
                g = dc // 2
                for tc in range(16):
                    for cs_ in range(2):
                        self.mm(pf.t[:], A.t[:, tc, cs_, dc * 128:(dc + 1) * 128], clb.t[:, tc, cs_, :],
                                tc == 0 and cs_ == 0, tc == 15 and cs_ == 1, [A[(tc, g)], clb[None]], [pf[None]],
                                inc=(tc == 15 and cs_ == 1))
                self.cp("scalar", fT.t[:, dc, :], pf.t[:], [pf[None]], [fT[dc]])
            for dc in range(8):
                po = self.ps[4 + dc % 2]
                tm = tmb[dc % 2]
                for k in range(8):
                    self.mm(po.t[:], wo.t[:, k, dc * 128:(dc + 1) * 128], fT.t[:, k, :], k == 0, k == 7,
                            [wo[None], fT[k]], [po[None]], inc=(k == 7))
                self.act(tm.t[:], po.t[:], AF.Identity, [po[None], bg[None], self.modb[i]], [tm[None]],
                         bias=bg.t[:, dc:dc + 1], scale=self.modv(i, 2, dc))
                self.tt("vector", self.h.t[:, dc, sl], self.h.t[:, dc, sl], tm.t[:], ALU.add,
                        [self.h[(dc, T)], tm[None]], [self.h[(dc, T)]])

    def norm_phase(self, i, slot_A, which_B):
        self.norm_mod(self.h, L, lambda c: self.lv.t[:, i, slot_A, c:c + 1], lambda c: self.modv(i, which_B, c), self.hn,
                      [self.lv[(i, slot_A)], self.modb[i]], self.ar)
        self.P.barrier()
        self.ar.reset()

    def mix_gmlp(self, i):
        P, ar = self.P, self.ar
        self.norm_phase(i, 0, 0)
        wud = self.inp("gm_wu", [128, 8 * 2048]).rearrange("p (k n) -> p k n", k=8)
        wvd = self.inp("gm_wv", [128, 8 * 2048]).rearrange("p (k n) -> p k n", k=8)
        wod = self.inp("gm_wo", [128, 16 * D]).rearrange("p (k n) -> p k n", k=16)
        wsd = self.inp("gm_wsT", [128, 8 * 128])
        bsd = self.inp("gm_bs", [1, 8 * 128])
        lnd = self.inp("gm_ln", [128, 2 * 16])
        wv = ar.alloc("gwv", (8, 2048), BF16)
        v32 = ar.alloc("gv32", (2048,), F32)
        vln = ar.alloc("gvln", (2048,), BF16)
        wsT = ar.alloc("gwsT", (8, 128), BF16)
        extra = ar.alloc("gextra", (16, 128), F32)
        sT = ar.alloc("gsT", (16, 512), BF16)
        wub = [ar.alloc("gwu%d" % n, (8, 128), BF16) for n in range(3)]
        ugb = [ar.alloc("gug%d" % n, (512,), BF16) for n in range(2)]
        prod = ar.alloc("gprod", (16, 512), BF16)
        wob = [ar.alloc("gwo%d" % n, (16, 128), BF16) for n in range(2)]
        ln = ar.alloc("gln", (2, 16), F32)
        stats = ar.alloc("gstats", (4, 6), F32)
        mv = ar.alloc("gmv", (4,), F32)
        for q in range(4):
            P.dma("gpsimd", wv.t[:, :, q * 512:(q + 1) * 512], wvd[:, :, q * 512:(q + 1) * 512], writes=[wv[q]])
        P.dma("gpsimd", wsT.t[:], wsd.rearrange("p (g q) -> p g q", g=8), writes=[wsT[None]])
        P.dma("sync", ln.t[:], lnd.rearrange("p (a b) -> p a b", a=2), writes=[ln[None]])
        rs_bc = v32.t[:, 0:1024]
        bs_bc = v32.t[:, 1024:2048]
        P.dma("sync", bs_bc, bsd[0:1, :].to_broadcast([128, 1024]), writes=[v32[None]])
        for hh in range(2):
            pr = self.ps[hh]
            self.mm(pr.t[:], self.ones_bf.t[:], wsT.t[:, hh * 4:(hh + 1) * 4, :].rearrange("p a b -> p (a b)"), True, True,
                    [self.ones_bf[None], wsT[None]], [pr[None]])
            self.cp("vector", rs_bc[:, hh * 512:(hh + 1) * 512], pr.t[:], [pr[None]], [v32[None]])
        for dcx in range(16):
            g = dcx // 2
            self.stt("vector", extra.t[:, dcx, :], rs_bc[:, g * 128:(g + 1) * 128], ln.t[:, 1, dcx:dcx + 1],
                     bs_bc[:, g * 128:(g + 1) * 128], ALU.mult, ALU.add, [v32[None], ln[None]], [extra[dcx]])
        nwu = 0
        STOP = int(os.environ.get('GM_STOP', '99'))
        if STOP <= 1:
            return
        for T in range(4):
            sl = slice(T * 512, (T + 1) * 512)
            for tcc in range(4):
                tc = T * 4 + tcc
                tsl = slice(tc * 128, (tc + 1) * 128)
                for ct in range(4):
                    pv = self.ps[ct % 2]
                    for k in range(8):
                        self.mm(pv.t[:], self.hn.t[:, k, tsl], wv.t[:, k, ct * 512:(ct + 1) * 512], k == 0, k == 7,
                                [self.hn[(k, T)], wv[ct]], [pv[None]], inc=(k == 7))
                    self.act(v32.t[:, ct * 512:(ct + 1) * 512], pv.t[:], AF.Gelu, [pv[None]], [v32[ct]])
                    P.op("vector", lambda e, ct=ct: e.bn_stats(out=stats.t[:, ct, :], in_=v32.t[:, ct * 512:(ct + 1) * 512]),
                         reads=[v32[ct]], writes=[stats[ct]])
                P.op("vector", lambda e: e.bn_aggr(out=mv.t[:, 0:2], in_=stats.t[:].rearrange("p a b -> p (a b)")),
                     reads=[stats[None]], writes=[mv[None]])
                self.act(mv.t[:, 2:3], mv.t[:, 1:2], AF.Sqrt, [mv[None]], [mv[None]], bias=EPS, scale=1.0)
                P.op("vector", lambda e: e.reciprocal(out=mv.t[:, 3:4], in_=mv.t[:, 2:3]), reads=[mv[None]], writes=[mv[None]])
                self.ts("vector", vln.t[:], v32.t[:], mv.t[:, 0:1], mv.t[:, 3:4], ALU.subtract, ALU.mult,
                        [v32[None], mv[None]], [vln[None]])
                if STOP <= 2:
                    return
                for q4 in range(4):
                    pS = self.ps[2 + q4 % 2]
                    for dd in range(4):
                        dcx = q4 * 4 + dd
                        self.mm(pS.t[:, dd * 128:(dd + 1) * 128], vln.t[:, dcx * 128:(dcx + 1) * 128], wsT.t[:, dcx // 2, :], True, True,
                                [vln[None], wsT[None]], [pS[dd]])
                    for dd in range(4):
                        dcx = q4 * 4 + dd
                        self.stt("vector", sT.t[:, dcx, tcc * 128:(tcc + 1) * 128], pS.t[:, dd * 128:(dd + 1) * 128],
                                 ln.t[:, 0, dcx:dcx + 1], extra.t[:, dcx, :], ALU.mult, ALU.add,
                                 [pS[dd], ln[None], extra[dcx]], [sT[(dcx, tcc)]])
                if STOP == 25:
                    return
            if STOP <= 3:
                return
            for fc in range(16):
                wu = wub[nwu % 3]
                ug = ugb[nwu % 2]
                nwu += 1
                P.dma("gpsimd", wu.t[:], wud[:, :, fc * 128:(fc + 1) * 128], writes=[wu[None]])
                pu = self.ps[4 + fc % 2]
                for k in range(8):
                    self.mm(pu.t[:], wu.t[:, k, :], self.hn.t[:, k, sl], k == 0, k == 7, [wu[None], self.hn[(k, T)]], [pu[None]], inc=(k == 7))
                self.act(ug.t[:], pu.t[:], AF.Gelu, [pu[None]], [ug[None]])
                self.tt("vector", prod.t[:, fc, :], ug.t[:], sT.t[:, fc, :], ALU.mult,
                        [ug[None]] + RK(sT, [(fc, q) for q in range(4)]), [prod[fc]])
            if STOP <= 4:
                return
            for dc in range(8):
                wo = wob[dc % 2]
                P.dma("gpsimd", wo.t[:], wod[:, :, dc * 128:(dc + 1) * 128], writes=[wo[None]])
                po = self.ps[6 + dc % 2]
                for fc in range(16):
                    self.mm(po.t[:], wo.t[:, fc, :], prod.t[:, fc, :], fc == 0, fc == 15, [wo[None], prod[fc]], [po[None]], inc=(fc == 15))
                self.stt("vector", self.h.t[:, dc, sl], po.t[:], self.modv(i, 2, dc), self.h.t[:, dc, sl], ALU.mult, ALU.add,
                         [po[None], self.modb[i], self.h[(dc, T)]], [self.h[(dc, T)]])

    def sin_mlp(self, out, ps_in, b_ap, f_ap, tmps, reads, writes):
        P = self.P
        arg, m, ki = tmps
        I32 = mybir.dt.int32
        kiv = ki.t[:].bitcast(I32)
        self.ts("vector", arg.t[:], ps_in, b_ap, f_ap, ALU.add, ALU.mult, reads, [arg[None]])
        self.ts("vector", m.t[:], arg.t[:], 1.0 / (2 * math.pi), 64.0, ALU.mult, ALU.add, [arg[None]], [m[None]])
        self.cp("vector", kiv, m.t[:], [m[None]], [ki[None]])
        self.cp("vector", m.t[:], kiv, [ki[None]], [m[None]])
        self.ts("vector", m.t[:], m.t[:], -64.0, -2 * math.pi, ALU.add, ALU.mult, [m[None]], [m[None]])
        self.tt("vector", arg.t[:], m.t[:], arg.t[:], ALU.add, [m[None], arg[None]], [arg[None]])
        self.ts("vector", m.t[:], arg.t[:], math.pi, -2 * math.pi, ALU.is_gt, ALU.mult, [arg[None]], [m[None]])
        self.tt("vector", arg.t[:], m.t[:], arg.t[:], ALU.add, [m[None], arg[None]], [arg[None]])
        self.act(out, arg.t[:], AF.Sin, [arg[None]], writes)

    def dft_fwd(self, src, ftab, cs_, banks, extra_reads=()):
        for dh in range(2):
            pb = banks[dh]
            for tc in range(16):
                self.mm(pb.t[:], ftab.t[:, tc, cs_, :], src.t[:, tc, dh * 512:(dh + 1) * 512], tc == 0, tc == 15,
                        [ftab[None], src[tc]] + list(extra_reads), [pb[None]], inc=(tc == 15))

    def mix_hyena(self, i):
        P, ar, ar2 = self.P, self.ar, self.ar2
        j = i // 4
        featd = self.inp("hy_featsT", [33, L])
        w0d = self.inp("hy_fw0", [33, 64])
        w1d = self.inp("hy_fw1", [64, 64])
        w2d = self.inp("hy_fw2", [64, 4 * D])
        fbd = self.inp("hy_fb", [64, 4])
        wind = self.inp("hy_win", [L, D])
        fwdd = self.inp("hy_fwd", [16, 128, 16 * 2 * 128])
        invd = self.inp("hy_inv", [16, 128, 32 * 128])
        biasd = self.inp("hy_bias", [2, D])
        wid = self.inp("hy_wi", [24, 128, 8 * 128])
        cvd = self.inp("hy_cv", [128, 24 * 4])
        wod = self.inp("hy_wo", [128, 8 * D])
        Gd = self.scratch("hy_G", [2, 16, 128, 3 * D], F32)
        xd = [self.scratch("hy_x%d" % o, [L, D], BF16) for o in range(2)]

        hid2T = ar.alloc("hid2T", (L,), BF16)
        w2 = ar.alloc("hw2", (4 * D,), BF16)
        acc = ar.alloc("hacc", (4 * D,), F32)
        hp = ar.alloc("hhp", (16, D), BF16)
        hm = ar.alloc("hhm", (16, D), BF16)
        tmps = [ar.alloc("htmp%d" % n, (512,), F32) for n in range(4)]
        featsT = ar2.alloc("featsT", (L,), F32)
        hid1T = ar2.alloc("hid1T", (L,), F32)
        hid2f = ar2.alloc("hid2f", (L,), F32)
        w0 = ar2.alloc("hw0", (64,), F32)
        w1 = ar2.alloc("hw1", (64,), F32)
        fb = ar2.alloc("hfb", (4,), F32)
        st_ = [ar2.alloc("hst%d" % n, (512,), F32) for n in range(3)]
        st64 = [Buf("hst64_%d" % n, b.t[0:64, :]) for n, b in enumerate(st_)]
        P.dma("sync", featsT.t[0:33, :], featd, writes=[featsT[None]])
        P.dma("sync", w0.t[0:33, :], w0d, writes=[w0[None]])
        P.dma("sync", w1.t[0:64, :], w1d, writes=[w1[None]])
        P.dma("sync", fb.t[0:64, :], fbd, writes=[fb[None]])
        P.dma("gpsimd", w2.t[0:64, :], w2d, writes=[w2[None]])
        self.P.op("vector", lambda e: e.memset(acc.t[:], 0.0), writes=[acc[None]])
        for t in range(4):
            sl = slice(t * 512, (t + 1) * 512)
            pp = self.ps[t % 2]
            self.mm(pp.t[0:64, :], w0.t[0:33, :], featsT.t[0:33, sl], True, True, [w0[None], featsT[None]], [pp[None]])
            self.sin_mlp(hid1T.t[0:64, sl], pp.t[0:64, :], fb.t[0:64, 0:1], fb.t[0:64, 2:3],
                         st64,
                         [pp[None], fb[None]], [hid1T[t]])
        for t in range(4):
            sl = slice(t * 512, (t + 1) * 512)
            pp = self.ps[2 + t % 2]
            self.mm(pp.t[0:64, :], w1.t[0:64, :], hid1T.t[0:64, sl], True, True, [w1[None], hid1T[t]], [pp[None]])
            self.sin_mlp(hid2f.t[0:64, sl], pp.t[0:64, :], fb.t[0:64, 1:2], fb.t[0:64, 3:4],
                         st64,
                         [pp[None], fb[None]], [hid2f[t]])
            self.cp("vector", hid2T.t[0:64, sl], hid2f.t[0:64, sl], [hid2f[t]], [hid2T[t]])
        P.barrier()
        ar2.reset()
        winb = [ar2.alloc("hwin%d" % n, (D,), F32) for n in range(2)]

        def filt_tile(lc, ct, pb):
            self.mm(pb.t[:], hid2T.t[0:64, lc * 128:(lc + 1) * 128], w2.t[0:64, ct * 512:(ct + 1) * 512], True, True,
                    [hid2T[None], w2[None]], [pb[None]])

        for lc in range(16):
            wn = winb[lc % 2]
            P.dma("sync", wn.t[:], wind[lc * 128:(lc + 1) * 128, :], writes=[wn[None]])
            for ct in range(8):
                pb = self.ps[ct % 4]
                tm = tmps[ct % 2]
                dh = ct % 2
                filt_tile(lc, ct, pb)
                self.tt("vector", tm.t[:], pb.t[:], wn.t[:, dh * 512:(dh + 1) * 512], ALU.mult, [pb[None], wn[None]], [tm[None]])
                tm2 = tmps[2 + ct % 2]
                self.act(tm2.t[:], tm.t[:], AF.Abs, [tm[None]], [tm2[None]])
                self.tt("gpsimd" if ct % 2 == 0 else "vector", acc.t[:, ct * 512:(ct + 1) * 512], tm2.t[:], acc.t[:, ct * 512:(ct + 1) * 512], ALU.add,
                        [tm2[None], acc[ct]], [acc[ct]])
        for ct in range(8):
            pb = self.ps[4 + ct % 2]
            self.mm(pb.t[:], self.ones_f.t[:], acc.t[:, ct * 512:(ct + 1) * 512], True, True, [self.ones_f[None], acc[ct]], [pb[None]])
            P.op("vector", lambda e, ct=ct, pb=pb: e.reciprocal(out=acc.t[:, ct * 512:(ct + 1) * 512], in_=pb.t[:]),
                 reads=[pb[None]], writes=[acc[ct]])
        w2pm = ar2.alloc("hw2pm", (2, D), BF16)
        for o in range(2):
            for dh in range(2):
                c0 = (0 * 4 + o * 2 + dh) * 512
                c1 = (1 * 4 + o * 2 + dh) * 512
                ta, tb = tmps[0], tmps[1]
                self.tt("vector", ta.t[0:64, :], w2.t[0:64, c0:c0 + 512], acc.t[0:64, c0:c0 + 512], ALU.mult, [w2[None], acc[None]], [ta[None]])
                self.tt("vector", tb.t[0:64, :], w2.t[0:64, c1:c1 + 512], acc.t[0:64, c1:c1 + 512], ALU.mult, [w2[None], acc[None]], [tb[None]])
                self.tt("vector", w2pm.t[0:64, 0, dh * 512:(dh + 1) * 512], ta.t[0:64, :], tb.t[0:64, :], ALU.add, [ta[None], tb[None]], [w2pm[(0, dh)]])
                self.tt("vector", w2pm.t[0:64, 1, dh * 512:(dh + 1) * 512], ta.t[0:64, :], tb.t[0:64, :], ALU.subtract, [ta[None], tb[None]], [w2pm[(1, dh)]])
            nb = 0
            for lc in range(16):
                wn = winb[lc % 2]
                P.dma("sync", wn.t[:], wind[lc * 128:(lc + 1) * 128, :], writes=[wn[None]])
                for dh in range(2):
                    dsl = slice(dh * 512, (dh + 1) * 512)
                    for pm, dst in ((0, hp), (1, hm)):
                        pb = self.ps[nb % 4]
                        nb += 1
                        self.mm(pb.t[:], hid2T.t[0:64, lc * 128:(lc + 1) * 128], w2pm.t[0:64, pm, dsl], True, True,
                                [hid2T[None], w2pm[(pm, dh)]], [pb[None]])
                        self.tt("vector", dst.t[:, lc, dsl], pb.t[:], wn.t[:, dsl], ALU.mult, [pb[None], wn[None]], [dst[lc]])
            P.barrier()
            ar2.reset()
            ftb = [ar2.alloc("hft%d" % n, (16, 2, 128), BF16) for n in range(2)]
            stage = ar2.alloc("hstage", (3, D), F32)
            bias_bc = ar2.alloc("hbias", (D,), F32)
            P.dma("sync", bias_bc.t[:], biasd[o:o + 1, :].to_broadcast([128, D]), writes=[bias_bc[None]])
            for fc in range(16):
                ft = ftb[fc % 2]
                P.dma("gpsimd", ft.t[:], fwdd[fc].rearrange("p (a b c) -> p a b c", a=16, b=2), writes=[ft[None]])
                b0 = 4 * (fc % 2)
                self.dft_fwd(hp, ft, 0, [self.ps[b0], self.ps[b0 + 1]])
                self.dft_fwd(hm, ft, 1, [self.ps[b0 + 2], self.ps[b0 + 3]])
                if fc == 0:
                    self.dft_fwd(hp, ft, 1, [self.ps[4], self.ps[5]])
                for dh in range(2):
                    dsl = slice(dh * 512, (dh + 1) * 512)
                    self.tt("vector", stage.t[:, 0, dsl], self.ps[b0 + dh].t[:], bias_bc.t[:, dsl], ALU.add,
                            [self.ps[b0 + dh][None], bias_bc[None]], [stage[(0, dh)]])
                    self.cp("scalar", stage.t[:, 2, dsl], stage.t[:, 0, dsl], [stage[(0, dh)]], [stage[(2, dh)]])
                    self.cp("scalar", stage.t[:, 1, dsl], self.ps[b0 + 2 + dh].t[:], [self.ps[b0 + 2 + dh][None]], [stage[(1, dh)]])
                    if fc == 0:
                        P.op("vector", lambda e, dsl=dsl: e.memset(stage.t[0:1, 1, dsl], 0.0), reads=[], writes=[stage[(1, dh)]])
                        self.tt("vector", stage.t[0:1, 2, dsl], self.ps[4 + dh].t[0:1, :], bias_bc.t[0:1, dsl], ALU.add,
                                [self.ps[4 + dh][None], bias_bc[None]], [stage[(2, dh)]])
                P.dma("sync", Gd.t[o, fc], stage.t[:].rearrange("p a b -> p (a b)"), reads=[stage[None]], writes=[Gd[(o, fc)]])
            P.barrier()
            ar2.reset()
            winb = [ar2.alloc("hwin%d_%d" % (n, o), (D,), F32) for n in range(2)]
            w2pm = ar2.alloc("hw2pm_%d" % o, (2, D), BF16)

        P.barrier()
        ar.reset()
        ar2.reset()
        self.norm_phase(i, 0, 0)
        z_tok = ar.alloc("hz", (16, D), BF16)
        mark = ar.off
        wib = [ar.alloc("hwi%d" % n, (8, 128), BF16) for n in range(2)]
        raw = [ar.alloc("hraw%d" % n, (L + 2,), F32) for n in range(2)]
        cacc = [ar.alloc("hcacc%d" % n, (L,), F32) for n in range(2)]
        obf = [ar.alloc("hobf%d" % n, (L,), BF16) for n in range(2)]
        xst = [ar.alloc("hxst%d" % n, (16, 128), BF16) for n in range(2)]
        cv = ar.alloc("hcv", (24, 4), F32)
        P.dma("sync", cv.t[:], cvd.rearrange("p (a b) -> p a b", b=4), writes=[cv[None]])
        for n in range(2):
            P.op("vector", lambda e, n=n: e.memset(raw[n].t[:, 0:1], 0.0), writes=[raw[n]["pad"]])
            P.op("vector", lambda e, n=n: e.memset(raw[n].t[:, L + 1:L + 2], 0.0), writes=[raw[n]["pad"]])
        ntrc = {"n": 0}

        def hy_part_b(fcx, ob, ca):
            self.act(ob.t[:], ca.t[:], AF.Identity, [ca[None], cv[None]], [ob[None]], bias=cv.t[:, fcx, 3:4], scale=1.0)
            kind = fcx // 8
            fcol = fcx % 8
            xs_ = xst[fcx % 2]
            for q in range(4):
                pT = self.ps[2 + ntrc["n"] % 2]
                ntrc["n"] += 1
                pTv = pT.t[:].bitcast(BF16)[:, 0:512].rearrange("p (a b) -> p a b", a=4)
                for tt_ in range(4):
                    tc = q * 4 + tt_
                    P.op("tensor", lambda e, pTv=pTv, tt_=tt_, ob=ob, tc=tc: e.transpose(pTv[:, tt_, :], ob.t[:, tc * 128:(tc + 1) * 128], self.ident.t[:]),
                         reads=[ob[None], self.ident[None]], writes=[pT[None]])
                if kind == 0:
                    self.cp("vector", z_tok.t[:, q * 4:(q + 1) * 4, fcol * 128:(fcol + 1) * 128], pTv, [pT[None]],
                            RK(z_tok, [q * 4 + a for a in range(4)]))
                else:
                    self.cp("vector", xs_.t[:, q * 4:(q + 1) * 4, :], pTv, [pT[None]], [xs_[q]])
            if kind > 0:
                xdd = xd[kind - 1]
                P.dma("sync", xdd.t.rearrange("(tc p) f -> p tc f", p=128)[:, :, fcol * 128:(fcol + 1) * 128], xs_.t[:],
                      reads=[xs_[None]], writes=[xdd[fcol]])

        hy_def = [None]
        for fcx in range(24):
            wi = wib[fcx % 2]
            rw = raw[fcx % 2]
            ca = cacc[fcx % 2]
            ob = obf[fcx % 2]
            P.dma("gpsimd", wi.t[:], wid[fcx].rearrange("p (k f) -> p k f", k=8), writes=[wi[None]])
            for T in range(4):
                pp = self.ps[T % 2]
                for k in range(8):
                    self.mm(pp.t[:], wi.t[:, k, :], self.hn.t[:, k, T * 512:(T + 1) * 512], k == 0, k == 7,
                            [wi[None], self.hn[(k, T)]], [pp[None]], inc=(k == 7))
                self.cp("scalar", rw.t[:, 1 + T * 512:1 + (T + 1) * 512], pp.t[:], [pp[None]], [rw[T]])
            rall = [rw[T] for T in range(4)] + [rw["pad"]]
            self.ts("vector", ca.t[:], rw.t[:, 0:L], cv.t[:, fcx, 0:1], 0.0, ALU.mult, ALU.add, rall + [cv[None]], [ca[None]])
            self.stt("vector", ca.t[:], rw.t[:, 1:L + 1], cv.t[:, fcx, 1:2], ca.t[:], ALU.mult, ALU.add, rall + [cv[None], ca[None]], [ca[None]])
            self.stt("vector", ca.t[:], rw.t[:, 2:L + 2], cv.t[:, fcx, 2:3], ca.t[:], ALU.mult, ALU.add, rall + [cv[None], ca[None]], [ca[None]])
            if hy_def[0] is not None:
                hy_def[0]()
            hy_def[0] = (lambda fcx=fcx, ob=ob, ca=ca: hy_part_b(fcx, ob, ca))
        hy_def[0]()
        P.barrier()
        ar.off = mark
        RQ = ar.alloc("hRQ", (32, D), BF16)
        tqa = [ar.alloc("htq%d" % n, (512,), F32) for n in range(2)]
        for o in range(2):
            ar2.reset()
            ftb = [ar2.alloc("hcft%d_%d" % (n, o), (16, 2, 128), BF16) for n in range(2)]
            gtb = [ar2.alloc("hcg%d_%d" % (n, o), (3, D), BF16) for n in range(2)]
            tq = tqa + [ar2.alloc("htqb%d_%d" % (n, o), (512,), F32) for n in range(2)]
            for fc in range(16):
                ft = ftb[fc % 2]
                gt = gtb[fc % 2]
                P.dma("gpsimd", ft.t[:], fwdd[fc].rearrange("p (a b c) -> p a b c", a=16, b=2), writes=[ft[None]])
                P.dma("gpsimd", gt.t[:], Gd.t[o, fc].rearrange("p (a b) -> p a b", a=3), reads=[Gd[(o, fc)]], writes=[gt[None]])
                b0 = 4 * (fc % 2)
                self.dft_fwd(z_tok, ft, 0, [self.ps[b0], self.ps[b0 + 1]])
                self.dft_fwd(z_tok, ft, 1, [self.ps[b0 + 2], self.ps[b0 + 3]])
                for dh in range(2):
                    dsl = slice(dh * 512, (dh + 1) * 512)
                    pA, pB = self.ps[b0 + dh], self.ps[b0 + 2 + dh]
                    t1, t2, t3, t4 = tq
                    self.tt("vector", t1.t[:], pA.t[:], gt.t[:, 0, dsl], ALU.mult, [pA[None], gt[None]], [t1[None]])
                    self.tt("vector", t2.t[:], pB.t[:], gt.t[:, 1, dsl], ALU.mult, [pB[None], gt[None]], [t2[None]])
                    self.tt("vector", RQ.t[:, fc, dsl], t1.t[:], t2.t[:], ALU.subtract, [t1[None], t2[None]], [RQ[fc]])
                    self.tt("vector", t3.t[:], pA.t[:], gt.t[:, 1, dsl], ALU.mult, [pA[None], gt[None]], [t3[None]])
                    self.tt("vector", t4.t[:], pB.t[:], gt.t[:, 2, dsl], ALU.mult, [pB[None], gt[None]], [t4[None]])
                    self.tt("vector", RQ.t[:, 16 + fc, dsl], t3.t[:], t4.t[:], ALU.add, [t3[None], t4[None]], [RQ[16 + fc]])
            P.barrier()
            ar2.reset()
            ivb = [ar2.alloc("hiv%d_%d" % (n, o), (32, 128), BF16) for n in range(2)]
            xcb = [ar2.alloc("hxc%d_%d" % (n, o), (D,), BF16) for n in range(2)]
            for tc in range(16):
                iv = ivb[tc % 2]
                xc = xcb[tc % 2]
                P.dma("gpsimd", iv.t[:], invd[tc].rearrange("p (a b) -> p a b", a=32), writes=[iv[None]])
                P.dma("sync", xc.t[:], xd[o].t[tc * 128:(tc + 1) * 128, :], reads=[xd[o][None]], writes=[xc[None]])
                for dh in range(2):
                    dsl = slice(dh * 512, (dh + 1) * 512)
                    py = self.ps[4 + dh]
                    for fcx in range(32):
                        self.mm(py.t[:], iv.t[:, fcx, :], RQ.t[:, fcx, dsl], fcx == 0, fcx == 31, [iv[None], RQ[fcx]], [py[None]], inc=(fcx == 31))
                    self.tt("vector", z_tok.t[:, tc, dsl], py.t[:], xc.t[:, dsl], ALU.mult, [py[None], xc[None]], [z_tok[tc]])
            P.barrier()
        ar2.reset()
        wo = ar2.alloc("hwo", (8, D), BF16)
        z3T = ar2.alloc("hz3T", (8, 512), BF16)
        P.dma("gpsimd", wo.t[:], wod.rearrange("p (k n) -> p k n", k=8), writes=[wo[None]])
        ntr = 0
        for T in range(4):
            sl = slice(T * 512, (T + 1) * 512)
            for tcc in range(4):
                tc = T * 4 + tcc
                for q in range(2):
                    pT = self.ps[ntr % 2]
                    ntr += 1
                    pTv = pT.t[:].bitcast(BF16)[:, 0:512].rearrange("p (a b) -> p a b", a=4)
                    for dd in range(4):
                        dc = q * 4 + dd
                        P.op("tensor", lambda e, pTv=pTv, dd=dd, tc=tc, dc=dc: e.transpose(pTv[:, dd, :], z_tok.t[:, tc, dc * 128:(dc + 1) * 128], self.ident.t[:]),
                             reads=[z_tok[tc], self.ident[None]], writes=[pT[None]])
                    self.cp("vector", z3T.t[:, q * 4:(q + 1) * 4, tcc * 128:(tcc + 1) * 128], pTv, [pT[None]],
                            RK(z3T, [q * 4 + a for a in range(4)]))
            for dc in range(8):
                po = self.ps[2 + dc % 2]
                for k in range(8):
                    self.mm(po.t[:], wo.t[:, k, dc * 128:(dc + 1) * 128], z3T.t[:, k, :], k == 0, k == 7, [wo[None], z3T[k]], [po[None]], inc=(k == 7))
                self.stt("vector", self.h.t[:, dc, sl], po.t[:], self.modv(i, 2, dc), self.h.t[:, dc, sl], ALU.mult, ALU.add,
                         [po[None], self.modb[i], self.h[(dc, T)]], [self.h[(dc, T)]])

    def mix_ssd(self, i):
        P, ar, ar2 = self.P, self.ar, self.ar2
        j = i // 4
        wid = self.inp("ss_wi", [32, 128, 8 * 128])
        wdtd = self.inp("ss_wdt", [128, 8 * 64])
        cvd = self.inp("ss_cv", [128, 32 * 4])
        smalld = self.inp("ss_small", [1, 64 + 64 + 32])
        nwd = self.inp("ss_nw", [1, 2048])
        wod = self.inp("ss_wo", [128, 16 * D])
        ctxd = self.inp("ctxT", [D, LC])
        zd = self.scratch("ss_z", [L, 2048], BF16)
        xd = self.scratch("ss_x", [L, 2048], BF16)
        btokd = self.scratch("ss_btok", [L, 1024], BF16)
        BTd = self.scratch("ss_BT", [1024, L], BF16)
        CTd = self.scratch("ss_CT", [1024, L], BF16)
        xcd = self.scratch("ss_xc", [LC, 2048], BF16)
        bcd = self.scratch("ss_bc", [LC, 1024], BF16)
        Hbd = self.scratch("ss_Hb", [16, 128, 2048], BF16)
        ynTd = self.scratch("ss_ynT", [2048, L], BF16)

        self.norm_phase(i, 0, 0)
        dt_tok = ar.alloc("sdt", (18, 64), F32)
        small = ar.alloc("ssmall", (160,), F32)
        a_bc = ar.alloc("sabc", (64,), F32)
        mark0 = ar.off
        P.dma("sync", small.t[:], smalld[0:1, :].to_broadcast([128, 160]), writes=[small[None]])
        self.act(a_bc.t[:], small.t[:, 0:64], AF.Exp, [small[None]], [a_bc[None]])
        self.ts("vector", a_bc.t[:], a_bc.t[:], -1.0, 0.0, ALU.mult, ALU.add, [a_bc[None]], [a_bc[None]])
        dtb_bc = small.t[:, 64:128]
        dsk_bc = small.t[:, 128:160]
        hc = ar.alloc("shc", (8, LC), F32)
        hcn = ar.alloc("shcn", (8, LC), BF16)
        for c in range(8):
            P.dma("sync", hc.t[:, c, :], ctxd[c * 128:(c + 1) * 128, :], writes=[hc[(c, 0)]])
        self.norm_mod(hc, LC, lambda c: self.lv.t[:, i, 2, c:c + 1], lambda c: self.modv(i, 0, c, 1), hcn,
                      [self.lv[(i, 2)], self.modb[i]], ar)
        wib = [ar.alloc("swi%d" % n, (8, 128), BF16) for n in range(2)]
        raw = [ar.alloc("sraw%d" % n, (L + 2,), F32) for n in range(2)]
        cacc = [ar.alloc("scacc%d" % n, (L,), F32) for n in range(2)]
        obf = [ar.alloc("sobf%d" % n, (L,), BF16) for n in range(2)]
        xst = [ar.alloc("sxst%d" % n, (16, 128), BF16) for n in range(2)]
        cv = ar.alloc("scv", (32, 4), F32)
        wdt = ar.alloc("swdt", (8, 64), BF16)
        sp = [ar.alloc("ssp%d" % n, (64,), F32) for n in range(4)]
        P.dma("sync", cv.t[:], cvd.rearrange("p (a b) -> p a b", b=4), writes=[cv[None]])
        P.dma("gpsimd", wdt.t[:], wdtd.rearrange("p (k f) -> p k f", k=8), writes=[wdt[None]])
        for n in range(2):
            P.op("vector", lambda e, n=n: e.memset(raw[n].t[:, 0:1], 0.0), writes=[raw[n]["pad"]])
        cnt = {"n": 0, "tr": 0}

        def proj_chunk(fcx, src, srckeys, ntok, dst_tok, dst_fm):
            n = cnt["n"]
            cnt["n"] += 1
            wi, rw, ca, ob, xs_ = wib[n % 2], raw[n % 2], cacc[n % 2], obf[n % 2], xst[n % 2]
            TW = min(512, ntok)
            NT = ntok // TW
            P.dma("gpsimd", wi.t[:], wid[fcx - 16].rearrange("p (k f) -> p k f", k=8), writes=[wi[None]])
            is_z = fcx < 16
            for T in range(NT):
                pp = self.ps[T % 2]
                for k in range(8):
                    self.mm(pp.t[:, 0:TW], wi.t[:, k, :], src.t[:, k, T * TW:(T + 1) * TW], k == 0, k == 7,
                            [wi[None], srckeys(k, T)], [pp[None]], inc=(k == 7))
                if is_z:
                    self.act(ob.t[:, T * TW:(T + 1) * TW], pp.t[:, 0:TW], AF.Silu, [pp[None]], [ob[T]])
                else:
                    self.cp("scalar", rw.t[:, 1 + T * TW:1 + (T + 1) * TW], pp.t[:, 0:TW], [pp[None]], [rw[T]])
            if not is_z:
                cx = fcx - 16
                P.op("vector", lambda e, rw=rw: e.memset(rw.t[:, ntok + 1:ntok + 2], 0.0), writes=[rw["pad2"]])
                rall = [rw[T] for T in range(NT)] + [rw["pad"], rw["pad2"]]
                self.ts("vector", ca.t[:, 0:ntok], rw.t[:, 0:ntok], cv.t[:, cx, 0:1], 0.0, ALU.mult, ALU.add, rall + [cv[None]], [ca[None]])
                self.stt("vector", ca.t[:, 0:ntok], rw.t[:, 1:ntok + 1], cv.t[:, cx, 1:2], ca.t[:, 0:ntok], ALU.mult, ALU.add, rall + [cv[None], ca[None]], [ca[None]])
                self.stt("vector", ca.t[:, 0:ntok], rw.t[:, 2:ntok + 2], cv.t[:, cx, 2:3], ca.t[:, 0:ntok], ALU.mult, ALU.add, rall + [cv[None], ca[None]], [ca[None]])
            prev = cnt.get("deferred")
            cnt["deferred"] = None
            if prev is not None:
                prev()
            cnt["deferred"] = lambda: part_b(dst_tok, dst_fm, ntok, ob, xs_, ca, fcx - 16)

        def part_b(dst_tok, dst_fm, ntok, ob, xs_, ca, cx):
            self.act(ob.t[:, 0:ntok], ca.t[:, 0:ntok], AF.Silu, [ca[None], cv[None]], [ob[None]], bias=cv.t[:, cx, 3:4], scale=1.0)
            if dst_fm is not None:
                dbuf, row0 = dst_fm
                P.dma("sync", dbuf.t[row0:row0 + 128, 0:ntok], ob.t[:, 0:ntok], reads=[ob[None]], writes=[dbuf[row0]])
            if dst_tok is not None:
                dbuf, fcol = dst_tok
                ntc = ntok // 128
                for q in range((ntc + 3) // 4):
                    nq = min(4, ntc - q * 4)
                    pT = self.ps[2 + cnt["tr"] % 2]
                    cnt["tr"] += 1
                    pTv = pT.t[:].bitcast(BF16)[:, 0:512].rearrange("p (a b) -> p a b", a=4)
                    for tt_ in range(nq):
                        tc = q * 4 + tt_
                        P.op("tensor", lambda e, pTv=pTv, tt_=tt_, ob=ob, tc=tc: e.transpose(pTv[:, tt_, :], ob.t[:, tc * 128:(tc + 1) * 128], self.ident.t[:]),
                             reads=[ob[None], self.ident[None]], writes=[pT[None]])
                    self.cp("vector", xs_.t[:, q * 4:q * 4 + nq, :], pTv[:, 0:nq, :], [pT[None]], [xs_[q]])
                P.dma("sync", dbuf.t.rearrange("(tc p) f -> p tc f", p=128)[:, :, fcol * 128:(fcol + 1) * 128], xs_.t[:, 0:ntc, :],
                      reads=[xs_[None]], writes=[dbuf[fcol]])

        hk = lambda k, T: self.hn[(k, T)]
        ck = lambda k, T: hcn[(k, 0)]
        wzd = self.inp("ss_wz", [128, 8 * 2048]).rearrange("p (k n) -> p k n", k=8)
        wzb = [ar.alloc("swz%d" % n, (8, 512), BF16) for n in range(2)]
        zst = [ar.alloc("szst%d" % n, (512,), BF16) for n in range(3)]
        nz = 0
        for piece in range(4):
            wz = wzb[piece % 2]
            P.dma("gpsimd", wz.t[:], wzd[:, :, piece * 512:(piece + 1) * 512], writes=[wz[None]])
            for tc in range(16):
                pp = self.ps[4 + nz % 2]
                zs_ = zst[nz % 3]
                nz += 1
                for k in range(8):
                    self.mm(pp.t[:], self.hn.t[:, k, tc * 128:(tc + 1) * 128], wz.t[:, k, :], k == 0, k == 7,
                            [self.hn[(k, tc // 4)], wz[None]], [pp[None]], inc=(k == 7))
                self.act(zs_.t[:], pp.t[:], AF.Silu, [pp[None]], [zs_[None]])
                P.dma("sync", zd.t[tc * 128:(tc + 1) * 128, piece * 512:(piece + 1) * 512], zs_.t[:], reads=[zs_[None]], writes=[zd[(tc, piece)]])
        for fcx in range(16, 32):
            proj_chunk(fcx, self.hn, hk, L, (xd, fcx - 16), None)
            proj_chunk(fcx, hcn, ck, LC, (xcd, fcx - 16), None)
        for fcx in range(32, 40):
            proj_chunk(fcx, self.hn, hk, L, (btokd, fcx - 32), (BTd, (fcx - 32) * 128))
            proj_chunk(fcx, hcn, ck, LC, (bcd, fcx - 32), None)
        for fcx in range(40, 48):
            proj_chunk(fcx, self.hn, hk, L, None, (CTd, (fcx - 40) * 128))
        if cnt.get("deferred") is not None:
            cnt["deferred"]()
            cnt["deferred"] = None
        for c in range(18):
            pp = self.ps[4 + c % 2]
            for k in range(8):
                if c < 16:
                    lhs, rk = self.hn.t[:, k, c * 128:(c + 1) * 128], self.hn[(k, c // 4)]
                else:
                    lhs, rk = hcn.t[:, k, (c - 16) * 128:(c - 15) * 128], hcn[(k, 0)]
                self.mm(pp.t[:, 0:64], lhs, wdt.t[:, k, :], k == 0, k == 7, [rk, wdt[None]], [pp[None]], inc=(k == 7))
            x_, ax, e_, l_ = sp
            self.tt("vector", x_.t[:], pp.t[:, 0:64], dtb_bc, ALU.add, [pp[None], small[None]], [x_[None]])
            self.act(ax.t[:], x_.t[:], AF.Abs, [x_[None]], [ax[None]])
            self.act(e_.t[:], ax.t[:], AF.Exp, [ax[None]], [e_[None]], scale=-1.0)
            self.act(l_.t[:], e_.t[:], AF.Ln, [e_[None]], [l_[None]], bias=1.0, scale=1.0)
            self.ts("vector", ax.t[:], x_.t[:], 0.0, 0.0, ALU.max, ALU.add, [x_[None]], [ax[None]])
            self.tt("vector", dt_tok.t[:, c, :], ax.t[:], l_.t[:], ALU.add, [ax[None], l_[None]], [dt_tok[c]])
        P.barrier()
        ar.off = mark0
        ar2.reset()

        tri = [ar.alloc("stri%d" % n, (128,), F32) for n in range(2)]
        neg = [ar.alloc("sneg%d" % n, (128,), F32) for n in range(2)]
        for d_, (pat, cm) in enumerate((([[1, 128]], -1), ([[-1, 128]], 1))):
            P.op("gpsimd", lambda e, d_=d_: e.memset(tri[d_].t[:], 1.0), writes=[tri[d_][None]])
            P.op("gpsimd", lambda e, d_=d_, pat=pat, cm=cm: e.affine_select(out=tri[d_].t[:], in_=tri[d_].t[:], pattern=pat,
                                                                             compare_op=ALU.is_ge, fill=0.0, base=0, channel_multiplier=cm),
                 reads=[tri[d_][None]], writes=[tri[d_][None]])
        for d_, (pat, cm) in enumerate((([[-1, 128]], 1), ([[1, 128]], -1))):
            P.op("gpsimd", lambda e, d_=d_: e.memset(neg[d_].t[:], -30000.0), writes=[neg[d_][None]])
            P.op("gpsimd", lambda e, d_=d_, pat=pat, cm=cm: e.affine_select(out=neg[d_].t[:], in_=neg[d_].t[:], pattern=pat,
                                                                             compare_op=ALU.is_gt, fill=0.0, base=0, channel_multiplier=cm),
                 reads=[neg[d_][None]], writes=[neg[d_][None]])
        neg4 = [ar.alloc("sneg4_%d" % n, (4, 128), BF16) for n in range(2)]
        for d_ in range(2):
            self.cp("vector", neg4[d_].t[:], neg[d_].t[:].rearrange("p (o l) -> p o l", o=1).to_broadcast([128, 4, 128]),
                    [neg[d_][None]], [neg4[d_][None]])
        nadt = [ar.alloc("snadt%d" % n, (32,), F32) for n in range(2)]
        ahi = [ar.alloc("sahi%d" % n, (32,), BF16) for n in range(2)]
        alo = [ar.alloc("salo%d" % n, (32,), BF16) for n in range(2)]
        nahi = [ar.alloc("snahi%d" % n, (32,), BF16) for n in range(2)]
        nalo = [ar.alloc("snalo%d" % n, (32,), BF16) for n in range(2)]
        tri_bf = [ar.alloc("stribf%d" % n, (128,), BF16) for n in range(2)]
        for d_ in range(2):
            self.cp("vector", tri_bf[d_].t[:], tri[d_].t[:], [tri[d_][None]], [tri_bf[d_][None]])
        H = [ar.alloc("sH%d" % n, (2048,), F32) for n in range(2)]
        for d_ in range(2):
            P.op("gpsimd", lambda e, d_=d_: e.memset(H[d_].t[:], 0.0), writes=[H[d_][None]])
        xtb = [ar.alloc("sxt%d" % n, (2048,), BF16) for n in range(2)]
        btb = [ar.alloc("sbt%d" % n, (8, 128), BF16) for n in range(2)]
        dxs = [ar.alloc("sdxs%d" % n, (2048,), BF16) for n in range(2)]
        cst = [ar.alloc("scst%d" % n, (64,), F32) for n in range(2)]
        adt = [ar.alloc("sadt%d" % n, (32,), F32) for n in range(2)]
        sw = [ar.alloc("ssw%d" % n, (32,), F32) for n in range(4)]
        edec = [ar.alloc("sedec%d" % n, (32,), F32) for n in range(2)]
        nld = {"n": 0}

        def load_xb(xsrc, bsrc, r0):
            n = nld["n"]
            nld["n"] += 1
            xt, bt = xtb[n % 2], btb[n % 2]
            P.dma("sync", xt.t[:], xsrc.t[r0:r0 + 128, :], reads=[xsrc[None]], writes=[xt[None]])
            P.dma("sync", bt.t[:], bsrc.t[r0:r0 + 128, :].rearrange("p (g n) -> p g n", g=8), reads=[bsrc[None]], writes=[bt[None]])
            return xt, bt

        def cums(d_, c):
            dtc = dt_tok.t[:, c, d_ * 32:(d_ + 1) * 32]
            self.tt("vector", adt[d_].t[:], dtc, a_bc.t[:, d_ * 32:(d_ + 1) * 32], ALU.mult, [dt_tok[c], a_bc[None]], [adt[d_][None]])
            pc = self.ps[6]
            self.mm(pc.t[:, 0:32], tri[d_].t[:], adt[d_].t[:], True, True, [tri[d_][None], adt[d_][None]], [pc[None]])
            self.mm(pc.t[:, 32:64], self.ones_f.t[:], adt[d_].t[:], True, True, [self.ones_f[None], adt[d_][None]], [pc[None]])
            self.cp("scalar", cst[d_].t[:], pc.t[:, 0:64], [pc[None]], [cst[d_][None]])
            self.cp("vector", ahi[d_].t[:], adt[d_].t[:], [adt[d_][None]], [ahi[d_][None]])
            self.tt("vector", alo[d_].t[:], adt[d_].t[:], ahi[d_].t[:], ALU.subtract, [adt[d_][None], ahi[d_][None]], [alo[d_][None]])
            self.ts("gpsimd", nahi[d_].t[:], ahi[d_].t[:], -1.0, 0.0, ALU.mult, ALU.add, [ahi[d_][None]], [nahi[d_][None]])
            self.ts("gpsimd", nalo[d_].t[:], alo[d_].t[:], -1.0, 0.0, ALU.mult, ALU.add, [alo[d_][None]], [nalo[d_][None]])

        def state_step(d_, c, xt, bt):
            dtc = dt_tok.t[:, c, d_ * 32:(d_ + 1) * 32]
            dd, ee, ww = sw[d_ * 2], sw[d_ * 2 + 1], sw[d_ * 2]
            self.tt("vector", dd.t[:], cst[d_].t[:, 32:64], cst[d_].t[:, 0:32], ALU.subtract, [cst[d_][None]], [dd[None]])
            self.act(ee.t[:], dd.t[:], AF.Exp, [dd[None]], [ee[None]])
            self.tt("vector", ww.t[:], ee.t[:], dtc, ALU.mult, [ee[None], dt_tok[c]], [ww[None]])
            self.act(edec[d_].t[:], cst[d_].t[:, 32:64], AF.Exp, [cst[d_][None]], [edec[d_][None]])
            dx = dxs[d_]
            self.tt("vector", dx.t[:].rearrange("p (h q) -> p h q", h=32), xt.t[:].rearrange("p (h q) -> p h q", h=32),
                    ww.t[:].rearrange("p (h o) -> p h o", o=1).to_broadcast([128, 32, 64]), ALU.mult, [xt[None], ww[None]], [dx[None]])
            Hd = H[d_]
            self.tt("gpsimd", Hd.t[:].rearrange("p (h q) -> p h q", h=32), Hd.t[:].rearrange("p (h q) -> p h q", h=32),
                    edec[d_].t[:].rearrange("p (h o) -> p h o", o=1).to_broadcast([128, 32, 64]), ALU.mult, [Hd[None], edec[d_][None]], [Hd[None]])
            for gp in range(4):
                pS = self.ps[7]
                for gi in range(2):
                    g = gp * 2 + gi
                    self.mm(pS.t[:, gi * 256:(gi + 1) * 256], bt.t[:, g, :], dx.t[:, g * 256:(g + 1) * 256], True, True,
                            [bt[None], dx[None]], [pS[None]])
                self.tt("vector", Hd.t[:, gp * 512:(gp + 1) * 512], Hd.t[:, gp * 512:(gp + 1) * 512], pS.t[:], ALU.add,
                        [Hd[None], pS[None]], [Hd[None]])

        for c in (0, 1):
            xt, bt = load_xb(xcd, bcd, c * 128)
            cums(0, 16 + c)
            state_step(0, 16 + c, xt, bt)
        for c in (1, 0):
            xt, bt = load_xb(xcd, bcd, c * 128)
            cums(1, 16 + c)
            state_step(1, 16 + c, xt, bt)
        hbst = [ar.alloc("shbst%d" % n, (2048,), BF16) for n in range(1)]
        for c in range(15, -1, -1):
            hb_ = hbst[0]
            self.cp("scalar", hb_.t[:], H[1].t[:], [H[1][None]], [hb_[None]])
            P.dma("sync", Hbd.t[c], hb_.t[:], reads=[hb_[None]], writes=[Hbd[c]])
            if c > 0:
                xt, bt = load_xb(xd, btokd, c * 128)
                cums(1, c)
                state_step(1, c, xt, bt)

        nw_bc = ar.alloc("snw", (2048,), F32)
        P.dma("sync", nw_bc.t[:], nwd[0:1, :].to_broadcast([128, 2048]), writes=[nw_bc[None]])
        ztb = [ar.alloc("szt%d" % n, (2048,), BF16) for n in range(2)]
        BTb = [ar.alloc("sBT%d" % n, (8, 128), BF16) for n in range(2)]
        CTb = [ar.alloc("sCT%d" % n, (8, 128), BF16) for n in range(2)]
        Hbb = [ar.alloc("sHbb%d" % n, (2048,), BF16) for n in range(2)]
        Hfb = ar.alloc("sHfb", (2048,), BF16)
        xsdt = [ar.alloc("sxsdt%d" % n, (2048,), BF16) for n in range(2)]
        ea = ar.alloc("sea", (64,), F32)
        cscol = ar.alloc("scscol", (64,), F32)
        yacc = ar2.alloc("syacc", (2048,), F32)
        Eb = [ar2.alloc("sE%d" % n, (4, 128), BF16) for n in range(4)]
        xdsk = ar2.alloc("sxdsk", (2048,), BF16)
        Mb = [ar2.alloc("sM%d" % n, (4, 128), BF16) for n in range(4)]
        GTs = [ar.alloc("sGT%d" % n, (128,), BF16) for n in range(2)]
        t12 = [ar2.alloc("st12_%d" % n, (512,), F32) for n in range(2)]
        junk = ar.alloc("sjunk", (256,), BF16)
        ss = ar.alloc("sss", (16,), F32)
        ynb = ar2.alloc("synb", (2048,), BF16)
        ynT = [ar.alloc("synT%d" % n, (16, 128), BF16) for n in range(1)]
        nrt = {"n": 0, "m": 0, "g": 0, "t": 0}
        for c in range(16):
            r0 = c * 128
            xt, bt = load_xb(xd, btokd, r0)
            zt, BT, CT, Hbc = ztb[c % 2], BTb[c % 2], CTb[c % 2], Hbb[c % 2]
            P.dma("sync", zt.t[:], zd.t[r0:r0 + 128, :], reads=[zd[None]], writes=[zt[None]])
            P.dma("sync", BT.t[:], BTd.t.rearrange("(g n) t -> n g t", n=128)[:, :, r0:r0 + 128], reads=[BTd[None]], writes=[BT[None]])
            P.dma("sync", CT.t[:], CTd.t.rearrange("(g n) t -> n g t", n=128)[:, :, r0:r0 + 128], reads=[CTd[None]], writes=[CT[None]])
            P.dma("sync", Hbc.t[:], Hbd.t[c], reads=[Hbd[c]], writes=[Hbc[None]])
            for d_ in range(2):
                cums(d_, c)
                self.cp("vector", cscol.t[:, d_ * 32:(d_ + 1) * 32], cst[d_].t[:, 0:32], [cst[d_][None]], [cscol[d_]])
                dtc = dt_tok.t[:, c, d_ * 32:(d_ + 1) * 32]
                self.tt("vector" if d_ == 0 else "gpsimd", xsdt[d_].t[:].rearrange("p (h q) -> p h q", h=32), xt.t[:].rearrange("p (h q) -> p h q", h=32),
                        dtc.rearrange("p (h o) -> p h o", o=1).to_broadcast([128, 32, 64]), ALU.mult, [xt[None], dt_tok[c]], [xsdt[d_][None]])
            self.tt("vector", xdsk.t[:].rearrange("p (h q) -> p h q", h=32), xt.t[:].rearrange("p (h q) -> p h q", h=32),
                    dsk_bc.rearrange("p (h o) -> p h o", o=1).to_broadcast([128, 32, 64]), ALU.mult, [xt[None], small[None]], [xdsk[None]])
            self.act(ea.t[:], cscol.t[:], AF.Exp, [cscol[None]], [ea[None]])
            self.cp("scalar", Hfb.t[:], H[0].t[:], [H[0][None]], [Hfb[None]])
            pyd, pyf, pyb = self.ps[3], self.ps[4], self.ps[5]
            v3 = lambda ap: ap.rearrange("p (h q) -> p h q", h=8)
            bc8 = lambda ap: ap.rearrange("p (h o) -> p h o", o=1).to_broadcast([128, 8, 64])

            def emit_y(g, Mg):
                gp, gi = divmod(g, 2)
                if gi == 0:
                    self.mm(pyd.t[:], self.ident.t[:], xdsk.t[:, gp * 512:(gp + 1) * 512], True, False,
                            [self.ident[None], xdsk[None]], [pyd[None]])
                for hh in range(4):
                    hcol = slice((g * 4 + hh) * 64, (g * 4 + hh + 1) * 64)
                    ocol = slice((gi * 4 + hh) * 64, (gi * 4 + hh + 1) * 64)
                    self.mm(pyd.t[:, ocol], Mg[0].t[:, hh, :], xsdt[0].t[:, hcol], False, False,
                            [Mg[0][None], xsdt[0][None]], [pyd[None]])
                    self.mm(pyd.t[:, ocol], Mg[1].t[:, hh, :], xsdt[1].t[:, hcol], False, gi == 1 and hh == 3,
                            [Mg[1][None], xsdt[1][None]], [pyd[None]])
                self.mm(pyf.t[:, gi * 256:(gi + 1) * 256], CT.t[:, g, :], Hfb.t[:, g * 256:(g + 1) * 256], True, True,
                        [CT[None], Hfb[None]], [pyf[None]])
                self.mm(pyb.t[:, gi * 256:(gi + 1) * 256], CT.t[:, g, :], Hbc.t[:, g * 256:(g + 1) * 256], True, True,
                        [CT[None], Hbc[None]], [pyb[None]])
                if gi == 1:
                    csl = slice(gp * 512, (gp + 1) * 512)
                    hs = slice(gp * 8, gp * 8 + 8)
                    t1, t2 = t12
                    self.tt("vector", v3(t1.t[:]), v3(pyf.t[:]), bc8(ea.t[:, hs]), ALU.mult, [pyf[None], ea[None]], [t1[None]])
                    self.tt("vector", v3(t2.t[:]), v3(pyb.t[:]), bc8(ea.t[:, 32 + gp * 8:32 + gp * 8 + 8]), ALU.mult, [pyb[None], ea[None]], [t2[None]])
                    self.tt("vector", yacc.t[:, csl], pyd.t[:], t1.t[:], ALU.add, [pyd[None], t1[None]], [yacc[gp]])
                    self.tt("gpsimd", yacc.t[:, csl], yacc.t[:, csl], t2.t[:], ALU.add, [yacc[gp], t2[None]], [yacc[gp]])
                    self.tt("gpsimd", yacc.t[:, csl], yacc.t[:, csl], zt.t[:, csl], ALU.mult, [yacc[gp], zt[None]], [yacc[gp]])

            pending = None
            for g in range(8):
                pG = self.ps[0]
                GT = GTs[nrt["g"] % 2]
                nrt["g"] += 1
                self.mm(pG.t[:, 0:128], BT.t[:, g, :], CT.t[:, g, :], True, True, [BT[None], CT[None]], [pG[None]])
                self.cp("scalar", GT.t[:], pG.t[:, 0:128], [pG[None]], [GT[None]])
                Mg = []
                for d_ in range(2):
                    n = nrt["n"]
                    nrt["n"] += 1
                    E_ = Eb[n % 4]
                    M_ = Mb[nrt["m"] % 4]
                    nrt["m"] += 1
                    pc = self.ps[1 + n % 2]
                    h0 = g * 4
                    bc4 = lambda b: b.t[:, h0:h0 + 4].rearrange("p (h o) -> p h o", o=1).to_broadcast([128, 4, 128])
                    self.mm(pc.t[:], tri_bf[d_].t[:], bc4(nahi[d_]), True, False, [tri_bf[d_][None], nahi[d_][None]], [pc[None]], inc=False)
                    self.mm(pc.t[:], tri_bf[d_].t[:], bc4(nalo[d_]), False, False, [tri_bf[d_][None], nalo[d_][None]], [pc[None]], inc=False)
                    self.mm(pc.t[:], self.ident.t[:], neg4[d_].t[:].rearrange("p a b -> p (a b)"), False, False,
                            [self.ident[None], neg4[d_][None]], [pc[None]], inc=False)
                    for hh in range(4):
                        self.mm(pc.t[:, hh * 128:(hh + 1) * 128], ahi[d_].t[:, h0 + hh:h0 + hh + 1].to_broadcast([128, 128]), tri_bf[d_].t[:],
                                False, False, [ahi[d_][None], tri_bf[d_][None]], [pc[None]], inc=False)
                        self.mm(pc.t[:, hh * 128:(hh + 1) * 128], alo[d_].t[:, h0 + hh:h0 + hh + 1].to_broadcast([128, 128]), tri_bf[d_].t[:],
                                False, hh == 3, [alo[d_][None], tri_bf[d_][None]], [pc[None]], inc=(hh == 3))
                    self.act(E_.t[:], pc.t[:].rearrange("p (a b) -> p a b", a=4), AF.Exp, [pc[None]], [E_[None]])
                    self.tt("vector" if n % 2 == 0 else "gpsimd", M_.t[:], E_.t[:],
                            GT.t[:].rearrange("p (o l) -> p o l", o=1).to_broadcast([128, 4, 128]), ALU.mult,
                            [E_[None], GT[None]], [M_[None]])
                    Mg.append(M_)
                if pending is not None:
                    emit_y(*pending)
                pending = (g, Mg)
            emit_y(*pending)
            P.op("vector", lambda e: e.memset(ss.t[:, 0:8], 0.0), writes=[ss[None]])
            for g in range(8):
                P.op("scalar", lambda e, g=g: e.activation(out=junk.t[:], in_=yacc.t[:, g * 256:(g + 1) * 256], func=AF.Square,
                                                           accum_out=ss.t[:, g:g + 1]),
                     reads=[yacc[g // 2], ss[None]], writes=[junk[None], ss[g]])
            self.act(ss.t[:, 8:16], ss.t[:, 0:8], AF.Sqrt, [ss[None]], [ss[None]], bias=EPS, scale=1.0 / 256.0)
            P.op("vector", lambda e: e.reciprocal(out=ss.t[:, 8:16], in_=ss.t[:, 8:16]), reads=[ss[None]], writes=[ss[None]])
            self.tt("vector", yacc.t[:].rearrange("p (g q) -> p g q", g=8), yacc.t[:].rearrange("p (g q) -> p g q", g=8),
                    ss.t[:, 8:16].rearrange("p (g o) -> p g o", o=1).to_broadcast([128, 8, 256]), ALU.mult, [yacc[None], ss[None]], [yacc[None]])
            self.tt("gpsimd", ynb.t[:], yacc.t[:], nw_bc.t[:], ALU.mult, [yacc[None], nw_bc[None]], [ynb[None]])
            yT = ynT[0]
            for q in range(4):
                pT = self.ps[6 + q % 2]
                pTv = pT.t[:].bitcast(BF16)[:, 0:512].rearrange("p (a b) -> p a b", a=4)
                for tt_ in range(4):
                    fc = q * 4 + tt_
                    P.op("tensor", lambda e, pTv=pTv, tt_=tt_, fc=fc: e.transpose(pTv[:, tt_, :], ynb.t[:, fc * 128:(fc + 1) * 128], self.ident.t[:]),
                         reads=[ynb[None], self.ident[None]], writes=[pT[None]])
                self.cp("vector", yT.t[:, q * 4:(q + 1) * 4, :], pTv, [pT[None]], [yT[q]])
            P.dma("sync", ynTd.t.rearrange("(fc p) t -> p fc t", p=128)[:, :, r0:r0 + 128], yT.t[:], reads=[yT[None]], writes=[ynTd[c]])
            if c < 15:
                state_step(0, c, xt, bt)
        P.barrier()
        ar.off = mark0
        ar2.reset()
        wo = ar.alloc("swo", (16, D), BF16)
        yTb = [ar.alloc("syTb%d" % n, (16, 512), BF16) for n in range(2)]
        P.dma("gpsimd", wo.t[:], wod.rearrange("p (k n) -> p k n", k=16), writes=[wo[None]])
        for T in range(4):
            sl = slice(T * 512, (T + 1) * 512)
            yb = yTb[T % 2]
            P.dma("sync", yb.t[:], ynTd.t.rearrange("(fc p) t -> p fc t", p=128)[:, :, sl], reads=[ynTd[None]], writes=[yb[None]])
            for dc in range(8):
                po = self.ps[dc % 2]
                for fc in range(16):
                    self.mm(po.t[:], wo.t[:, fc, dc * 128:(dc + 1) * 128], yb.t[:, fc, :], fc == 0, fc == 15, [wo[None], yb[None]], [po[None]], inc=(fc == 15))
                self.stt("vector", self.h.t[:, dc, sl], po.t[:], self.modv(i, 2, dc), self.h.t[:, dc, sl], ALU.mult, ALU.add,
                         [po[None], self.modb[i], self.h[(dc, T)]], [self.h[(dc, T)]])


def _c(a):
    return np.ascontiguousarray(a, dtype=np.float32)


def _grid_T():
    t = np.arange(L)
    row, col = t // 64, t % 64
    q = D // 4
    omega = (10000.0 ** (-np.arange(q, dtype=np.float32) / q)).astype(np.float32)

    def enc(p):
        ang = p.astype(np.float32)[:, None] * omega
        return np.concatenate([np.sin(ang), np.cos(ang)], axis=-1)
    g = np.concatenate([enc(row), enc(col)], axis=-1).astype(np.float32)
    return _c(g.T)


def _fnet_tables():
    k = np.arange(256)
    ang = 2.0 * np.pi * ((k[:, None] * k[None, :]) % 256) / 256.0
    cw = np.stack([np.cos(ang), np.sin(ang)], axis=1)
    cw = cw.reshape(2, 128, 2, 256).transpose(1, 0, 2, 3)
    t = np.arange(L)
    angl = 2.0 * np.pi * ((t[:, None] * t[None, :]) % L) / float(L)
    sc = 1.0 / math.sqrt(L * 256.0)
    cl = np.stack([np.cos(angl) * sc, -np.sin(angl) * sc], axis=1)
    cl = cl.reshape(16, 128, 2, 4, 512).transpose(3, 1, 0, 2, 4)
    return _c(cw.reshape(128, -1)), _c(cl.reshape(4, 128, -1))


_HY_CACHE = {}


def _hyena_tables():
    if _HY_CACHE:
        return _HY_CACHE
    f32 = np.float32
    pos = np.arange(L, dtype=f32)[:, None]
    t = pos / f32(L - 1)
    freqs = np.linspace(1e-4, 15, 16, dtype=f32)[None, :]
    ang = freqs * (f32(2.0 * math.pi) * pos / f32(L))
    feats = np.concatenate([t, np.cos(ang), -np.sin(ang)], axis=-1).astype(f32)
    max_decay = math.log(1e-2) / 0.3
    min_decay = math.log(1e-2) / 1.5
    deltas = np.linspace(min_decay, max_decay, D, dtype=f32)
    win = np.exp(-t * np.abs(deltas)[None, :]).astype(f32)
    N = 2 * L
    tt_ = np.arange(L, dtype=np.int64)
    ff = np.arange(L, dtype=np.int64)
    th = 2.0 * np.pi * ((tt_[:, None] * ff[None, :]) % N) / float(N)
    cosm = np.cos(th)
    sinm = np.sin(th)
    nyq = np.where(tt_ % 2 == 0, 1.0, -1.0)
    sinm_f = sinm.copy()
    sinm_f[:, 0] = nyq
    fwd = np.stack([cosm, sinm_f], axis=1)
    fwd = fwd.reshape(16, 128, 2, 16, 128).transpose(3, 1, 0, 2, 4)
    wf = np.full(L, 2.0)
    wf[0] = 1.0
    icos = (cosm * wf[None, :] / N).T
    isin = (2.0 * sinm / N).T
    isin[0, :] = nyq / N
    inv = np.concatenate([icos.reshape(16, 128, L), isin.reshape(16, 128, L)], 0)
    inv = inv.reshape(32, 128, 16, 128).transpose(2, 1, 0, 3)
    _HY_CACHE.update({"hy_featsT": _c(feats.T), "hy_win": _c(win), "hy_fwd": _c(fwd.reshape(16, 128, -1)),
                      "hy_inv": _c(inv.reshape(16, 128, -1))})
    return _HY_CACHE


def prep_shared(inp, layers, phases):
    sh = {}
    sh["ada_bT"] = _c(inp["ada_b"].reshape(4, 48, 128).transpose(2, 0, 1))
    sh["nw"] = _c(np.stack([inp["norm_mix_w"], inp["norm_ffn_w"]], 0).reshape(2, 4, 8, 128).transpose(3, 1, 0, 2))
    sh["fnw"] = _c(inp["final_norm_w"].reshape(8, 128).T)
    sh["gridT"] = _grid_T()
    for i in layers:
        sh["ada_w%d" % i] = _c(inp["ada_w"][i])
    for (kind, i) in phases:
        if kind == "ffn":
            w1 = inp["ffn_w1"][i].reshape(8, 128, NF, 128).transpose(2, 1, 0, 3)
            w3 = inp["ffn_w3"][i].reshape(8, 128, NF, 128).transpose(2, 1, 0, 3)
            sh["w13_%d" % i] = _c(np.stack([w1, w3], axis=2).reshape(NF, 128, -1))
            sh["w2_%d" % i] = _c(inp["ffn_w2"][i].reshape(NF, 128, 8, 128).transpose(2, 1, 0, 3).reshape(8, 128, -1))
        elif i % 4 == 0:
            j = i // 4
            w = inp["ssd_w_in"][j]
            sh["ss_wi"] = _c(w[:, 2048:6144].reshape(8, 128, 32, 128).transpose(2, 1, 0, 3).reshape(32, 128, -1))
            sh["ss_wz"] = _c(w[:, :2048].reshape(8, 128, 2048).transpose(1, 0, 2).reshape(128, -1))
            sh["ss_wdt"] = _c(w[:, 6144:].reshape(8, 128, 64).transpose(1, 0, 2).reshape(128, -1))
            cvw = inp["ssd_conv_w"][j].reshape(3, 32, 128).transpose(2, 1, 0)
            cvb = inp["ssd_conv_b"][j].reshape(32, 128).T[:, :, None]
            sh["ss_cv"] = _c(np.concatenate([cvw, cvb], 2).reshape(128, -1))
            sh["ss_small"] = _c(np.concatenate([inp["ssd_a_log"][j].reshape(-1), inp["ssd_dt_bias"][j].reshape(-1), inp["ssd_d"][j].reshape(-1)])[None, :])
            sh["ss_nw"] = _c(inp["ssd_norm_w"][j][None, :])
            sh["ss_wo"] = _c(inp["ssd_w_out"][j].reshape(16, 128, D).transpose(1, 0, 2).reshape(128, -1))
        elif i % 4 == 1:
            j = i // 4
            sh["gm_wu"] = _c(inp["gm_w_in"][j][:, :2048].reshape(8, 128, 2048).transpose(1, 0, 2).reshape(128, -1))
            sh["gm_wv"] = _c(inp["gm_w_in"][j][:, 2048:].reshape(8, 128, 2048).transpose(1, 0, 2).reshape(128, -1))
            sh["gm_wo"] = _c(inp["gm_w_out"][j].reshape(16, 128, D).transpose(1, 0, 2).reshape(128, -1))
            sh["gm_wsT"] = _c(inp["gm_w_s"][j].transpose(2, 0, 1).reshape(128, -1))
            sh["gm_bs"] = _c(inp["gm_b_s"][j].reshape(1, -1))
            sh["gm_ln"] = _c(np.stack([inp["gm_ln_w"][j].reshape(16, 128).T, inp["gm_ln_b"][j].reshape(16, 128).T], 1).reshape(128, -1))
        elif i % 4 == 2:
            j = i // 4
            sh.update(_hyena_tables())
            sh["hy_fw0"] = _c(inp["hy_f_w0"][j])
            sh["hy_fw1"] = _c(inp["hy_f_w1"][j])
            sh["hy_fw2"] = _c(inp["hy_f_w2"][j])
            sh["hy_fb"] = _c(np.stack([inp["hy_f_b0"][j], inp["hy_f_b1"][j], inp["hy_sin_freq"][j][0], inp["hy_sin_freq"][j][1]], 1))
            sh["hy_bias"] = _c(inp["hy_bias"][j])
            sh["hy_wi"] = _c(inp["hy_w_in"][j].reshape(8, 128, 24, 128).transpose(2, 1, 0, 3).reshape(24, 128, -1))
            cvw = inp["hy_conv_w"][j].reshape(3, 24, 128).transpose(2, 1, 0)
            cvb = inp["hy_conv_b"][j].reshape(24, 128).T[:, :, None]
            sh["hy_cv"] = _c(np.concatenate([cvw, cvb], 2).reshape(128, -1))
            sh["hy_wo"] = _c(inp["hy_w_out"][j].reshape(8, 128, D).transpose(1, 0, 2).reshape(128, -1))
        elif i % 4 == 3:
            sh["fn_cw"], sh["fn_cl"] = _fnet_tables()
            sh["fn_wo"] = _c(inp["fn_w_out"][i // 4].reshape(8, 128, D).transpose(1, 0, 2).reshape(128, -1))
            sh["fn_bo"] = _c(inp["fn_b_out"][i // 4].reshape(8, 128).T)
    return sh


def prep_core(inp, b, xb=None):
    x = inp["x"][b] if xb is None else xb
    d = {"xT": _c(x.T)}
    d["cc"] = _c(np.stack([inp["c"][b], inp["c_ctx"]], 0).reshape(2, 8, 128).transpose(2, 1, 0))
    d["ctxT"] = _c(inp["ctx"][b].T)
    return d


ALL_PHASES = [(k, i) for i in range(4) for k in ("mix", "ffn")]


def run(inp, phases, cores, add_grid=True, final_norm=True, xs=None, trace=False):
    kb = KB(phases, add_grid, final_norm)
    nc = kb.build()
    sh = prep_shared(inp, kb.layers, phases)
    in_maps = []
    for n, b in enumerate(cores):
        m = dict(sh)
        m.update(prep_core(inp, b, None if xs is None else xs[n]))
        in_maps.append({k: v for k, v in m.items() if k in kb.din})
    res = run_bass_kernel_spmd(nc, in_maps, core_ids=list(range(len(cores))), trace=trace)
    outs = [np.ascontiguousarray(r["yT"].T) for r in res.results]
    return outs, res


def kernel(**inputs):
    inp = {k: np.asarray(v) for k, v in inputs.items()}
    outs, _ = run(inp, ALL_PHASES, list(range(NCORES)))
    return np.stack(outs, 0).astype(np.float32)
```
